# Optimizing a Trainium2 kernel written in Bass

```python
import math
import jax, jax.numpy as jnp
from jax import lax
import numpy as np

D_MODEL = 1024
BATCH = 8
SEQ = 8192
DEPTH = 1
DEC_BATCH = 32
DEC_SEQ = 16
PAST_LEN = 2048

CHUNK = 64
N_META = 16
Q_BLOCK = 128
MIX_WIDTH = D_MODEL
N_HEADS_A = 4
VAL_W_A = MIX_WIDTH // 2
HEAD_V_A = VAL_W_A // N_HEADS_A
HEAD_DIM_A = HEAD_V_A // 2
QK_W_A = N_HEADS_A * 2 * HEAD_DIM_A
ROT_DIM = HEAD_DIM_A // 4
ROPE_THETA = 500000.0
N_HEADS_B = 4
VAL_W_B = MIX_WIDTH - VAL_W_A
VAL_DIM_B = VAL_W_B // N_HEADS_B
KEY_DIM_B = VAL_DIM_B // 2
KEY_W_B = N_HEADS_B * KEY_DIM_B
GATE_RANK = 16
GATE_TAU = 16.0
D_FF = -(-8 * D_MODEL // (3 * 256)) * 256
EPS = 1e-6
SPLIT_SIZES = (QK_W_A, QK_W_A, VAL_W_A, KEY_W_B, KEY_W_B, VAL_W_B, VAL_W_B, GATE_RANK)
SPLIT_IDX = tuple(int(i) for i in np.cumsum(SPLIT_SIZES)[:-1])
N_IN = sum(SPLIT_SIZES)

kernel_name = 'hymba_diffattn_gla_streaming_step'


def rmsnorm(x, g):
    xf = x.astype(jnp.float32)
    y = xf * lax.rsqrt(jnp.mean(xf * xf, axis=-1, keepdims=True) + EPS)
    return (y * g.astype(jnp.float32)).astype(x.dtype)


def rope(x, pos):
    half = ROT_DIM // 2
    inv_freq = ROPE_THETA ** (-jnp.arange(0, ROT_DIM, 2, dtype=jnp.float32) / ROT_DIM)
    ang = pos.astype(jnp.float32)[:, None] * inv_freq[None, :]
    cos = jnp.cos(ang)[:, None, None, :].astype(x.dtype)
    sin = jnp.sin(ang)[:, None, None, :].astype(x.dtype)
    x1, x2, rest = x[..., :half], x[..., half:ROT_DIM], x[..., ROT_DIM:]
    return jnp.concatenate([x1 * cos - x2 * sin, x2 * cos + x1 * sin, rest], axis=-1)


def lambda_init(layer):
    return 0.8 - 0.6 * math.exp(-0.3 * layer)


def diff_lambda(lw, li):
    f32 = jnp.float32
    return (jnp.exp(jnp.sum(lw['lambda_q1'].astype(f32) * lw['lambda_k1'].astype(f32)))
            - jnp.exp(jnp.sum(lw['lambda_q2'].astype(f32) * lw['lambda_k2'].astype(f32))) + li)


def project(hn, pos, lw):
    lead = hn.shape[:-1]
    p = hn @ lw['w_in']
    qa, ka, va, qb, kb, vb, gb, a_low = jnp.split(p, SPLIT_IDX, axis=-1)
    qa = rope(rmsnorm(qa.reshape(lead + (N_HEADS_A, 2, HEAD_DIM_A)), lw['q_norm']), pos)
    ka = rope(rmsnorm(ka.reshape(lead + (N_HEADS_A, 2, HEAD_DIM_A)), lw['k_norm']), pos)
    va = va.reshape(lead + (N_HEADS_A, HEAD_V_A))
    qb = qb.reshape(lead + (N_HEADS_B, KEY_DIM_B)) * (KEY_DIM_B ** -0.5)
    kb = kb.reshape(lead + (N_HEADS_B, KEY_DIM_B))
    vb = vb.reshape(lead + (N_HEADS_B, VAL_DIM_B))
    log_a = (jax.nn.log_sigmoid((a_low @ lw['w_a2'] + lw['b_a']).astype(jnp.float32))
             / GATE_TAU).reshape(lead + (N_HEADS_B, KEY_DIM_B))
    return qa, ka, va, qb, kb, vb, gb, log_a


def diff_attend(q, k, v, lam, mask):
    s = jnp.einsum('...qhcd,...khcd->...hcqk', q, k).astype(jnp.float32) * (HEAD_DIM_A ** -0.5)
    if mask is not None:
        s = jnp.where(mask, s, -jnp.inf)
    p = jax.nn.softmax(s, axis=-1)
    a = p[..., 0, :, :] - lam * p[..., 1, :, :]
    return jnp.einsum('...hqk,...khe->...qhe', a.astype(v.dtype), v)


def gla_block(q, k, v, log_a, s):
    f32 = jnp.float32
    t = q.shape[-3]
    b = jnp.cumsum(log_a.astype(f32), axis=-3)
    b_last = b[..., -1, :, :]
    qf = q.astype(f32) * jnp.exp(b)
    kf = k.astype(f32)
    vf = v.astype(f32)
    sf = s.astype(f32)
    causal = jnp.tril(jnp.ones((t, t), dtype=bool))
    scores = jnp.where(causal, jnp.einsum('...thd,...shd->...hts', qf, kf * jnp.exp(-b)), 0.0)
    o = (jnp.einsum('...thd,...hde->...the', qf, sf)
         + jnp.einsum('...hts,...she->...the', scores, vf))
    s_new = (jnp.exp(b_last)[..., None] * sf
             + jnp.einsum('...thd,...the->...hde', kf * jnp.exp(b_last[..., None, :, :] - b), vf))
    return o.astype(v.dtype), s_new.astype(s.dtype)


def mix_out(o_a, o_b, gb, lw, li):
    lead = o_a.shape[:-2]
    ya = rmsnorm(o_a, lw['g_diff'].reshape(N_HEADS_A, HEAD_V_A)) * (1.0 - li)
    yb = rmsnorm(o_b, lw['g_gla'].reshape(N_HEADS_B, VAL_DIM_B)).reshape(lead + (VAL_W_B,))
    yb = yb * jax.nn.silu(gb)
    y = jnp.concatenate([ya.reshape(lead + (VAL_W_A,)), yb], axis=-1)
    return y @ lw['w_out']


def ffn(h, lw):
    hn = rmsnorm(h, lw['norm_ffn'])
    return (jax.nn.silu(hn @ lw['w_ffn_gate']) * (hn @ lw['w_ffn_up'])) @ lw['w_ffn_down']


def prompt_layer(hm, hx, lw, li):
    b_sz, s_len = hx.shape[0], hx.shape[1]
    lam = diff_lambda(lw, li)
    pos_m = jnp.arange(N_META, dtype=jnp.int32)
    pos_x = N_META + jnp.arange(s_len, dtype=jnp.int32)
    qa_m, ka_m, va_m, qb_m, kb_m, vb_m, gb_m, la_m = project(rmsnorm(hm, lw['norm_mix']), pos_m, lw)
    qa_x, ka_x, va_x, qb_x, kb_x, vb_x, gb_x, la_x = project(rmsnorm(hx, lw['norm_mix']), pos_x, lw)

    k_all = jnp.concatenate([jnp.broadcast_to(ka_m[None], (b_sz,) + ka_m.shape), ka_x], axis=1)
    v_all = jnp.concatenate([jnp.broadcast_to(va_m[None], (b_sz,) + va_m.shape), va_x], axis=1)
    oa_m = diff_attend(qa_m, ka_m, va_m, lam, None)
    key_chunk = jnp.concatenate([jnp.full((N_META,), -1, jnp.int32),
                                 jnp.arange(s_len, dtype=jnp.int32) // CHUNK])
    n_qb = s_len // Q_BLOCK
    q_blocks = qa_x.reshape((b_sz, n_qb, Q_BLOCK) + qa_x.shape[2:]).swapaxes(0, 1)
    q_chunk = (jnp.arange(s_len, dtype=jnp.int32) // CHUNK).reshape(n_qb, Q_BLOCK)

    def attend_block(args):
        q_blk, qc = args
        return diff_attend(q_blk, k_all, v_all, lam, qc[:, None] >= key_chunk[None, :])

    oa_x = lax.map(attend_block, (q_blocks, q_chunk)).swapaxes(0, 1).reshape(
        b_sz, s_len, N_HEADS_A, HEAD_V_A)

    s_zero = jnp.zeros((N_HEADS_B, KEY_DIM_B, VAL_DIM_B), vb_m.dtype)
    ob_m, s_meta = gla_block(qb_m, kb_m, vb_m, la_m, s_zero)

    def to_chunks(t):
        return t.reshape((b_sz, s_len // CHUNK, CHUNK) + t.shape[2:]).swapaxes(0, 1)

    def scan_step(s, blk):
        o, s_next = gla_block(blk[0], blk[1], blk[2], blk[3], s)
        return s_next, o

    s_final, ob_x = lax.scan(scan_step, jnp.broadcast_to(s_meta[None], (b_sz,) + s_meta.shape),
                             (to_chunks(qb_x), to_chunks(kb_x), to_chunks(vb_x), to_chunks(la_x)))
    ob_x = ob_x.swapaxes(0, 1).reshape(b_sz, s_len, N_HEADS_B, VAL_DIM_B)

    hm = hm + mix_out(oa_m, ob_m, gb_m, lw, li)
    hm = hm + ffn(hm, lw)
    hx = hx + mix_out(oa_x, ob_x, gb_x, lw, li)
    hx = hx + ffn(hx, lw)
    new_k = k_all.reshape(b_sz, N_META + s_len, N_HEADS_A, 2 * HEAD_DIM_A)
    return hm, hx, new_k, v_all, s_final


def sample_layer(hs, cache_k, cache_v, state, lw, li):
    db, t_new = hs.shape[0], hs.shape[1]
    past = cache_k.shape[1]
    lam = diff_lambda(lw, li)
    pos = past + jnp.arange(t_new, dtype=jnp.int32)
    qa, ka, va, qb, kb, vb, gb, la = project(rmsnorm(hs, lw['norm_mix']), pos, lw)
    k_all = jnp.concatenate([cache_k.reshape(db, past, N_HEADS_A, 2, HEAD_DIM_A), ka], axis=1)
    v_all = jnp.concatenate([cache_v, va], axis=1)
    oa = diff_attend(qa, k_all, v_all, lam, None)
    ob, s_new = gla_block(qb, kb, vb, la, state)
    hs = hs + mix_out(oa, ob, gb, lw, li)
    hs = hs + ffn(hs, lw)
    return hs, ka.reshape(db, t_new, N_HEADS_A, 2 * HEAD_DIM_A), va, s_new


def setup_inputs(seed: int = 0) -> dict:
    key = jax.random.key(seed)
    ks = jax.random.split(key, 24)
    f32 = jnp.float32

    def nrm(k, shape, scale):
        return jax.random.normal(k, shape, f32) * scale

    return {
        'x_prompt': nrm(ks[0], (BATCH, SEQ, D_MODEL), 1.0),
        'x_sample': nrm(ks[1], (DEC_BATCH, DEC_SEQ, D_MODEL), 1.0),
        'cache_k_diff': nrm(ks[2], (DEPTH, DEC_BATCH, N_META + PAST_LEN, N_HEADS_A, 2 * HEAD_DIM_A), 1.0),
        'cache_v_diff': nrm(ks[3], (DEPTH, DEC_BATCH, N_META + PAST_LEN, N_HEADS_A, HEAD_V_A), 1.0),
        'state_gla': nrm(ks[4], (DEPTH, DEC_BATCH, N_HEADS_B, KEY_DIM_B, VAL_DIM_B), 1.0),
        'meta_tokens': nrm(ks[5], (N_META, D_MODEL), 1.0),
        'norm_mix': 1.0 + nrm(ks[6], (DEPTH, D_MODEL), 0.02),
        'w_in': nrm(ks[7], (DEPTH, D_MODEL, N_IN), D_MODEL ** -0.5),
        'w_a2': nrm(ks[8], (DEPTH, GATE_RANK, KEY_W_B), GATE_RANK ** -0.5),
        'b_a': nrm(ks[9], (DEPTH, KEY_W_B), 0.01),
        'q_norm': 1.0 + nrm(ks[10], (DEPTH, HEAD_DIM_A), 0.02),
        'k_norm': 1.0 + nrm(ks[11], (DEPTH, HEAD_DIM_A), 0.02),
        'lambda_q1': nrm(ks[12], (DEPTH, HEAD_DIM_A), 0.1),
        'lambda_k1': nrm(ks[13], (DEPTH, HEAD_DIM_A), 0.1),
        'lambda_q2': nrm(ks[14], (DEPTH, HEAD_DIM_A), 0.1),
        'lambda_k2': nrm(ks[15], (DEPTH, HEAD_DIM_A), 0.1),
        'g_diff': 1.0 + nrm(ks[16], (DEPTH, VAL_W_A), 0.02),
        'g_gla': 1.0 + nrm(ks[17], (DEPTH, VAL_W_B), 0.02),
        'w_out': nrm(ks[18], (DEPTH, MIX_WIDTH, D_MODEL), MIX_WIDTH ** -0.5),
        'norm_ffn': 1.0 + nrm(ks[19], (DEPTH, D_MODEL), 0.02),
        'w_ffn_gate': nrm(ks[20], (DEPTH, D_MODEL, D_FF), D_MODEL ** -0.5),
        'w_ffn_up': nrm(ks[21], (DEPTH, D_MODEL, D_FF), D_MODEL ** -0.5),
        'w_ffn_down': nrm(ks[22], (DEPTH, D_FF, D_MODEL), D_FF ** -0.5),
    }


def reference(x_prompt, x_sample, cache_k_diff, cache_v_diff, state_gla, meta_tokens, norm_mix,
              w_in, w_a2, b_a, q_norm, k_norm, lambda_q1, lambda_k1, lambda_q2, lambda_k2,
              g_diff, g_gla, w_out, norm_ffn, w_ffn_gate, w_ffn_up, w_ffn_down):
    hm, hx, hs = meta_tokens, x_prompt, x_sample
    k_p, v_p, s_p, k_s, v_s, s_s = [], [], [], [], [], []
    for layer in range(DEPTH):
        lw = {
            'norm_mix': norm_mix[layer], 'w_in': w_in[layer], 'w_a2': w_a2[layer], 'b_a': b_a[layer],
            'q_norm': q_norm[layer], 'k_norm': k_norm[layer],
            'lambda_q1': lambda_q1[layer], 'lambda_k1': lambda_k1[layer],
            'lambda_q2': lambda_q2[layer], 'lambda_k2': lambda_k2[layer],
            'g_diff': g_diff[layer], 'g_gla': g_gla[layer], 'w_out': w_out[layer],
            'norm_ffn': norm_ffn[layer], 'w_ffn_gate': w_ffn_gate[layer],
            'w_ffn_up': w_ffn_up[layer], 'w_ffn_down': w_ffn_down[layer],
        }
        li = lambda_init(layer)
        hm, hx, kp, vp, sp = prompt_layer(hm, hx, lw, li)
        hs, kn, vn, sn = sample_layer(hs, cache_k_diff[layer], cache_v_diff[layer], state_gla[layer], lw, li)
        k_p.append(kp); v_p.append(vp); s_p.append(sp)
        k_s.append(kn); v_s.append(vn); s_s.append(sn)
    return (hx, hs, jnp.stack(k_p), jnp.stack(v_p), jnp.stack(s_p),
            jnp.stack(k_s), jnp.stack(v_s), jnp.stack(s_s))
```

```python
import contextlib
import math
import os
import numpy as np
import concourse.bass as bass
import concourse.mybir as mybir
from concourse.bass_utils import run_bass_kernel_spmd

F32 = mybir.dt.float32
BF16 = mybir.dt.bfloat16
AF = mybir.ActivationFunctionType
ALU = mybir.AluOpType
AX = mybir.AxisListType

D = 1024
NIN = 3088
DFF = 2816
NFC = DFF // 128
NMETA = 16
EPS = 1e-6
LI = 0.8 - 0.6 * math.exp(-0.3 * 0)
PAST = 2064
NSS = 4
TS = 16
NCONST = 969


class Buf:
    __slots__ = ("name", "lw", "rd")

    def __init__(self, name):
        self.name = name
        self.lw = None
        self.rd = []


class Prog:
    ENG = ("pe", "act", "dve", "pool", "sp")

    def __init__(self):
        self.ops = []
        self.last = {e: None for e in self.ENG}
        self.pend = {e: set() for e in self.ENG}
        self.lastdma = {}
        self.enabled = True
        self._rec = None
        self._atom = None

    def begin(self):
        self._rec = []
        self._atom = None

    def end(self):
        r = self._rec
        self._rec = None
        return r

    def atom_begin(self):
        if self._rec is not None:
            self._atom = []

    def atom_end(self):
        if self._rec is not None and self._atom is not None:
            self._rec.append(self._atom)
            self._atom = None

    def replay(self, recs):
        for grp in recs:
            for a in grp:
                self.op(*a)

    def interleave(self, *streams, bias=None):
        if bias is None:
            bias = [0.0] * len(streams)
        keep = [i for i, st in enumerate(streams) if st]
        bias = [bias[i] for i in keep]
        streams = [streams[i] for i in keep]
        tot = [sum(len(g) for g in st) for st in streams]
        pos = [0] * len(streams)
        done = [0] * len(streams)
        while True:
            best, bf = None, None
            for i, st in enumerate(streams):
                if pos[i] < len(st):
                    f = done[i] / tot[i] - bias[i]
                    if bf is None or f < bf:
                        best, bf = i, f
            if best is None:
                break
            grp = streams[best][pos[best]]
            for a in grp:
                self.op(*a)
            pos[best] += 1
            done[best] += len(grp)

    def op(self, eng, fn, r=(), w=(), dma=None):
        if not self.enabled:
            return None
        if self._rec is not None:
            a = (eng, fn, tuple(r), tuple(w), dma)
            if self._atom is not None:
                self._atom.append(a)
            else:
                self._rec.append([a])
            return None
        idx = len(self.ops)
        deps = set()
        for b in r:
            if b.lw is not None:
                deps.add(b.lw)
        for b in w:
            if b.lw is not None:
                deps.add(b.lw)
            deps.update(b.rd)
        for b in r:
            b.rd.append(idx)
        for b in w:
            b.lw = idx
            b.rd = []
        deps.discard(idx)
        if self.pend[eng]:
            deps.update(self.pend[eng])
            self.pend[eng] = set()
        self.ops.append(dict(eng=eng, fn=fn, deps=deps, dma=dma, need=False))
        if dma is None:
            self.last[eng] = idx
        else:
            self.lastdma[dma] = idx
        return idx

    def barrier(self):
        s = set(v for v in self.last.values() if v is not None)
        s.update(self.lastdma.values())
        for e in self.ENG:
            self.pend[e].update(s)

    def emit(self, nc):
        ops = self.ops
        for o in ops:
            for d in o["deps"]:
                ops[d]["need"] = True
        cnt = {e: 0 for e in self.ENG}
        dcount = {}
        for o in ops:
            if o["dma"] is not None:
                k = o["dma"]
                dcount[k] = dcount.get(k, 0) + 16
                o["sig"] = ("d", k, dcount[k])
            elif o["need"]:
                cnt[o["eng"]] += 1
                o["sig"] = ("e", o["eng"], cnt[o["eng"]])
            else:
                o["sig"] = None
        with contextlib.ExitStack() as st:
            esem = {e: st.enter_context(nc.semaphore("s_" + e)) for e in self.ENG}
            dsem = {k: st.enter_context(nc.semaphore("d_%s" % (k,))) for k in dcount}
            block = st.enter_context(nc.Block())

            def body(ename):
                def f(eng):
                    waited = {}
                    for o in ops:
                        if o["eng"] != ename:
                            continue
                        need = {}
                        for d in o["deps"]:
                            s = ops[d]["sig"]
                            if s is None:
                                continue
                            if s[0] == "e" and s[1] == "pe" and ename == "pe":
                                continue
                            key = (s[0], s[1])
                            need[key] = max(need.get(key, 0), s[2])
                        pend = []
                        for key, v in need.items():
                            if waited.get(key, 0) >= v:
                                continue
                            waited[key] = v
                            pend.append((esem[key[1]] if key[0] == "e" else dsem[key[1]], v))
                        attach = None
                        if pend:
                            attach = pend.pop()
                        for sem, v in pend:
                            eng.wait_ge(sem, v)
                        ins = o["fn"](eng)
                        if attach is not None:
                            ins._wait_ge(attach[0], attach[1])
                        s = o["sig"]
                        if s is not None:
                            if s[0] == "d":
                                ins.then_inc(dsem[s[1]], 16)
                            else:
                                ins.then_inc(esem[s[1]], 1)
                    if ename == "sp":
                        for k, v in dcount.items():
                            if waited.get(("d", k), 0) < v:
                                eng.wait_ge(dsem[k], v)
                return f

            block.tensor(body("pe"))
            block.scalar(body("act"))
            block.vector(body("dve"))
            block.gpsimd(body("pool"))
            block.sync(body("sp"))


def build(S, stop=None):
    NT = S // 128
    nc = bass.Bass("TRN2", target_bir_lowering=False)
    P = Prog()
    op = P.op

    def din(name, shape, dt=F32):
        return nc.dram_tensor(name, list(shape), dt, kind="ExternalInput").ap()

    def dout(name, shape):
        return nc.dram_tensor(name, list(shape), F32, kind="ExternalOutput").ap()

    def dscr(name, shape, dt):
        return nc.dram_tensor(name, list(shape), dt, kind="Internal").ap()

    x_d = din("x", [S, D]); xs_d = din("xs", [NSS * TS, D])
    ck_d = din("ck", [NSS, PAST, 512]); cv_d = din("cv", [NSS, PAST, 512])
    st_d = din("st", [NSS, 256, 128]); meta_d = din("meta", [NMETA, D])
    win_d = din("w_in", [D, NIN]); wa2b_d = din("wa2b", [17, 256])
    gmix_d = din("gmix", [128, 8]); gffn_d = din("gffn", [128, 8])
    gqk_d = din("gqk", [128, 1024]); lamv_d = din("lamv", [128, 256])
    gd_d = din("gd", [128, 512]); gg_d = din("gg", [128, 512])
    wout_d = din("w_out", [D, D]); wg_d = din("wg", [D, DFF]); wu_d = din("wu", [D, DFF])
    wd_d = din("wd", [DFF, D])
    ropep_d = din("ropep", [S, 16]); ropem_d = din("ropem", [16, 16]); ropes_d = din("ropes", [64, 16])
    cst_d = din("cst", [128, NCONST])

    y_d = dout("y", [S, D]); ys_d = dout("ys", [NSS * TS, D])
    nk_d = dout("nk", [NMETA + S, 512]); nv_d = dout("nv", [NMETA + S, 512])
    ngla_d = dout("ngla", [256, 128])
    nks_d = dout("nks", [NSS * TS, 512]); nvs_d = dout("nvs", [NSS * TS, 512])
    nglas_d = dout("nglas", [NSS, 256, 128])

    qt_scr = dscr("qt_scr", [128, 4, S], BF16)
    kt_scr = dscr("kt_scr", [128, 4, NMETA + S], BF16)
    v_scr = dscr("v_scr", [NMETA + S, 516], BF16)
    ym_scr = dscr("ym_scr", [S, D], BF16)

    OUTB = Buf("outs")

    with contextlib.ExitStack() as st0:
        def sbt(st, name, shape, dt):
            return st.enter_context(nc.sbuf_tensor("sb_" + name, list(shape), dt))

        def pst(st, name, shape, dt):
            return st.enter_context(nc.psum_tensor("ps_" + name, list(shape), dt))

        identb = sbt(st0, "identb", [128, 128], BF16); Bident = Buf("identb")
        gffn = sbt(st0, "gffn", [128, 8], F32); Bgffn = Buf("gffn")
        ymS = sbt(st0, "ymS", [64, 1024], BF16); BymS = Buf("ymS")
        stAB = st0.enter_context(contextlib.ExitStack())
        cst = sbt(stAB, "cst", [128, NCONST], F32); Bcst = Buf("cst")
        gd = sbt(stAB, "gd", [128, 512], F32); Bgd = Buf("gd")
        gmix = sbt(stAB, "gmix", [128, 8], F32); Bgmix = Buf("gmix")
        sm = sbt(stAB, "sm", [128, 16], F32); Bsm = Buf("sm")
        wa2f = sbt(stAB, "wa2f", [32, 256], F32); Bwa2f = Buf("wa2f")
        wa2b = sbt(stAB, "wa2b", [32, 256], BF16); Bwa2 = Buf("wa2b")
        alT = sbt(stAB, "alT", [32, 128], BF16); BalT = Buf("alT")
        QTs = sbt(stAB, "QTs", [128, 8, 64], BF16); BQTs = Buf("QTs")
        vnewS = sbt(stAB, "vnewS", [64, 512], F32); BvnewS = Buf("vnewS")
        stA = stAB.enter_context(contextlib.ExitStack())
        gqk = sbt(stA, "gqk", [128, 1024], F32); Bgqk = Buf("gqk")
        gg = sbt(stA, "gg", [128, 512], F32); Bgg = Buf("gg")
        lamv = sbt(stA, "lamv", [128, 256], F32); Blamv = Buf("lamv")
        neglam = sm[:, 0:1]; negB = sm[:, 1:2]
        ident_f = cst[:, 0:128]
        triI = cst[:, 128:256]; triS = cst[:, 256:384]; maskST = cst[:, 384:512]
        triIs = cst[:, 512:576]; triSs = cst[:, 576:640]; maskSTs = cst[:, 640:704]
        sel16s = cst[:, 704:708]; rowmask = cst[:, 708:712]; sel16p = cst[:, 712:713]
        cmask = cst[:, 713:969]

        op("sp", lambda e: e.dma_start(out=cst[:], in_=cst_d[:, :]), w=[Bcst], dma="c0")
        op("sp", lambda e: e.dma_start(out=gqk[:], in_=gqk_d[:, :]), w=[Bgqk], dma="c1")
        op("sp", lambda e: e.dma_start(out=gd[:], in_=gd_d[:, :]), w=[Bgd], dma="c2")
        op("sp", lambda e: e.dma_start(out=gg[:], in_=gg_d[:, :]), w=[Bgg], dma="c3")
        op("sp", lambda e: e.dma_start(out=gmix[:], in_=gmix_d[:, :]), w=[Bgmix], dma="c4")
        op("sp", lambda e: e.dma_start(out=gffn[:], in_=gffn_d[:, :]), w=[Bgffn], dma="c5")
        op("sp", lambda e: e.dma_start(out=lamv[:], in_=lamv_d[:, :]), w=[Blamv], dma="c6")
        op("sp", lambda e: e.dma_start(out=wa2f[0:17, :], in_=wa2b_d[:, :]), w=[Bwa2f], dma="c7")
        op("dve", lambda e: e.tensor_copy(out=identb[:], in_=ident_f), r=[Bcst], w=[Bident])
        op("dve", lambda e: e.tensor_copy(out=wa2b[0:17, :], in_=wa2f[0:17, :]), r=[Bwa2f], w=[Bwa2])
        op("pool", lambda e: e.memset(alT[:], 1.0), w=[BalT])
        tmpA = sbt(stA, "tmpA", [128, 128], F32); BtmpA = Buf("tmpA")
        op("dve", lambda e: e.tensor_tensor(out=tmpA[:, 0:64], in0=lamv[:, 0:64], in1=lamv[:, 64:128], op=ALU.mult), r=[Blamv], w=[BtmpA])
        op("dve", lambda e: e.tensor_tensor(out=tmpA[:, 64:128], in0=lamv[:, 128:192], in1=lamv[:, 192:256], op=ALU.mult), r=[Blamv, BtmpA], w=[BtmpA])
        op("dve", lambda e: e.reduce_sum(out=sm[:, 2:4], in_=tmpA[:].rearrange("p (a b) -> p a b", b=64), axis=AX.X), r=[BtmpA], w=[Bsm])
        op("act", lambda e: e.activation(out=sm[:, 4:6], in_=sm[:, 2:4], func=AF.Exp), r=[Bsm], w=[Bsm])
        op("dve", lambda e: e.tensor_tensor(out=sm[:, 0:1], in0=sm[:, 5:6], in1=sm[:, 4:5], op=ALU.subtract), r=[Bsm], w=[Bsm])
        op("dve", lambda e: e.tensor_scalar(out=sm[:, 0:1], in0=sm[:, 0:1], scalar1=-LI, scalar2=None, op0=ALU.add), r=[Bsm], w=[Bsm])
        op("dve", lambda e: e.tensor_tensor(out=tmpA[:, 0:64], in0=gqk[:, 0:64], in1=gqk[:, 0:64], op=ALU.mult), r=[Bgqk, BtmpA], w=[BtmpA])
        op("dve", lambda e: e.tensor_tensor(out=tmpA[:, 64:128], in0=gqk[:, 512:576], in1=gqk[:, 512:576], op=ALU.mult), r=[Bgqk, BtmpA], w=[BtmpA])
        op("dve", lambda e: e.tensor_reduce(out=sm[:, 6:8], in_=tmpA[:].rearrange("p (a b) -> p a b", b=64), axis=AX.X, op=ALU.max), r=[BtmpA], w=[Bsm])
        op("dve", lambda e: e.tensor_tensor(out=sm[:, 8:9], in0=sm[:, 6:7], in1=sm[:, 7:8], op=ALU.mult), r=[Bsm], w=[Bsm])
        op("act", lambda e: e.activation(out=sm[:, 9:10], in_=sm[:, 8:9], func=AF.Ln), r=[Bsm], w=[Bsm])
        op("act", lambda e: e.activation(out=sm[:, 10:11], in_=sm[:, 9:10], func=AF.Exp, scale=0.5), r=[Bsm], w=[Bsm])
        op("dve", lambda e: e.tensor_scalar(out=sm[:, 1:2], in0=sm[:, 10:11], scalar1=-8.0, scalar2=None, op0=ALU.mult), r=[Bsm], w=[Bsm])
        op("dve", lambda e: e.tensor_scalar(out=gd[:], in0=gd[:], scalar1=1.0 - LI, scalar2=None, op0=ALU.mult), r=[Bgd], w=[Bgd])

        if True:
            WI = sbt(stA, "WI", [128, 8, NIN], BF16); BWI = Buf("WI")
            with contextlib.ExitStack() as stg:
                wst = [sbt(stg, "wst%d" % i, [128, NIN], F32) for i in range(2)]
                Bwst = [Buf("wst0"), Buf("wst1")]
                for kc in range(8):
                    i = kc % 2
                    op("sp", lambda e, kc=kc, i=i: e.dma_start(out=wst[i][:], in_=win_d[kc * 128:(kc + 1) * 128, :]), w=[Bwst[i]], dma="wst%d" % i)
                    if i == 0:
                        op("dve", lambda e, kc=kc, i=i: e.tensor_scalar(out=WI[:, kc, :], in0=wst[i][:], scalar1=gmix[:, kc:kc + 1], scalar2=None, op0=ALU.mult), r=[Bwst[i], Bgmix], w=[BWI])
                    else:
                        op("act", lambda e, kc=kc, i=i: e.activation(out=WI[:, kc, :], in_=wst[i][:], func=AF.Copy, scale=gmix[:, kc:kc + 1]), r=[Bwst[i], Bgmix], w=[BWI])
                P.barrier()

            if stop == "SETUP":
                P.enabled = False
            xt = [sbt(stA, "xt%d" % i, [128, D], F32) for i in range(2)]; Bxt = [Buf("xt%d" % i) for i in range(2)]
            xn = sbt(stA, "xn", [128, D], BF16); Bxn = Buf("xn")
            xT = sbt(stA, "xT", [128, 8, 128], BF16); BxT = Buf("xT")
            pj = [sbt(stA, "pj%d" % i, [128, NIN], F32) for i in range(3)]; Bpj = [Buf("pj%d" % i) for i in range(3)]
            sq = sbt(stA, "sq", [128, D], F32); Bsq = Buf("sq")
            qkn = [sbt(stA, "qkn%d" % i, [128, D], F32) for i in range(3)]; Bqkn = [Buf("qkn%d" % i) for i in range(3)]
            qkb = sbt(stA, "qkb", [128, D], BF16); Bqkb = Buf("qkb")
            rtmp = sbt(stA, "rtmp", [128, 4, 16, 8], F32); Brtmp = Buf("rtmp")
            st16 = sbt(stA, "st16", [128, 48], F32); Bst16 = Buf("st16")
            stq = sbt(stA, "stq", [128, 48], F32); Bstq = Buf("stq")
            sto = sbt(stA, "sto", [128, 48], F32); Bsto = Buf("sto")
            qkT = [sbt(stA, "qkT%d" % i, [128, 8, 512], BF16) for i in range(2)]; BqkT = [Buf("qkT0"), Buf("qkT1")]
            v16 = [sbt(stA, "v16_%d" % i, [128, 4, 129], BF16) for i in range(3)]; Bv16 = [Buf("v16_%d" % i) for i in range(3)]
            ropeP = sbt(stA, "ropeP", [128, NT, 16], F32); BropeP = Buf("ropeP")
            ropeM = sbt(stA, "ropeM", [16, 16], F32); BropeM = Buf("ropeM")
            ropeS = sbt(stA, "ropeS", [64, 16], F32); BropeS = Buf("ropeS")
            al16 = sbt(stA, "al16", [128, 16], BF16); Bal16 = Buf("al16")
            gz = sbt(stA, "gz", [128, 6, 256], F32); Bgz = [Buf("gz%d" % i) for i in range(6)]
            gb16 = sbt(stA, "gb16", [128, 3, 256], BF16); Bgb16 = [Buf("gb16_%d" % i) for i in range(3)]
            qfm = sbt(stA, "qfm", [128, 4, NSS, 64], BF16); Bqfm = Buf("qfm")
            qTm = sbt(stA, "qTm", [128, 4, 128], BF16); BqTm = Buf("qTm")
            kTm = sbt(stA, "kTm", [128, 4, 128], BF16); BkTm = Buf("kTm")
            khm = sbt(stA, "khm", [64, NSS, 256], BF16); Bkhm = Buf("khm")
            AT = sbt(stA, "AT", [128, 4, 128], BF16); BAT = Buf("AT")
            dec = sbt(stA, "dec", [128, 8], F32); Bdec = Buf("dec")
            Sst = sbt(stA, "Sst", [128, 2, 128], F32); BSst = Buf("Sst")
            S16 = sbt(stA, "S16", [128, 2, 128], BF16); BS16 = Buf("S16")
            SstS = sbt(stA, "SstS", [128, NSS, 2, 128], F32); BSstS = Buf("SstS")
            S16S = sbt(stA, "S16S", [128, NSS, 2, 128], BF16); BS16S = Buf("S16S")
            on = sbt(stA, "on", [128, 4, 512], F32); Bon = [Buf("on%d" % i) for i in range(4)]
            yb16 = [sbt(stA, "yb16_%d" % i, [128, 512], BF16) for i in range(3)]; Byb16 = [Buf("yb16_%d" % i) for i in range(3)]
            pA = pst(stA, "pA", [128, 8, 128], BF16); BpA = Buf("pA")
            pP = [pst(stA, "pP%d" % i, [128, 512], F32) for i in range(2)]; BpP = [Buf("pP0"), Buf("pP1")]
            pU2 = pst(stA, "pU2", [128, 512], F32); BpU2 = Buf("pU2")
            pBC = pst(stA, "pBC", [128, 512], F32); BpBC = Buf("pBC")
            pT2 = pst(stA, "pT2", [128, 1024], BF16); BpT2 = Buf("pT2")
            pSC = pst(stA, "pSC", [128, 512], F32); BpSC = Buf("pSC")
            pO = pst(stA, "pO", [128, 512], F32); BpO = Buf("pO")

            op("sp", lambda e: e.dma_start(out=ropeP[:], in_=ropep_d.rearrange("(t p) c -> p t c", p=128)), w=[BropeP], dma="c8")
            op("sp", lambda e: e.dma_start(out=ropeM[:], in_=ropem_d[:, :]), w=[BropeM], dma="c9")
            op("sp", lambda e: e.dma_start(out=ropeS[:], in_=ropes_d[:, :]), w=[BropeS], dma="c10")
            for i in range(3):
                op("pool", lambda e, i=i: e.memset(v16[i][:], 1.0), w=[Bv16[i]])
            op("pool", lambda e: e.memset(qTm[:], 0.0), w=[BqTm])

            op("pool", lambda e: e.memset(kTm[:], 0.0), w=[BkTm])

            cols = [(0, 512), (512, 512), (1024, 512), (1536, 512), (2048, 512), (2560, 512), (3072, 16)]
            state = dict(tile=0, ppar=0)

            def load_x(src, n, par):
                op("sp", lambda e: e.dma_start(out=xt[par][:n, :], in_=src), w=[Bxt[par]], dma="x%d" % par)

            def proj_tile(n, par, rope_ap, Brope):
                proj_tile_a(n, par, par)
                proj_tile_b(n, par, rope_ap, Brope)

            def proj_tile_a(n, par, xpar):
                x_ = xt[xpar]; pj_ = pj[par]; q_ = qkn[par]
                op("act", lambda e: e.activation(out=xn[:n, :], in_=x_[:n, :], func=AF.Square, scale=1.0 / 32.0, accum_out=st16[:n, 0:1]), r=[Bxt[xpar]], w=[Bxn, Bst16])
                op("act", lambda e: e.activation(out=st16[:n, 1:2], in_=st16[:n, 0:1], func=AF.Ln, bias=EPS), r=[Bst16], w=[Bst16])
                op("act", lambda e: e.activation(out=st16[:n, 2:3], in_=st16[:n, 1:2], func=AF.Exp, scale=-0.5), r=[Bst16], w=[Bst16])
                op("act", lambda e: e.activation(out=xn[:n, :], in_=x_[:n, :], func=AF.Copy, scale=st16[:n, 2:3]), r=[Bxt[xpar], Bst16, Bxn], w=[Bxn])
                P.atom_begin()
                for kc in range(8):
                    op("pe", lambda e, kc=kc: e.transpose(out=pA[:, kc, :n], in_=xn[:n, kc * 128:(kc + 1) * 128], identity=identb[:n, :n]), r=[Bxn, Bident], w=[BpA])
                op("act", lambda e: e.activation(out=xT[:, :, :n], in_=pA[:, :, :n], func=AF.Copy), r=[BpA], w=[BxT])
                P.atom_end()
                for ci, (c0, cw) in enumerate(cols):
                    pp = state["ppar"]; state["ppar"] ^= 1
                    for kc in range(8):
                        op("pe", lambda e, kc=kc, c0=c0, cw=cw, pp=pp: e.matmul(pP[pp][:n, :cw], lhsT=xT[:, kc, :n], rhs=WI[:, kc, c0:c0 + cw], start=(kc == 0), stop=(kc == 7)), r=[BxT, BWI], w=[BpP[pp]])
                    if ci % 2 == 0:
                        op("dve", lambda e, c0=c0, cw=cw, pp=pp: e.tensor_copy(out=pj_[:n, c0:c0 + cw], in_=pP[pp][:n, :cw]), r=[BpP[pp]], w=[Bpj[par]])
                    else:
                        op("act", lambda e, c0=c0, cw=cw, pp=pp: e.activation(out=pj_[:n, c0:c0 + cw], in_=pP[pp][:n, :cw], func=AF.Copy), r=[BpP[pp]], w=[Bpj[par]])

            def proj_tile_b(n, par, rope_ap, Brope):
                pj_ = pj[par]; q_ = qkn[par]
                op("act", lambda e: e.activation(out=sq[:n, :], in_=pj_[:n, 0:1024], func=AF.Square, scale=0.125), r=[Bpj[par]], w=[Bsq])
                op("dve", lambda e: e.reduce_sum(out=stq[:n, 16:32], in_=sq[:n, :].rearrange("p (g d) -> p g d", d=64), axis=AX.X), r=[Bsq], w=[Bstq])
                op("act", lambda e: e.activation(out=stq[:n, 32:48], in_=stq[:n, 16:32], func=AF.Ln, bias=EPS), r=[Bstq], w=[Bstq])
                op("act", lambda e: e.activation(out=stq[:n, 16:32], in_=stq[:n, 32:48], func=AF.Exp, scale=-0.5), r=[Bstq], w=[Bstq])
                op("dve", lambda e: e.tensor_tensor(out=sq[:n, :].rearrange("p (g d) -> p g d", d=64), in0=pj_[:n, 0:1024].rearrange("p (g d) -> p g d", d=64), in1=stq[:n, 16:32].unsqueeze(2).broadcast_to([n, 16, 64]), op=ALU.mult), r=[Bpj[par], Bstq, Bsq], w=[Bsq])
                op("dve", lambda e: e.tensor_tensor(out=q_[:n, :], in0=sq[:n, :], in1=gqk[:n, :], op=ALU.mult), r=[Bsq, Bgqk], w=[Bqkn[par]])
                qv = q_[:n, :].rearrange("p (g d) -> p g d", d=64)
                x1 = qv[:, :, 0:8]; x2 = qv[:, :, 8:16]
                cosb = rope_ap[:, 0:8].unsqueeze(1).broadcast_to([n, 16, 8])
                sinb = rope_ap[:, 8:16].unsqueeze(1).broadcast_to([n, 16, 8])
                op("dve", lambda e: e.tensor_tensor(out=rtmp[:n, 0], in0=x1, in1=cosb, op=ALU.mult), r=[Bqkn[par], Brope], w=[Brtmp])
                op("dve", lambda e: e.tensor_tensor(out=rtmp[:n, 1], in0=x2, in1=sinb, op=ALU.mult), r=[Bqkn[par], Brope, Brtmp], w=[Brtmp])
                op("dve", lambda e: e.tensor_tensor(out=rtmp[:n, 2], in0=x2, in1=cosb, op=ALU.mult), r=[Bqkn[par], Brope, Brtmp], w=[Brtmp])
                op("dve", lambda e: e.tensor_tensor(out=rtmp[:n, 3], in0=x1, in1=sinb, op=ALU.mult), r=[Bqkn[par], Brope, Brtmp], w=[Brtmp])
                op("dve", lambda e: e.tensor_tensor(out=x1, in0=rtmp[:n, 0], in1=rtmp[:n, 1], op=ALU.subtract), r=[Brtmp, Bqkn[par]], w=[Bqkn[par]])
                op("dve", lambda e: e.tensor_tensor(out=x2, in0=rtmp[:n, 2], in1=rtmp[:n, 3], op=ALU.add), r=[Brtmp, Bqkn[par]], w=[Bqkn[par]])

            def qk_transposes(n, par, dst, Bdst, coff, which):
                op("act", lambda e: e.activation(out=qkb[:n, :], in_=qkn[par][:n, :], func=AF.Copy), r=[Bqkn[par]], w=[Bqkb])
                js = [j for j in range(8) if (j < 4 and "q" in which) or (j >= 4 and "k" in which)]
                P.atom_begin()
                for j in js:
                    op("pe", lambda e, j=j: e.transpose(out=pA[:, j, :n], in_=qkb[:n, j * 128:(j + 1) * 128], identity=identb[:n, :n]), r=[Bqkb, Bident], w=[BpA])
                j0, j1 = js[0], js[-1] + 1
                op("act", lambda e: e.activation(out=dst[:, j0:j1, coff:coff + n], in_=pA[:, j0:j1, :n], func=AF.Copy), r=[BpA], w=[Bdst])
                P.atom_end()

            def gla(n, par, nstr, tri_i, tri_s, mask_st, sel, Sf, BSf, Sb, BSb, want_out, ybdst, Bybdst):
                vv = gv16[par]; Bvv = Bgv16[par]
                key = (n, nstr)
                if state.get("trikey") != key:
                    state["trikey"] = key
                    op("dve", lambda e: e.tensor_copy(out=trib[:n, 0, :n], in_=tri_i), r=[Bcst, Btrib], w=[Btrib])
                    op("dve", lambda e: e.tensor_copy(out=trib[:n, 1, :n], in_=tri_s), r=[Bcst, Btrib], w=[Btrib])
                    op("dve", lambda e: e.tensor_copy(out=trib[:n, 2, 0:nstr], in_=sel), r=[Bcst, Btrib], w=[Btrib])
                pj_ = pj[par]
                qb = pj_[:n, 1536:1792]; kb = pj_[:n, 1792:2048]; gbv = pj_[:n, 2560:3072]
                z_az, z_e, z_l, z_la, z_eb, z_x = [gz[:n, i, :] for i in range(6)]
                if want_out:
                    o_sq, o_t1, o_e, o_t3 = [on[:n, i, :] for i in range(4)]
                    op("act", lambda e: e.activation(out=o_e, in_=gbv, func=AF.Exp, scale=-1.0), r=[Bpj[par]], w=[Bon[2]])
                    op("act", lambda e: e.activation(out=o_t3, in_=o_e, func=AF.Ln, bias=1.0), r=[Bon[2]], w=[Bon[3]])
                    op("act", lambda e: e.activation(out=o_e, in_=o_t3, func=AF.Exp, scale=-1.0), r=[Bon[3], Bon[2]], w=[Bon[2]])
                    op("dve", lambda e: e.tensor_tensor(out=o_t3, in0=o_e, in1=gbv, op=ALU.mult), r=[Bon[2], Bon[3], Bpj[par]], w=[Bon[3]])
                op("dve", lambda e: e.tensor_copy(out=al16[:n, :], in_=pj_[:n, 3072:3088]), r=[Bpj[par]], w=[Bal16])
                op("pe", lambda e: e.transpose(out=pT2[0:16, 512:512 + n], in_=al16[:n, :], identity=identb[:n, :n]), r=[Bal16, Bident], w=[BpT2])
                op("dve", lambda e: e.tensor_copy(out=alT[0:16, :n], in_=pT2[0:16, 512:512 + n]), r=[BpT2], w=[BalT])
                op("pe", lambda e: e.matmul(pSC[:n, 0:256], lhsT=alT[0:17, :n], rhs=wa2b[0:17, :], start=True, stop=True), r=[BalT, Bwa2], w=[BpSC])
                op("act", lambda e: e.activation(out=z_az, in_=pSC[:n, 0:256], func=AF.Abs), r=[BpSC], w=[Bgz[0]])
                op("act", lambda e: e.activation(out=z_e, in_=z_az, func=AF.Exp, scale=-1.0), r=[Bgz[0]], w=[Bgz[1]])
                op("act", lambda e: e.activation(out=z_l, in_=z_e, func=AF.Ln, bias=1.0), r=[Bgz[1]], w=[Bgz[2]])
                op("dve", lambda e: e.tensor_scalar(out=z_az, in0=pSC[:n, 0:256], scalar1=0.0, scalar2=None, op0=ALU.min), r=[BpSC, Bgz[0]], w=[Bgz[0]])
                op("dve", lambda e: e.tensor_tensor(out=z_la, in0=z_az, in1=z_l, op=ALU.subtract), r=[Bgz[0], Bgz[2]], w=[Bgz[3]])
                if stop == "G1":
                    P.enabled = False
                la_hi = lahl[:n, 0, :]; la_lo = lahl[:n, 1, :]
                op("act", lambda e: e.activation(out=la_hi, in_=z_la, func=AF.Copy), r=[Bgz[3]], w=[Blahl])
                op("dve", lambda e: e.tensor_tensor(out=z_x, in0=z_la, in1=la_hi, op=ALU.subtract), r=[Bgz[3], Blahl], w=[Bgz[5]])
                op("act", lambda e: e.activation(out=la_lo, in_=z_x, func=AF.Copy), r=[Bgz[5], Blahl], w=[Blahl])
                tI = trib[:n, 0, :n]; tS = trib[:n, 1, :n]
                for hl in range(2):
                    op("pe", lambda e, hl=hl: e.matmul(pBC[:n, 0:256], lhsT=tI, rhs=lahl[:n, hl, :], start=(hl == 0), stop=(hl == 1)), r=[Blahl, Btrib], w=[BpBC])
                for hl in range(2):
                    op("pe", lambda e, hl=hl: e.matmul(pBC[:n, 256:512], lhsT=tS, rhs=lahl[:n, hl, :], start=(hl == 0), stop=(hl == 1)), r=[Blahl, Btrib], w=[BpBC])
                for j in range(2):
                    for hl in range(2):
                        op("pe", lambda e, j=j, hl=hl: e.matmul(pSC[:, 256 + j * nstr:256 + (j + 1) * nstr], lhsT=lahl[:n, hl, j * 128:(j + 1) * 128], rhs=trib[:n, 2, 0:nstr], start=(hl == 0), stop=(hl == 1)), r=[Blahl, Btrib], w=[BpSC])
                op("act", lambda e: e.activation(out=z_eb, in_=pBC[:n, 0:256], func=AF.Exp), r=[BpBC], w=[Bgz[4]])
                op("act", lambda e: e.activation(out=z_e, in_=pBC[:n, 0:256], func=AF.Exp, scale=-1.0), r=[BpBC, Bgz[1]], w=[Bgz[1]])
                op("act", lambda e: e.activation(out=z_l, in_=pBC[:n, 256:512], func=AF.Exp), r=[BpBC, Bgz[2]], w=[Bgz[2]])
                op("act", lambda e: e.activation(out=dec[:, 0:2 * nstr], in_=pSC[:, 256:256 + 2 * nstr], func=AF.Exp), r=[BpSC], w=[Bdec])
                if stop == "G2":
                    P.enabled = False
                qf = gb16[:n, 0, :]; kt = gb16[:n, 1, :]; kh = gb16[:n, 2, :]
                op("dve", lambda e: e.scalar_tensor_tensor(out=qf, in0=qb, scalar=0.125, in1=z_eb, op0=ALU.mult, op1=ALU.mult), r=[Bpj[par], Bgz[4]], w=[Bgb16[0]])
                op("dve", lambda e: e.tensor_tensor(out=kt, in0=kb, in1=z_e, op=ALU.mult), r=[Bpj[par], Bgz[1]], w=[Bgb16[1]])
                op("dve", lambda e: e.tensor_tensor(out=kh, in0=kb, in1=z_l, op=ALU.mult), r=[Bpj[par], Bgz[2]], w=[Bgb16[2]])
                if want_out:
                    for j in range(2):
                        op("pe", lambda e, j=j: e.transpose(out=pT2[:, j * 128:j * 128 + n], in_=qf[:, j * 128:(j + 1) * 128], identity=identb[:n, :n]), r=[Bgb16[0], Bident], w=[BpT2])
                        op("pe", lambda e, j=j: e.transpose(out=pT2[:, (2 + j) * 128:(2 + j) * 128 + n], in_=kt[:, j * 128:(j + 1) * 128], identity=identb[:n, :n]), r=[Bgb16[1], Bident], w=[BpT2])
                    pv = pT2[:, 0:512].rearrange("p (a t) -> p a t", t=128)
                    for i in range(2):
                        sl = slice(i * 64, (i + 1) * 64)
                        op("dve", lambda e, i=i, sl=sl: e.tensor_copy(out=qTm[sl, i::2, :n], in_=pv[sl, 0:2, :n]), r=[BpT2, BqTm], w=[BqTm])
                        op("dve", lambda e, i=i, sl=sl: e.tensor_copy(out=kTm[sl, i::2, :n], in_=pv[sl, 2:4, :n]), r=[BpT2, BkTm], w=[BkTm])
                    for h in range(4):
                        op("pe", lambda e, h=h: e.matmul(pSC[:n, h * 128:h * 128 + n], lhsT=kTm[:, h, :n], rhs=qTm[:, h, :n], start=True, stop=True), r=[BkTm, BqTm], w=[BpSC])
                    op("dve", lambda e: e.tensor_tensor(out=AT[:n, :, :n], in0=pSC[:n, :].rearrange("p (h t) -> p h t", t=128)[:, :, :n], in1=mask_st.unsqueeze(1).broadcast_to([n, 4, n]), op=ALU.mult), r=[BpSC, Bcst], w=[BAT])
                    if nstr > 1:
                        for h in range(4):
                            op("dve", lambda e, h=h: e.tensor_tensor(out=qfm[:, h, :, :n], in0=qTm[:, h, :n].unsqueeze(1).broadcast_to([128, nstr, n]), in1=cmask.rearrange("p (s t) -> p s t", t=64), op=ALU.mult), r=[BqTm, Bcst, Bqfm], w=[Bqfm])
                    for h in range(4):
                        j = h // 2
                        op("pe", lambda e, h=h: e.matmul(pO[:n, h * 128:(h + 1) * 128], lhsT=AT[:n, h, :n], rhs=vv[:n, h, 0:128], start=True, stop=False), r=[BAT, Bvv], w=[BpO])
                        if nstr == 1:
                            op("pe", lambda e, h=h, j=j: e.matmul(pO[:n, h * 128:(h + 1) * 128], lhsT=qTm[:, h, :n], rhs=Sb[:, j, :], start=False, stop=True), r=[BqTm, BSb], w=[BpO])
                        else:
                            for s in range(nstr):
                                op("pe", lambda e, h=h, j=j, s=s: e.matmul(pO[:n, h * 128:(h + 1) * 128], lhsT=qfm[:, h, s, :n], rhs=Sb[:, s, j, :], start=False, stop=(s == nstr - 1)), r=[Bqfm, BSb], w=[BpO])
                if stop == "G4":
                    P.enabled = False
                if nstr == 1:
                    for j in range(2):
                        op("pe", lambda e, j=j: e.matmul(pSC[:, j * 256:(j + 1) * 256] if not want_out else pBC[:, j * 256:(j + 1) * 256], lhsT=kh[:, j * 128:(j + 1) * 128], rhs=vv[:n, 2 * j:2 * j + 2, 0:128], start=True, stop=True), r=[Bgb16[2], Bvv], w=[BpSC if not want_out else BpBC])
                    pU = pSC if not want_out else pBC
                    BpU = BpSC if not want_out else BpBC
                    for j in range(2):
                        for i in range(2):
                            sl = slice(i * 64, (i + 1) * 64)
                            op("dve", lambda e, j=j, i=i, sl=sl: e.scalar_tensor_tensor(out=Sf[sl, j, :], in0=Sf[sl, j, :], scalar=dec[sl, j:j + 1], in1=pU[sl, j * 256 + i * 128:j * 256 + (i + 1) * 128], op0=ALU.mult, op1=ALU.add), r=[BSf, Bdec, BpU], w=[BSf])
                    op("act", lambda e: e.activation(out=Sb[:], in_=Sf[:], func=AF.Copy), r=[BSf], w=[BSb])
                else:
                    for s in range(nstr):
                        op("dve", lambda e, s=s: e.tensor_scalar(out=khm[:n, s, :], in0=kh, scalar1=rowmask[:n, s:s + 1], scalar2=None, op0=ALU.mult), r=[Bgb16[2], Bcst, Bkhm], w=[Bkhm])
                    for s in range(nstr):
                        for j in range(2):
                            op("pe", lambda e, s=s, j=j: e.matmul(pBC[:, j * 256:(j + 1) * 256], lhsT=khm[:n, s, j * 128:(j + 1) * 128], rhs=vv[:n, 2 * j:2 * j + 2, 0:128], start=True, stop=True), r=[Bkhm, Bvv], w=[BpBC])
                        for j in range(2):
                            for i in range(2):
                                sl = slice(i * 64, (i + 1) * 64)
                                op("dve", lambda e, s=s, j=j, i=i, sl=sl: e.scalar_tensor_tensor(out=Sf[sl, s, j, :], in0=Sf[sl, s, j, :], scalar=dec[sl, j * nstr + s:j * nstr + s + 1], in1=pBC[sl, j * 256 + i * 128:j * 256 + (i + 1) * 128], op0=ALU.mult, op1=ALU.add), r=[BSf, Bdec, BpBC], w=[BSf])
                if want_out:
                    if stop == "G5":
                        P.enabled = False
                    o_sq, o_t1, o_e, o_t3 = [on[:n, i, :] for i in range(4)]
                    op("act", lambda e: e.activation(out=o_sq, in_=pO[:n, :], func=AF.Square, scale=1.0 / math.sqrt(128.0)), r=[BpO], w=[Bon[0]])
                    op("dve", lambda e: e.reduce_sum(out=sto[:n, 4:8], in_=o_sq.rearrange("p (h d) -> p h d", d=128), axis=AX.X), r=[Bon[0]], w=[Bsto])
                    op("act", lambda e: e.activation(out=sto[:n, 8:12], in_=sto[:n, 4:8], func=AF.Ln, bias=EPS), r=[Bsto], w=[Bsto])
                    op("act", lambda e: e.activation(out=sto[:n, 4:8], in_=sto[:n, 8:12], func=AF.Exp, scale=-0.5), r=[Bsto], w=[Bsto])
                    op("dve", lambda e: e.tensor_tensor(out=o_t1.rearrange("p (h d) -> p h d", d=128), in0=pO[:n, :].rearrange("p (h d) -> p h d", d=128), in1=sto[:n, 4:8].unsqueeze(2).broadcast_to([n, 4, 128]), op=ALU.mult), r=[BpO, Bsto], w=[Bon[1]])
                    op("dve", lambda e: e.tensor_tensor(out=o_t1, in0=o_t1, in1=gg[:n, :], op=ALU.mult), r=[Bon[1], Bgg], w=[Bon[1]])
                    op("dve", lambda e: e.tensor_tensor(out=ybdst, in0=o_t1, in1=o_t3, op=ALU.mult), r=[Bon[1], Bon[3]], w=[Bybdst])

            def gla_a(t):
                par = t % 3; p = t % 2; n = 128
                pj_ = pj[par]
                qb = pj_[:n, 1536:1792]; kb = pj_[:n, 1792:2048]; gbv = pj_[:n, 2560:3072]
                z_az, z_e, z_l, z_la, z_eb, z_x = [gz[:n, i, :] for i in range(6)]
                o_e = on[:n, 2, :]; o_tmp = on[:n, 3, :]
                op("act", lambda e: e.activation(out=o_e, in_=gbv, func=AF.Exp, scale=-1.0), r=[Bpj[par]], w=[Bon[2]])
                op("act", lambda e: e.activation(out=o_tmp, in_=o_e, func=AF.Ln, bias=1.0), r=[Bon[2]], w=[Bon[3]])
                op("act", lambda e: e.activation(out=o_e, in_=o_tmp, func=AF.Exp, scale=-1.0), r=[Bon[3], Bon[2]], w=[Bon[2]])
                op("dve", lambda e: e.tensor_tensor(out=t3_2[p][:, :], in0=o_e, in1=gbv, op=ALU.mult), r=[Bon[2], Bpj[par]], w=[Bt3_2[p]])
                gla_v(n, par)
                op("dve", lambda e: e.tensor_copy(out=al16[:n, :], in_=pj_[:n, 3072:3088]), r=[Bpj[par]], w=[Bal16])
                op("pe", lambda e: e.transpose(out=pT2[0:16, 512:512 + n], in_=al16[:n, :], identity=identb[:n, :n]), r=[Bal16, Bident], w=[BpT2])
                op("dve", lambda e: e.tensor_copy(out=alT[0:16, :n], in_=pT2[0:16, 512:512 + n]), r=[BpT2], w=[BalT])
                op("pe", lambda e: e.matmul(pSC[:n, 0:256], lhsT=alT[0:17, :n], rhs=wa2b[0:17, :], start=True, stop=True), r=[BalT, Bwa2], w=[BpSC])
                op("act", lambda e: e.activation(out=z_az, in_=pSC[:n, 0:256], func=AF.Abs), r=[BpSC], w=[Bgz[0]])
                op("act", lambda e: e.activation(out=z_e, in_=z_az, func=AF.Exp, scale=-1.0), r=[Bgz[0]], w=[Bgz[1]])
                op("act", lambda e: e.activation(out=z_l, in_=z_e, func=AF.Ln, bias=1.0), r=[Bgz[1]], w=[Bgz[2]])
                op("dve", lambda e: e.tensor_scalar(out=z_az, in0=pSC[:n, 0:256], scalar1=0.0, scalar2=None, op0=ALU.min), r=[BpSC, Bgz[0]], w=[Bgz[0]])
                op("dve", lambda e: e.tensor_tensor(out=z_la, in0=z_az, in1=z_l, op=ALU.subtract), r=[Bgz[0], Bgz[2]], w=[Bgz[3]])
                la_hi = lahl[:n, 0, :]; la_lo = lahl[:n, 1, :]
                op("act", lambda e: e.activation(out=la_hi, in_=z_la, func=AF.Copy), r=[Bgz[3]], w=[Blahl])
                op("dve", lambda e: e.tensor_tensor(out=z_x, in0=z_la, in1=la_hi, op=ALU.subtract), r=[Bgz[3], Blahl], w=[Bgz[5]])
                op("act", lambda e: e.activation(out=la_lo, in_=z_x, func=AF.Copy), r=[Bgz[5], Blahl], w=[Blahl])
                tI = trib[:n, 0, :n]; tS = trib[:n, 1, :n]
                for hl in range(2):
                    op("pe", lambda e, hl=hl: e.matmul(pBC[:n, 0:256], lhsT=tI, rhs=lahl[:n, hl, :], start=(hl == 0), stop=(hl == 1)), r=[Blahl, Btrib], w=[BpBC])
                for hl in range(2):
                    op("pe", lambda e, hl=hl: e.matmul(pBC[:n, 256:512], lhsT=tS, rhs=lahl[:n, hl, :], start=(hl == 0), stop=(hl == 1)), r=[Blahl, Btrib], w=[BpBC])
                for j in range(2):
                    for hl in range(2):
                        op("pe", lambda e, j=j, hl=hl: e.matmul(pSC[:, 256 + j:257 + j], lhsT=lahl[:n, hl, j * 128:(j + 1) * 128], rhs=trib[:n, 2, 0:1], start=(hl == 0), stop=(hl == 1)), r=[Blahl, Btrib], w=[BpSC])
                op("act", lambda e: e.activation(out=z_eb, in_=pBC[:n, 0:256], func=AF.Exp), r=[BpBC], w=[Bgz[4]])
                op("act", lambda e: e.activation(out=z_e, in_=pBC[:n, 0:256], func=AF.Exp, scale=-1.0), r=[BpBC, Bgz[1]], w=[Bgz[1]])
                op("act", lambda e: e.activation(out=z_l, in_=pBC[:n, 256:512], func=AF.Exp), r=[BpBC, Bgz[2]], w=[Bgz[2]])
                op("act", lambda e: e.activation(out=dec2[p][:, 0:2], in_=pSC[:, 256:258], func=AF.Exp), r=[BpSC], w=[Bdec2[p]])
                qf = gb16[:n, 0, :]; kt = gb16[:n, 1, :]
                op("dve", lambda e: e.scalar_tensor_tensor(out=qf, in0=qb, scalar=0.125, in1=z_eb, op0=ALU.mult, op1=ALU.mult), r=[Bpj[par], Bgz[4]], w=[Bgb16[0]])
                op("dve", lambda e: e.tensor_tensor(out=kt, in0=kb, in1=z_e, op=ALU.mult), r=[Bpj[par], Bgz[1]], w=[Bgb16[1]])
                op("dve", lambda e: e.tensor_tensor(out=kh2[p][:, :], in0=kb, in1=z_l, op=ALU.mult), r=[Bpj[par], Bgz[2]], w=[Bkh2[p]])
                for j in range(2):
                    op("pe", lambda e, j=j: e.transpose(out=pT2[:, j * 128:j * 128 + n], in_=qf[:, j * 128:(j + 1) * 128], identity=identb[:n, :n]), r=[Bgb16[0], Bident], w=[BpT2])
                    op("pe", lambda e, j=j: e.transpose(out=pT2[:, (2 + j) * 128:(2 + j) * 128 + n], in_=kt[:, j * 128:(j + 1) * 128], identity=identb[:n, :n]), r=[Bgb16[1], Bident], w=[BpT2])
                pv = pT2[:, 0:512].rearrange("p (a t) -> p a t", t=128)
                for i in range(2):
                    sl = slice(i * 64, (i + 1) * 64)
                    op("dve", lambda e, i=i, sl=sl: e.tensor_copy(out=qTm2[p][sl, i::2, :n], in_=pv[sl, 0:2, :n]), r=[BpT2, BqTm2[p]], w=[BqTm2[p]])
                    op("dve", lambda e, i=i, sl=sl: e.tensor_copy(out=kTm[sl, i::2, :n], in_=pv[sl, 2:4, :n]), r=[BpT2, BkTm], w=[BkTm])
                for h in range(4):
                    op("pe", lambda e, h=h: e.matmul(pSC[:n, h * 128:h * 128 + n], lhsT=kTm[:, h, :n], rhs=qTm2[p][:, h, :n], start=True, stop=True), r=[BkTm, BqTm2[p]], w=[BpSC])
                op("dve", lambda e: e.tensor_tensor(out=AT2[p][:n, :, :n], in0=pSC[:n, :].rearrange("p (h t) -> p h t", t=128)[:, :, :n], in1=maskST.unsqueeze(1).broadcast_to([n, 4, n]), op=ALU.mult), r=[BpSC, Bcst], w=[BAT2[p]])

            def gla_b(t):
                par = t % 3; p = t % 2; n = 128
                vv = gv16[par]; Bvv = Bgv16[par]
                for h in range(4):
                    j = h // 2
                    op("pe", lambda e, h=h: e.matmul(pO[:n, h * 128:(h + 1) * 128], lhsT=AT2[p][:n, h, :n], rhs=vv[:n, h, 0:128], start=True, stop=False), r=[BAT2[p], Bvv], w=[BpO])
                    op("pe", lambda e, h=h, j=j: e.matmul(pO[:n, h * 128:(h + 1) * 128], lhsT=qTm2[p][:, h, :n], rhs=S16[:, j, :], start=False, stop=True), r=[BqTm2[p], BS16], w=[BpO])
                for j in range(2):
                    op("pe", lambda e, j=j: e.matmul(pU2[:, j * 256:(j + 1) * 256], lhsT=kh2[p][:, j * 128:(j + 1) * 128], rhs=vv[:n, 2 * j:2 * j + 2, 0:128], start=True, stop=True), r=[Bkh2[p], Bvv], w=[BpU2])
                for j in range(2):
                    for i in range(2):
                        sl = slice(i * 64, (i + 1) * 64)
                        op("dve", lambda e, j=j, i=i, sl=sl: e.scalar_tensor_tensor(out=Sst[sl, j, :], in0=Sst[sl, j, :], scalar=dec2[p][sl, j:j + 1], in1=pU2[sl, j * 256 + i * 128:j * 256 + (i + 1) * 128], op0=ALU.mult, op1=ALU.add), r=[BSst, Bdec2[p], BpU2], w=[BSst])
                op("act", lambda e: e.activation(out=S16[:], in_=Sst[:], func=AF.Copy), r=[BSst], w=[BS16])
                o_sq = on[:n, 0, :]; o_t1 = on[:n, 1, :]
                op("act", lambda e: e.activation(out=o_sq, in_=pO[:n, :], func=AF.Square, scale=1.0 / math.sqrt(128.0)), r=[BpO], w=[Bon[0]])
                op("dve", lambda e: e.reduce_sum(out=sto[:n, 4:8], in_=o_sq.rearrange("p (h d) -> p h d", d=128), axis=AX.X), r=[Bon[0]], w=[Bsto])
                op("act", lambda e: e.activation(out=sto[:n, 8:12], in_=sto[:n, 4:8], func=AF.Ln, bias=EPS), r=[Bsto], w=[Bsto])
                op("act", lambda e: e.activation(out=sto[:n, 4:8], in_=sto[:n, 8:12], func=AF.Exp, scale=-0.5), r=[Bsto], w=[Bsto])
                op("dve", lambda e: e.tensor_tensor(out=o_t1.rearrange("p (h d) -> p h d", d=128), in0=pO[:n, :].rearrange("p (h d) -> p h d", d=128), in1=sto[:n, 4:8].unsqueeze(2).broadcast_to([n, 4, 128]), op=ALU.mult), r=[BpO, Bsto], w=[Bon[1]])
                op("dve", lambda e: e.tensor_tensor(out=o_t1, in0=o_t1, in1=gg[:n, :], op=ALU.mult), r=[Bon[1], Bgg], w=[Bon[1]])
                op("dve", lambda e: e.tensor_tensor(out=yb16[par][:, :], in0=o_t1, in1=t3_2[p][:, :], op=ALU.mult), r=[Bon[1], Bt3_2[p]], w=[Byb16[par]])
                op("sp", lambda e, t=t, par=par: e.dma_start(out=ym_scr[t * 128:(t + 1) * 128, 512:1024], in_=yb16[par][:, :]), r=[Byb16[par]], dma="s_yb%d" % par)

            def v_to_bf16(n, par):
                op("act", lambda e: e.activation(out=v16[par][:n, :, 0:128], in_=pj[par][:n, 1024:1536].rearrange("p (h d) -> p h d", d=128), func=AF.Copy), r=[Bpj[par]], w=[Bv16[par]])

            qTm2 = [sbt(stA, "qTm2_%d" % i, [128, 4, 128], BF16) for i in range(2)]; BqTm2 = [Buf("qTm2_0"), Buf("qTm2_1")]
            for i in range(2):
                op("pool", lambda e, i=i: e.memset(qTm2[i][:], 0.0), w=[BqTm2[i]])
            AT2 = [sbt(stA, "AT2_%d" % i, [128, 4, 128], BF16) for i in range(2)]; BAT2 = [Buf("AT2_0"), Buf("AT2_1")]
            kh2 = [sbt(stA, "kh2_%d" % i, [128, 256], BF16) for i in range(2)]; Bkh2 = [Buf("kh2_0"), Buf("kh2_1")]
            dec2 = [sbt(stA, "dec2_%d" % i, [128, 8], F32) for i in range(2)]; Bdec2 = [Buf("dec2_0"), Buf("dec2_1")]
            t3_2 = [sbt(stA, "t3_2_%d" % i, [128, 512], F32) for i in range(2)]; Bt3_2 = [Buf("t3_2_0"), Buf("t3_2_1")]
            lahl = sbt(stA, "lahl", [128, 2, 256], BF16); Blahl = Buf("lahl")
            trib = sbt(stA, "trib", [128, 3, 128], BF16); Btrib = Buf("trib")
            gv16 = [sbt(stA, "gv16_%d" % i, [128, 4, 128], BF16) for i in range(3)]; Bgv16 = [Buf("gv16_%d" % i) for i in range(3)]

            def gla_v(n, par):
                op("act", lambda e: e.activation(out=gv16[par][:n, :, :], in_=pj[par][:n, 2048:2560].rearrange("p (h d) -> p h d", d=128), func=AF.Copy), r=[Bpj[par]], w=[Bgv16[par]])

            NS_TOK = NSS * TS
            load_x(xs_d[:, :], NS_TOK, 0)
            op("sp", lambda e: e.dma_start(out=SstS[:].rearrange("p s j v -> p (s j) v"), in_=st_d.rearrange("s (j p) v -> p (s j) v", p=128)), w=[BSstS], dma="st")
            op("dve", lambda e: e.tensor_copy(out=S16S[:], in_=SstS[:]), r=[BSstS], w=[BS16S])
            proj_tile(NS_TOK, 0, ropeS[:NS_TOK, :], BropeS)
            if stop == "S1":
                P.enabled = False
            op("sp", lambda e: e.dma_start(out=nks_d[:, :], in_=qkn[0][:NS_TOK, 512:1024]), r=[Bqkn[0]], dma="o_nks")
            op("sp", lambda e: e.dma_start(out=nvs_d[:, :], in_=pj[0][:NS_TOK, 1024:1536]), r=[Bpj[0]], dma="o_nvs")
            if stop == "S2":
                P.enabled = False
            gla_v(NS_TOK, 0)
            gla(NS_TOK, 0, NSS, triIs[:NS_TOK, :NS_TOK], triSs[:NS_TOK, :NS_TOK], maskSTs[:NS_TOK, :NS_TOK], sel16s[:NS_TOK, :], SstS, BSstS, S16S, BS16S, True, ymS[:NS_TOK, 512:1024], BymS)
            op("sp", lambda e: e.dma_start(out=nglas_d.rearrange("s (j p) v -> p (s j) v", p=128), in_=SstS[:].rearrange("p s j v -> p (s j) v")), r=[BSstS], dma="o_nglas")

            qk_transposes(NS_TOK, 0, QTs, BQTs, 0, "qk")
            op("dve", lambda e: e.tensor_copy(out=vnewS[:, :], in_=pj[0][:NS_TOK, 1024:1536]), r=[Bpj[0]], w=[BvnewS])

            if stop == "S0":
                P.enabled = False
            load_x(meta_d[:, :], NMETA, 1)
            op("pool", lambda e: e.memset(Sst[:], 0.0), w=[BSst])
            proj_tile(NMETA, 1, ropeM[:, :], BropeM)
            op("sp", lambda e: e.dma_start(out=nk_d[0:NMETA, :], in_=qkn[1][:NMETA, 512:1024]), r=[Bqkn[1]], dma="o_nk1")
            op("sp", lambda e: e.dma_start(out=nv_d[0:NMETA, :], in_=pj[1][:NMETA, 1024:1536]), r=[Bpj[1]], dma="o_nv1")
            qk_transposes(NMETA, 1, qkT[1], BqkT[1], 0, "k")
            op("sp", lambda e: e.dma_start(out=kt_scr[:, :, 0:NMETA], in_=qkT[1][:, 4:8, 0:NMETA]), r=[BqkT[1]], dma="s_kt1")
            v_to_bf16(NMETA, 1)
            op("sp", lambda e: e.dma_start(out=v_scr[0:NMETA, :], in_=v16[1][:NMETA, :, :].rearrange("p h d -> p (h d)")), r=[Bv16[1]], dma="s_v1")
            gla_v(NMETA, 1)
            gla(NMETA, 1, 1, triI[:NMETA, :NMETA], triS[:NMETA, :NMETA], maskST[:NMETA, :NMETA], sel16p[:NMETA, :], Sst, BSst, S16, BS16, False, None, None)

            load_x(x_d[0:128, :], 128, 0)

            def stream_x(t):
                par = t % 3
                if t + 1 < NT:
                    load_x(x_d[(t + 1) * 128:(t + 2) * 128, :], 128, (t + 1) % 2)
                proj_tile_a(128, par, t % 2)

            def stream_y1(t):
                par = t % 3
                proj_tile_b(128, par, ropeP[:, t, :], BropeP)
                r0 = NMETA + t * 128
                op("sp", lambda e, r0=r0, par=par: e.dma_start(out=nk_d[r0:r0 + 128, :], in_=qkn[par][:, 512:1024]), r=[Bqkn[par]], dma="o_nk%d" % par)
                op("sp", lambda e, r0=r0, par=par: e.dma_start(out=nv_d[r0:r0 + 128, :], in_=pj[par][:, 1024:1536]), r=[Bpj[par]], dma="o_nv%d" % par)
                g4, sub = t // 4, t % 4
                gp = g4 % 2
                qk_transposes(128, par, qkT[gp], BqkT[gp], sub * 128, "qk")
                if sub == 3:
                    op("sp", lambda e, g4=g4, gp=gp: e.dma_start(out=qt_scr[:, :, g4 * 512:(g4 + 1) * 512], in_=qkT[gp][:, 0:4, :]), r=[BqkT[gp]], dma="s_qt%d" % gp)
                    op("sp", lambda e, g4=g4, gp=gp: e.dma_start(out=kt_scr[:, :, NMETA + g4 * 512:NMETA + (g4 + 1) * 512], in_=qkT[gp][:, 4:8, :]), r=[BqkT[gp]], dma="s_kt%d" % gp)
                v_to_bf16(128, par)
                op("sp", lambda e, r0=r0, par=par: e.dma_start(out=v_scr[r0:r0 + 128, :], in_=v16[par][:, :, :].rearrange("p h d -> p (h d)")), r=[Bv16[par]], dma="s_v%d" % par)

            state["trikey"] = (128, 1)
            op("dve", lambda e: e.tensor_copy(out=trib[:, 0, :], in_=triI), r=[Bcst, Btrib], w=[Btrib])
            op("dve", lambda e: e.tensor_copy(out=trib[:, 1, :], in_=triS), r=[Bcst, Btrib], w=[Btrib])
            op("dve", lambda e: e.tensor_copy(out=trib[:, 2, 0:1], in_=sel16p), r=[Bcst, Btrib], w=[Btrib])
            for step_ in range(NT + 3):
                strs = []; bs = []
                for fn_, tt, b_ in ((stream_x, step_, 0.0), (stream_y1, step_ - 1, 0.1), (gla_a, step_ - 2, 0.4), (gla_b, step_ - 3, 0.25)):
                    if 0 <= tt < NT:
                        P.begin(); fn_(tt); strs.append(P.end()); bs.append(b_)
                P.interleave(*strs, bias=bs)
            op("sp", lambda e: e.dma_start(out=ngla_d.rearrange("(j p) v -> p j v", p=128), in_=Sst[:]), r=[BSst], dma="o_ngla")
            P.barrier()

        stA.close()
        if stop == "A":
            P.enabled = False
        with contextlib.ExitStack() as stS:
            NKT_S = 17
            kst = sbt(stS, "kst", [128, NKT_S, 512], F32); Bkst = Buf("kst")
            vst = sbt(stS, "vst", [128, NKT_S, 512], F32); Bvst = Buf("vst")
            kb16s = sbt(stS, "kb16s", [128, NKT_S, 512], BF16); Bkb16s = Buf("kb16s")
            KTs = sbt(stS, "KTs", [128, 4, NKT_S * 128], BF16); BKTs = Buf("KTs")
            VAs = sbt(stS, "VAs", [128, NKT_S, 4, 129], BF16); BVAs = Buf("VAs")
            STs = sbt(stS, "STs", [128, 8, NKT_S * 16], F32); BSTs = Buf("STs")
            PTs = sbt(stS, "PTs", [128, 8, NKT_S * 16], BF16); BPTs = Buf("PTs")
            mx = sbt(stS, "mx", [128, 16], F32); Bmx = Buf("mx")
            mxT = sbt(stS, "mxT", [8, 128], F32); BmxT = Buf("mxT")
            mx1 = sbt(stS, "mx1", [8, 8], F32); Bmx1 = Buf("mx1")
            mxd = sbt(stS, "mxd", [8, 8], F32); Bmxd = Buf("mxd")
            ones8b = sbt(stS, "ones8b", [8, 128], BF16); Bones8 = Buf("ones8")
            mxb = sbt(stS, "mxb", [128, 8], BF16); Bmxb = Buf("mxb")
            mxdb = sbt(stS, "mxdb", [8, 8], BF16)
            oas = sbt(stS, "oas", [16, 4, 128], F32); Boas = Buf("oas")
            oat = sbt(stS, "oat", [16, 128], F32); Boat = Buf("oat")
            osm = sbt(stS, "osm", [16, 32], F32); Bosm = Buf("osm")
            ya_s = sbt(stS, "ya_s", [16, 512], BF16); Bya_s = Buf("ya_s")
            junk16 = sbt(stS, "junk16", [16, 128], F32); Bjunk16 = Buf("junk16")
            sA = pst(stS, "sA", [128, 8, 128], BF16); BsA = Buf("sA")
            sP = [pst(stS, "sP%d" % i, [128, 512], F32) for i in range(2)]; BsP = [Buf("sP0"), Buf("sP1")]
            sM = pst(stS, "sM", [128, 512], F32); BsM = Buf("sM")
            sO = pst(stS, "sO", [128, 512], F32); BsO = Buf("sO")
            op("pool", lambda e: e.memset(ones8b[:], 1.0), w=[Bones8])
            QTm = sbt(stS, "QTm", [128, 2, 4, 64], BF16); BQTm = Buf("QTm")
            op("pool", lambda e: e.memset(QTm[:], 0.0), w=[BQTm])
            for c in range(2):
                op("dve", lambda e, c=c: e.tensor_copy(out=QTm[c * 64:(c + 1) * 64, c, :, :], in_=QTs[c * 64:(c + 1) * 64, 0:4, :]), r=[BQTs, BQTm], w=[BQTm])
            op("pool", lambda e: e.memset(VAs[:], 1.0), w=[BVAs])
            op("pool", lambda e: e.memset(kb16s[:], 0.0), w=[Bkb16s])
            def sa_load(s):
                op("sp", lambda e, s=s: e.dma_start(out=kst[:, 0:16, :], in_=ck_d[s, 0:2048, :].rearrange("(p t) c -> p t c", p=128)), w=[Bkst], dma="kst")
                op("sp", lambda e, s=s: e.dma_start(out=kst[0:16, 16, :], in_=ck_d[s, 2048:2064, :]), w=[Bkst], dma="kst")
                op("sp", lambda e, s=s: e.dma_start(out=vst[:, 0:16, :], in_=cv_d[s, 0:2048, :].rearrange("(p t) c -> p t c", p=128)), w=[Bvst], dma="vst")
                op("sp", lambda e, s=s: e.dma_start(out=vst[0:16, 16, :], in_=cv_d[s, 2048:2064, :]), w=[Bvst], dma="vst")
                op("sp", lambda e, s=s: e.dma_start(out=vst[16:32, 16, :], in_=vnewS[s * TS:(s + 1) * TS, :]), r=[BvnewS], w=[Bvst], dma="vst")

            sa_load(0)
            for s in range(NSS):
                op("dve", lambda e: e.tensor_copy(out=kb16s[:, 0:16, :], in_=kst[:, 0:16, :]), r=[Bkst], w=[Bkb16s])
                op("dve", lambda e: e.tensor_copy(out=kb16s[0:16, 16, :], in_=kst[0:16, 16, :]), r=[Bkst, Bkb16s], w=[Bkb16s])
                op("act", lambda e: e.activation(out=VAs[:, 0:16, :, 0:128], in_=vst[:, 0:16, :].rearrange("p t (h d) -> p t h d", d=128), func=AF.Copy), r=[Bvst], w=[BVAs])
                op("act", lambda e: e.activation(out=VAs[0:32, 16, :, 0:128], in_=vst[0:32, 16, :].rearrange("p (h d) -> p h d", d=128), func=AF.Copy), r=[Bvst, BVAs], w=[BVAs])
                if s + 1 < NSS:
                    sa_load(s + 1)
                for t in range(NKT_S):
                    rows = 128 if t < 16 else 16
                    for h in range(4):
                        op("pe", lambda e, t=t, h=h, rows=rows: e.transpose(out=sA[:, h, :rows], in_=kb16s[:rows, t, h * 128:(h + 1) * 128], identity=identb[:rows, :rows]), r=[Bkb16s, Bident], w=[BsA])
                    eng = "dve" if t % 2 == 0 else "act"
                    if eng == "dve":
                        op("dve", lambda e, t=t, rows=rows: e.tensor_copy(out=KTs[:, :, t * 128:t * 128 + rows], in_=sA[:, 0:4, :rows]), r=[BsA], w=[BKTs])
                    else:
                        op("act", lambda e, t=t, rows=rows: e.activation(out=KTs[:, :, t * 128:t * 128 + rows], in_=sA[:, 0:4, :rows], func=AF.Copy), r=[BsA], w=[BKTs])
                op("dve", lambda e, s=s: e.tensor_copy(out=KTs[:, :, 16 * 128 + 16:16 * 128 + 32], in_=QTs[:, 4:8, s * TS:(s + 1) * TS]), r=[BQTs, BKTs], w=[BKTs])
                for h in range(4):
                    for c in range(2):
                        hc = h * 2 + c
                        pp = state["ppar"]; state["ppar"] ^= 1
                        for t in range(NKT_S):
                            rows = 128 if t < 16 else 32
                            op("pe", lambda e, t=t, h=h, c=c, rows=rows, pp=pp, s=s: e.matmul(sP[pp][:rows, t * 16:(t + 1) * 16], lhsT=KTs[:, h, t * 128:t * 128 + rows], rhs=QTm[:, c, h, s * TS:(s + 1) * TS], start=True, stop=True), r=[BKTs, BQTm], w=[BsP[pp]])
                        op("dve", lambda e, hc=hc, pp=pp: e.tensor_copy(out=STs[:, hc, 0:256], in_=sP[pp][:, 0:256]), r=[BsP[pp]], w=[BSTs])
                        op("dve", lambda e, hc=hc, pp=pp: e.tensor_copy(out=STs[0:32, hc, 256:272], in_=sP[pp][0:32, 256:272]), r=[BsP[pp], BSTs], w=[BSTs])
                op("dve", lambda e: e.tensor_reduce(out=mx[:, 0:8], in_=STs[:, :, 0:256], axis=AX.X, op=ALU.max), r=[BSTs], w=[Bmx])
                op("dve", lambda e: e.tensor_reduce(out=mx[0:32, 8:16], in_=STs[0:32, :, 256:272], axis=AX.X, op=ALU.max), r=[BSTs, Bmx], w=[Bmx])
                op("dve", lambda e: e.tensor_tensor(out=mx[0:32, 0:8], in0=mx[0:32, 0:8], in1=mx[0:32, 8:16], op=ALU.max), r=[Bmx], w=[Bmx])
                op("dve", lambda e: e.tensor_copy(out=mxb[:, 0:8], in_=mx[:, 0:8]), r=[Bmx], w=[Bmxb])
                op("pe", lambda e: e.transpose(out=sA[0:8, 0, :], in_=mxb[:, 0:8], identity=identb[:, :]), r=[Bmxb, Bident], w=[BsA])
                op("dve", lambda e: e.tensor_reduce(out=mx1[:, 0:1], in_=sA[0:8, 0, :], axis=AX.X, op=ALU.max), r=[BsA], w=[Bmx1])
                op("dve", lambda e: e.tensor_scalar(out=mxdb[:, :], in0=ident_f[0:8, 0:8], scalar1=mx1[:, 0:1], scalar2=-0.125, op0=ALU.mult, op1=ALU.mult), r=[Bmx1, Bcst], w=[Bmxd])
                op("pe", lambda e: e.matmul(sM[:, 128:136], lhsT=ones8b[:, :], rhs=mxdb[:, :], start=True, stop=True), r=[Bones8, Bmxd], w=[BsM])
                op("dve", lambda e: e.tensor_copy(out=mx[:, 8:16], in_=sM[:, 128:136]), r=[BsM, Bmx], w=[Bmx])
                for hc in range(8):
                    op("act", lambda e, hc=hc: e.activation(out=PTs[:, hc, 0:256], in_=STs[:, hc, 0:256], func=AF.Exp, scale=0.125, bias=mx[:, 8 + hc:9 + hc]), r=[BSTs, Bmx], w=[BPTs])
                    op("act", lambda e, hc=hc: e.activation(out=PTs[0:32, hc, 256:272], in_=STs[0:32, hc, 256:272], func=AF.Exp, scale=0.125, bias=mx[0:32, 8 + hc:9 + hc]), r=[BSTs, Bmx, BPTs], w=[BPTs])
                for h in range(4):
                    for c in range(2):
                        hc = h * 2 + c
                        for t in range(NKT_S):
                            rows = 128 if t < 16 else 32
                            op("pe", lambda e, t=t, h=h, c=c, hc=hc, rows=rows: e.matmul(sO[0:16, c * 129:(c + 1) * 129], lhsT=PTs[:rows, hc, t * 16:(t + 1) * 16], rhs=VAs[:rows, t, h, :], start=(t == 0), stop=(t == NKT_S - 1)), r=[BPTs, BVAs], w=[BsO])
                    op("dve", lambda e: e.reciprocal(out=osm[:, 0:1], in_=sO[0:16, 128:129]), r=[BsO], w=[Bosm])
                    op("dve", lambda e: e.reciprocal(out=osm[:, 1:2], in_=sO[0:16, 257:258]), r=[BsO, Bosm], w=[Bosm])
                    op("dve", lambda e: e.tensor_tensor(out=osm[:, 2:3], in0=osm[:, 1:2], in1=neglam[0:16, :], op=ALU.mult), r=[Bosm, Bsm], w=[Bosm])
                    op("dve", lambda e: e.tensor_scalar(out=oat[:, :], in0=sO[0:16, 0:128], scalar1=osm[:, 0:1], scalar2=None, op0=ALU.mult), r=[BsO, Bosm], w=[Boat])
                    op("dve", lambda e, h=h: e.scalar_tensor_tensor(out=oas[:, h, :], in0=sO[0:16, 129:257], scalar=osm[:, 2:3], in1=oat[:, :], op0=ALU.mult, op1=ALU.add), r=[BsO, Bosm, Boat], w=[Boas])
                    op("act", lambda e, h=h: e.activation(out=junk16[:, :], in_=oas[:, h, :], func=AF.Square, scale=1.0 / math.sqrt(128.0), accum_out=osm[:, 4 + h:5 + h]), r=[Boas], w=[Bjunk16, Bosm])
                op("act", lambda e: e.activation(out=osm[:, 8:12], in_=osm[:, 4:8], func=AF.Ln, bias=EPS), r=[Bosm], w=[Bosm])
                op("act", lambda e: e.activation(out=osm[:, 12:16], in_=osm[:, 8:12], func=AF.Exp, scale=-0.5), r=[Bosm], w=[Bosm])
                for h in range(4):
                    op("dve", lambda e, h=h: e.scalar_tensor_tensor(out=ya_s[:, h * 128:(h + 1) * 128], in0=oas[:, h, :], scalar=osm[:, 12 + h:13 + h], in1=gd[0:16, h * 128:(h + 1) * 128], op0=ALU.mult, op1=ALU.mult), r=[Boas, Bosm, Bgd], w=[Bya_s])
                op("sp", lambda e, s=s: e.dma_start(out=ymS[s * TS:(s + 1) * TS, 0:512], in_=ya_s[:, :]), r=[Bya_s], w=[BymS], dma="yms")
            P.barrier()


        if stop == "SA":
            P.enabled = False
        NKT = NT + 1
        NQB = S // 512
        with contextlib.ExitStack() as stB:
            KT = sbt(stB, "KT", [128, 4, NMETA + S], BF16); BKT = Buf("KT")
            KTm = sbt(stB, "KTm", [128, 4, 128], BF16); BKTm = Buf("KTm")
            VA = sbt(stB, "VA", [128, NKT, 516], BF16); BVA = Buf("VA")
            QB = [sbt(stB, "QB%d" % i, [128, 2, 4, 512], BF16) for i in range(2)]; BQB = [Buf("QB0"), Buf("QB1")]
            for i in range(2):
                op("pool", lambda e, i=i: e.memset(QB[i][:], 0.0), w=[BQB[i]])
            PT2 = [sbt(stB, "PT2_%d" % i, [128, 2, 512], BF16) for i in range(2)]; BPT2 = [Buf("PT2_0"), Buf("PT2_1")]
            fsm = sbt(stB, "fsm", [128, 32], F32); Bfsm = Buf("fsm")
            facc = sbt(stB, "facc", [128, 8, 129], F32); Bfacc = Buf("facc")
            ft8 = sbt(stB, "ft8", [128, 8, 128], F32); Bft8 = Buf("ft8")
            foa = sbt(stB, "foa", [128, 4, 128], F32); Bfoa = Buf("foa")
            fsq = sbt(stB, "fsq", [128, 4, 128], F32); Bfsq = Buf("fsq")
            yab = [sbt(stB, "yab%d" % i, [128, 4, 512], BF16) for i in range(2)]; Byab = [Buf("yab0"), Buf("yab1")]
            pS2 = [pst(stB, "pS2_%d" % i, [128, 2, 512], F32) for i in range(2)]; BpS2 = [Buf("pS2_0"), Buf("pS2_1")]
            pAcc = [pst(stB, "pAcc%d" % i, [128, 512], F32) for i in range(3)]
            BAcc = {}

            def acc_ap(c, j):
                if j < 3:
                    return pAcc[c][:, j * 129:(j + 1) * 129]
                return pAcc[2][:, c * 129:(c + 1) * 129]
            BAccBank = [Buf("accbank%d" % i) for i in range(3)]
            for c in range(2):
                for j in range(4):
                    BAcc[(c, j)] = BAccBank[c] if j < 3 else BAccBank[2]

            nchunk = max(1, (NMETA + S) // 2048)
            BKTc = [Buf("KTc%d" % i) for i in range(nchunk)]
            kedges = [0] + [NMETA + (i + 1) * (S // nchunk) for i in range(nchunk)]

            def kt_buf(kc0, rows):
                return [BKTc[i] for i in range(nchunk) if kc0 < kedges[i + 1] and kc0 + rows > kedges[i]]
            tper = 8
            BVAc = [Buf("VAc%d" % i) for i in range((NT + tper - 1) // tper)]

            def va_buf(vt):
                return BVAc[0] if vt == 0 else BVAc[(vt - 1) // tper]
            op("pool", lambda e: e.memset(KTm[:], 0.0), w=[BKTm])
            op("pool", lambda e: e.memset(VA[:, 0, :], 0.0), w=[BVAc[0]])
            op("sp", lambda e: e.dma_start(out=KTm[:, :, 0:NMETA], in_=kt_scr[:, :, 0:NMETA]), w=[BKTm], dma="l_ktm")
            op("sp", lambda e: e.dma_start(out=VA[0:NMETA, 0, :], in_=v_scr[0:NMETA, :]), w=[BVAc[0]], dma="l_va0")
            tiles_per_chunk = max(1, NT // nchunk)
            for i in range(nchunk):
                a_, b_ = kedges[i], kedges[i + 1]
                for h in range(4):
                    op("sp", lambda e, h=h, a_=a_, b_=b_: e.dma_start(out=KT[:, h, a_:b_], in_=kt_scr[:, h, a_:b_]), w=[BKTc[i]], dma="l_kt%d" % i)
                for t0 in range(i * tiles_per_chunk, (i + 1) * tiles_per_chunk if i + 1 < nchunk else NT, tper):
                    t1 = min(NT, t0 + tper)
                    op("sp", lambda e, t0=t0, t1=t1: e.dma_start(out=VA[:, 1 + t0:1 + t1, :], in_=v_scr[NMETA + t0 * 128:NMETA + t1 * 128, :].rearrange("(t p) c -> p t c", p=128)), w=[BVAc[t0 // tper]], dma="l_va%d" % (t0 // tper))

            def load_q(qb):
                qp = qb % 2
                for c in range(2):
                    op("sp", lambda e, qb=qb, qp=qp, c=c: e.dma_start(out=QB[qp][c * 64:(c + 1) * 64, c, :, :], in_=qt_scr[c * 64:(c + 1) * 64, :, qb * 512:(qb + 1) * 512]), w=[BQB[qp]], dma="l_q%d" % qp)

            steps = []
            for qb in range(NQB):
                for h in range(4):
                    ktiles = [(-1, 0)] + [(i, 0) for i in range(4 * qb)] + [(4 * qb + d, d) for d in range(4)]
                    for n_, (ki, d) in enumerate(ktiles):
                        steps.append(dict(qb=qb, h=h, ki=ki, d=d, first=(n_ == 0), last=(n_ == len(ktiles) - 1)))

            def geom(stp):
                if stp["ki"] < 0:
                    return 128, -1, 0
                return 128, NMETA + stp["ki"] * 128, 1 + stp["ki"]

            def front(stp, idx):
                sp_ = idx % 2
                qb, h, ki, d = stp["qb"], stp["h"], stp["ki"], stp["d"]
                qp = qb % 2
                if stp["first"] and h == 0 and qb + 1 < NQB:
                    load_q(qb + 1)
                rows, kc0, vt = geom(stp)
                q0 = d * 128
                for c in range(2):
                    op("pe", lambda e, c=c, sp_=sp_, rows=rows, kc0=kc0, q0=q0, h=h, qp=qp: e.matmul(pS2[sp_][:rows, c, q0:512], lhsT=(KTm[:, h, :] if kc0 < 0 else KT[:, h, kc0:kc0 + rows]), rhs=QB[qp][:, c, h, q0:512], start=True, stop=True), r=([BKTm] if kc0 < 0 else kt_buf(kc0, rows)) + [BQB[qp]], w=[BpS2[sp_]])
                op("act", lambda e, sp_=sp_, rows=rows, q0=q0: e.activation(out=PT2[sp_][:rows, :, q0:512], in_=pS2[sp_][:rows, :, q0:512], func=AF.Exp, scale=0.125, bias=negB[:rows, :]), r=[BpS2[sp_], Bsm], w=[BPT2[sp_]])
                if ki >= 4 * qb:
                    op("pool", lambda e, sp_=sp_, q0=q0: e.memset(PT2[sp_][64:128, :, q0:q0 + 64], 0.0), r=[BPT2[sp_]], w=[BPT2[sp_]])

            def back(stp, idx):
                sp_ = idx % 2
                qb, h, ki, d = stp["qb"], stp["h"], stp["ki"], stp["d"]
                qp = qb % 2
                rows, kc0, vt = geom(stp)
                for c in range(2):
                    for j in range(d, 4):
                        first = (ki < 0) and ((j == 0) or (j == 3 and c == 0))
                        last = (ki == 4 * qb + j)
                        op("pe", lambda e, c=c, j=j, sp_=sp_, rows=rows, vt=vt, h=h, first=first, last=last: e.matmul(acc_ap(c, j), lhsT=PT2[sp_][:rows, c, j * 128:(j + 1) * 128], rhs=VA[:rows, vt, h * 129:(h + 1) * 129], start=first, stop=last, skip_group_check=True), r=[BPT2[sp_], va_buf(vt)], w=[BAcc[(c, j)]])
                if not stp["last"]:
                    return
                for c in range(2):
                    op("dve", lambda e, c=c: e.tensor_copy(out=facc[:, c * 4:c * 4 + 3, :], in_=pAcc[c][:, 0:387].rearrange("p (j e) -> p j e", e=129)), r=[BAccBank[c], Bfacc], w=[Bfacc])
                op("dve", lambda e: e.tensor_copy(out=facc[:, 3::4, :], in_=pAcc[2][:, 0:258].rearrange("p (j e) -> p j e", e=129)), r=[BAccBank[2], Bfacc], w=[Bfacc])
                op("dve", lambda e: e.reciprocal(out=fsm[:, 0:8], in_=facc[:, :, 128]), r=[Bfacc], w=[Bfsm])
                op("dve", lambda e: e.tensor_scalar(out=fsm[:, 4:8], in0=fsm[:, 4:8], scalar1=neglam, scalar2=None, op0=ALU.mult), r=[Bfsm, Bsm], w=[Bfsm])
                op("dve", lambda e: e.tensor_tensor(out=ft8[:], in0=facc[:, :, 0:128], in1=fsm[:, 0:8].unsqueeze(2).broadcast_to([128, 8, 128]), op=ALU.mult), r=[Bfacc, Bfsm], w=[Bft8])
                op("dve", lambda e: e.tensor_tensor(out=foa[:], in0=ft8[:, 0:4, :], in1=ft8[:, 4:8, :], op=ALU.add), r=[Bft8], w=[Bfoa])
                op("dve", lambda e: e.tensor_tensor(out=fsq[:], in0=foa[:], in1=foa[:], op=ALU.mult), r=[Bfoa], w=[Bfsq])
                op("dve", lambda e: e.reduce_sum(out=fsm[:, 8:12], in_=fsq[:], axis=AX.X), r=[Bfsq], w=[Bfsm])
                op("act", lambda e: e.activation(out=fsm[:, 12:16], in_=fsm[:, 8:12], func=AF.Ln, scale=1.0 / 128.0, bias=EPS), r=[Bfsm], w=[Bfsm])
                op("act", lambda e: e.activation(out=fsm[:, 16:20], in_=fsm[:, 12:16], func=AF.Exp, scale=-0.5), r=[Bfsm], w=[Bfsm])
                op("dve", lambda e: e.tensor_tensor(out=fsq[:], in0=foa[:], in1=fsm[:, 16:20].unsqueeze(2).broadcast_to([128, 4, 128]), op=ALU.mult), r=[Bfoa, Bfsm, Bfsq], w=[Bfsq])
                op("dve", lambda e, h=h, qp=qp: e.tensor_tensor(out=yab[qp][:, :, h * 128:(h + 1) * 128], in0=fsq[:], in1=gd[:, h * 128:(h + 1) * 128].unsqueeze(1).broadcast_to([128, 4, 128]), op=ALU.mult), r=[Bfsq, Bgd], w=[Byab[qp]])
                if h == 3:
                    op("sp", lambda e, qb=qb, qp=qp: e.dma_start(out=ym_scr[qb * 512:(qb + 1) * 512, 0:512].rearrange("(j p) c -> p j c", p=128), in_=yab[qp][:]), r=[Byab[qp]], dma="s_ya%d" % qp)

            load_q(0)
            for idx, stp in enumerate(steps):
                front(stp, idx)
                if idx > 0:
                    back(steps[idx - 1], idx - 1)
            back(steps[-1], len(steps) - 1)
            P.barrier()

        if stop == "B":
            P.enabled = False
        stAB.close()
        with contextlib.ExitStack() as stC:
            WO = sbt(stC, "WO", [128, 8, D], BF16); BWO = Buf("WO")
            WG = sbt(stC, "WG", [128, 8, DFF], BF16); BWG = Buf("WG")
            WU = sbt(stC, "WU", [128, 8, DFF], BF16); BWU = Buf("WU")
            WD = sbt(stC, "WD", [128, NFC, D], BF16); BWD = Buf("WD")
            GT = 256
            ym = [sbt(stC, "ym%d" % i, [128, D], BF16) for i in range(2)]; Bym = [Buf("ym0"), Buf("ym1")]
            xc = [sbt(stC, "xc%d" % i, [128, D], F32) for i in range(1)]; Bxc = [Buf("xc0")]
            ymT = sbt(stC, "ymT", [128, 8, 128], BF16); BymT = Buf("ymT")
            h1 = [sbt(stC, "h1_%d" % i, [128, 2, D], F32) for i in range(2)]; Bh1 = [[Buf("h1_%d_%d" % (i, s_)) for s_ in range(2)] for i in range(2)]
            h1n = sbt(stC, "h1n", [128, D], BF16); Bh1n = Buf("h1n")
            h1nT = [sbt(stC, "h1nT%d" % i, [128, 8, GT], BF16) for i in range(2)]; Bh1nT = [Buf("h1nT0"), Buf("h1nT1")]
            hhT = sbt(stC, "hhT", [128, NFC, GT], BF16); BhhT = Buf("hhT")
            sgb = [sbt(stC, "sgb%d" % i, [128, GT], BF16) for i in range(2)]; Bsgb = [Buf("sgb0"), Buf("sgb1")]
            csm = sbt(stC, "csm", [128, 8], F32); Bcsm = Buf("csm")
            pTp = pst(stC, "pTp", [128, 8, 128], BF16); BpTp = Buf("pTp")
            pH = [pst(stC, "pH%d" % i, [128, 512], F32) for i in range(2)]; BpH = [Buf("pH0"), Buf("pH1")]
            pG = [pst(stC, "pG%d" % i, [128, 512], F32) for i in range(2)]; BpG = [Buf("pG0"), Buf("pG1")]
            pU_ = [pst(stC, "pUu%d" % i, [128, 512], F32) for i in range(2)]; BpUu = [Buf("pU0"), Buf("pU1")]
            pD = pst(stC, "pD", [128, 512], F32); BpD = Buf("pD")

            SW = 704
            NSTG = 3
            wstc = [sbt(stC, "wstc%d" % i, [128, SW], F32) for i in range(NSTG)]
            Bwstc = [Buf("wstc%d" % i) for i in range(NSTG)]
            wk = dict(k=0)

            def wload(src_rows, ncols, dst_fn, Bd, fold_col):
                c0 = 0
                while c0 < ncols:
                    cw = min(SW, ncols - c0)
                    i = wk["k"] % NSTG; wk["k"] += 1
                    op("sp", lambda e, i=i, c0=c0, cw=cw: e.dma_start(out=wstc[i][:, :cw], in_=src_rows[:, c0:c0 + cw]), w=[Bwstc[i]], dma="wstc%d" % i)
                    dst = dst_fn(c0, cw)
                    if wk["k"] % 2 == 0:
                        if fold_col is None:
                            op("dve", lambda e, i=i, cw=cw, dst=dst: e.tensor_copy(out=dst, in_=wstc[i][:, :cw]), r=[Bwstc[i]], w=[Bd])
                        else:
                            op("dve", lambda e, i=i, cw=cw, dst=dst: e.tensor_scalar(out=dst, in0=wstc[i][:, :cw], scalar1=fold_col, scalar2=None, op0=ALU.mult), r=[Bwstc[i], Bgffn], w=[Bd])
                    else:
                        if fold_col is None:
                            op("act", lambda e, i=i, cw=cw, dst=dst: e.activation(out=dst, in_=wstc[i][:, :cw], func=AF.Copy), r=[Bwstc[i]], w=[Bd])
                        else:
                            op("act", lambda e, i=i, cw=cw, dst=dst: e.activation(out=dst, in_=wstc[i][:, :cw], func=AF.Copy, scale=fold_col), r=[Bwstc[i], Bgffn], w=[Bd])
                    c0 += cw

            for kc in range(8):
                wload(wout_d[kc * 128:(kc + 1) * 128, :], D, (lambda c0, cw, kc=kc: WO[:, kc, c0:c0 + cw]), BWO, None)
            for kc in range(8):
                wload(wg_d[kc * 128:(kc + 1) * 128, :], DFF, (lambda c0, cw, kc=kc: WG[:, kc, c0:c0 + cw]), BWG, gffn[:, kc:kc + 1])
                wload(wu_d[kc * 128:(kc + 1) * 128, :], DFF, (lambda c0, cw, kc=kc: WU[:, kc, c0:c0 + cw]), BWU, gffn[:, kc:kc + 1])
            for fc in range(NFC):
                wload(wd_d[fc * 128:(fc + 1) * 128, :], D, (lambda c0, cw, fc=fc: WD[:, fc, c0:c0 + cw]), BWD, None)

            groups = [(g * GT, GT, False) for g in range(S // GT)] + [(0, NSS * TS, True)]
            cstate = dict(ldi=0, fstep=0)

            def c_pro(gi):
                g0, gt, is_s = groups[gi]
                hp = gi % 2
                nsub = max(1, gt // 128)
                n = min(128, gt)
                for sub in range(nsub):
                    lp = cstate["ldi"] % 2; cstate["ldi"] += 1
                    if is_s:
                        op("sp", lambda e, n=n: e.dma_start(out=xc[0][:n, :], in_=xs_d[:, :]), w=[Bxc[0]], dma="l_xc0")
                        ymsrc = ymS; Bymsrc = BymS
                    else:
                        r0 = g0 + sub * 128
                        op("sp", lambda e, lp=lp, r0=r0: e.dma_start(out=ym[lp][:], in_=ym_scr[r0:r0 + 128, :]), w=[Bym[lp]], dma="l_ym%d" % lp)
                        op("sp", lambda e, r0=r0: e.dma_start(out=xc[0][:], in_=x_d[r0:r0 + 128, :]), w=[Bxc[0]], dma="l_xc0")
                        ymsrc = ym[lp]; Bymsrc = Bym[lp]
                    for kc in range(8):
                        op("pe", lambda e, kc=kc, n=n, ymsrc=ymsrc: e.transpose(out=pTp[:, kc, :n], in_=ymsrc[:n, kc * 128:(kc + 1) * 128], identity=identb[:n, :n]), r=[Bymsrc, Bident], w=[BpTp])
                    op("act", lambda e, n=n: e.activation(out=ymT[:, :, :n], in_=pTp[:, :, :n], func=AF.Copy), r=[BpTp], w=[BymT])
                    for half in range(2):
                        for kc in range(8):
                            op("pe", lambda e, kc=kc, half=half, n=n: e.matmul(pH[half][:n, :], lhsT=ymT[:, kc, :n], rhs=WO[:, kc, half * 512:(half + 1) * 512], start=(kc == 0), stop=(kc == 7)), r=[BymT, BWO], w=[BpH[half]])
                        op("dve", lambda e, half=half, n=n, hp=hp, sub=sub: e.tensor_tensor(out=h1[hp][:n, sub, half * 512:(half + 1) * 512], in0=pH[half][:n, :], in1=xc[0][:n, half * 512:(half + 1) * 512], op=ALU.add), r=[BpH[half], Bxc[0]], w=[Bh1[hp][sub]])
                    op("act", lambda e, n=n, hp=hp, sub=sub: e.activation(out=h1n[:n, :], in_=h1[hp][:n, sub, :], func=AF.Square, scale=1.0 / 32.0, accum_out=csm[:n, 0:1]), r=[Bh1[hp][sub]], w=[Bh1n, Bcsm])
                    op("act", lambda e, n=n: e.activation(out=csm[:n, 1:2], in_=csm[:n, 0:1], func=AF.Ln, bias=EPS), r=[Bcsm], w=[Bcsm])
                    op("act", lambda e, n=n: e.activation(out=csm[:n, 2:3], in_=csm[:n, 1:2], func=AF.Exp, scale=-0.5), r=[Bcsm], w=[Bcsm])
                    op("act", lambda e, n=n, hp=hp, sub=sub: e.activation(out=h1n[:n, :], in_=h1[hp][:n, sub, :], func=AF.Copy, scale=csm[:n, 2:3]), r=[Bh1[hp][sub], Bcsm, Bh1n], w=[Bh1n])
                    for kc in range(8):
                        op("pe", lambda e, kc=kc, n=n: e.transpose(out=pTp[:, kc, :n], in_=h1n[:n, kc * 128:(kc + 1) * 128], identity=identb[:n, :n]), r=[Bh1n, Bident], w=[BpTp])
                    op("dve", lambda e, n=n, sub=sub, hp=hp: e.tensor_copy(out=h1nT[hp][:, :, sub * 128:sub * 128 + n], in_=pTp[:, :, :n]), r=[BpTp], w=[Bh1nT[hp]])

            def c_ffn(gi):
                g0, gt, is_s = groups[gi]
                hp = gi % 2
                nsub = max(1, gt // 128)
                n = min(128, gt)
                for fc in range(NFC):
                    fp = cstate["fstep"] % 2; cstate["fstep"] += 1
                    for kc in range(8):
                        op("pe", lambda e, kc=kc, fc=fc, fp=fp, gt=gt, hp=hp: e.matmul(pG[fp][:, :gt], lhsT=WG[:, kc, fc * 128:(fc + 1) * 128], rhs=h1nT[hp][:, kc, :gt], start=(kc == 0), stop=(kc == 7)), r=[BWG, Bh1nT[hp]], w=[BpG[fp]])
                    for kc in range(8):
                        op("pe", lambda e, kc=kc, fc=fc, fp=fp, gt=gt, hp=hp: e.matmul(pU_[fp][:, :gt], lhsT=WU[:, kc, fc * 128:(fc + 1) * 128], rhs=h1nT[hp][:, kc, :gt], start=(kc == 0), stop=(kc == 7)), r=[BWU, Bh1nT[hp]], w=[BpUu[fp]])
                    op("act", lambda e, fp=fp, gt=gt: e.activation(out=sgb[fp][:, :gt], in_=pG[fp][:, :gt], func=AF.Silu), r=[BpG[fp]], w=[Bsgb[fp]])
                    op("dve", lambda e, fp=fp, fc=fc, gt=gt: e.tensor_tensor(out=hhT[:, fc, :gt], in0=pU_[fp][:, :gt], in1=sgb[fp][:, :gt], op=ALU.mult), r=[BpUu[fp], Bsgb[fp]], w=[BhhT])
                for sub in range(nsub):
                    for half in range(2):
                        for fc in range(NFC):
                            op("pe", lambda e, fc=fc, half=half, n=n, sub=sub: e.matmul(pD[:n, :], lhsT=hhT[:, fc, sub * 128:sub * 128 + n], rhs=WD[:, fc, half * 512:(half + 1) * 512], start=(fc == 0), stop=(fc == NFC - 1)), r=[BhhT, BWD], w=[BpD])
                        op("dve", lambda e, half=half, n=n, hp=hp, sub=sub: e.tensor_tensor(out=h1[hp][:n, sub, half * 512:(half + 1) * 512], in0=pD[:n, :], in1=h1[hp][:n, sub, half * 512:(half + 1) * 512], op=ALU.add), r=[BpD, Bh1[hp][sub]], w=[Bh1[hp][sub]])
                    if is_s:
                        op("sp", lambda e, n=n, hp=hp, sub=sub: e.dma_start(out=ys_d[:, :], in_=h1[hp][:n, sub, :]), r=[Bh1[hp][sub]], dma="o_ys")
                    else:
                        r0 = g0 + sub * 128
                        op("sp", lambda e, r0=r0, hp=hp, sub=sub: e.dma_start(out=y_d[r0:r0 + 128, :], in_=h1[hp][:, sub, :]), r=[Bh1[hp][sub]], dma="o_y%d_%d" % (hp, sub))

            c_pro(0)
            for gi in range(len(groups)):
                P.begin(); c_ffn(gi); Yc = P.end()
                if gi + 1 < len(groups):
                    P.begin(); c_pro(gi + 1); Xc = P.end()
                    P.interleave(Xc, Yc)
                else:
                    P.replay(Yc)
            P.barrier()
        P.emit(nc)
    return nc


def _consts():
    c = np.zeros((128, NCONST), np.float32)
    idx = np.arange(128)
    c[:, 0:128] = np.eye(128, dtype=np.float32)
    le = (idx[:, None] <= idx[None, :]).astype(np.float32)
    gt = (idx[:, None] > idx[None, :]).astype(np.float32)
    c[:, 128:256] = le / 16.0
    c[:, 256:384] = gt / 16.0
    c[:, 384:512] = le
    i64 = np.arange(64)
    same = (i64[:, None] // TS == i64[None, :] // TS).astype(np.float32)
    c[:64, 512:576] = same * le[:64, :64] / 16.0
    c[:64, 576:640] = same * gt[:64, :64] / 16.0
    c[:64, 640:704] = same * le[:64, :64]
    oh = (i64[:, None] // TS == np.arange(NSS)[None, :]).astype(np.float32)
    c[:64, 704:708] = oh / 16.0
    c[:64, 708:712] = oh
    c[:, 712] = 1.0 / 16.0
    c[:, 713:969] = np.broadcast_to(oh.T.reshape(1, NSS * 64), (128, NSS * 64))
    return c


def _rope(pos):
    half = 8
    inv = (500000.0 ** (-np.arange(0, 16, 2, dtype=np.float32) / np.float32(16))).astype(np.float32)
    ang = pos.astype(np.float32)[:, None] * inv[None, :]
    return np.concatenate([np.cos(ang), np.sin(ang)], axis=1).astype(np.float32)


_NC_CACHE = {}


def _run(S, ncores, inputs):
    if S not in _NC_CACHE:
        _NC_CACHE[S] = build(S, os.environ.get("KSTOP"))
    nc = _NC_CACHE[S]
    f = lambda a: np.ascontiguousarray(np.asarray(a, dtype=np.float32))
    i = {k: np.asarray(v) for k, v in inputs.items()}
    rep = lambda v, n=128: np.ascontiguousarray(np.broadcast_to(np.asarray(v, np.float32).reshape(1, -1), (n, np.asarray(v).size)))
    qn, kn = i["q_norm"][0], i["k_norm"][0]
    gqk = np.concatenate([np.tile(qn, 8), np.tile(kn, 8)])
    lamv = np.concatenate([i["lambda_q1"][0], i["lambda_k1"][0], i["lambda_q2"][0], i["lambda_k2"][0]])
    shared = {
        "meta": f(i["meta_tokens"]), "w_in": f(i["w_in"][0]),
        "wa2b": f(np.concatenate([i["w_a2"][0], i["b_a"][0][None, :]], axis=0)),
        "gmix": f(i["norm_mix"][0].reshape(8, 128).T), "gffn": f(i["norm_ffn"][0].reshape(8, 128).T),
        "gqk": rep(gqk), "lamv": rep(lamv), "gd": rep(i["g_diff"][0]), "gg": rep(i["g_gla"][0]),
        "w_out": f(i["w_out"][0]), "wg": f(i["w_ffn_gate"][0]), "wu": f(i["w_ffn_up"][0]), "wd": f(i["w_ffn_down"][0]),
        "ropep": _rope(NMETA + np.arange(S)), "ropem": _rope(np.arange(NMETA)),
        "ropes": np.ascontiguousarray(np.tile(_rope(PAST + np.arange(TS)), (NSS, 1))),
        "cst": _consts(),
    }
    in_maps = []
    for c in range(ncores):
        m = dict(shared)
        m["x"] = f(i["x_prompt"][c, :S])
        sl = slice(NSS * c, NSS * (c + 1))
        m["xs"] = f(i["x_sample"][sl].reshape(NSS * TS, D))
        m["ck"] = f(i["cache_k_diff"][0, sl].reshape(NSS, PAST, 512))
        m["cv"] = f(i["cache_v_diff"][0, sl].reshape(NSS, PAST, 512))
        m["st"] = f(i["state_gla"][0, sl].reshape(NSS, 256, 128))
        in_maps.append(m)
    res = run_bass_kernel_spmd(nc, in_maps, core_ids=list(range(ncores)))
    R = res.results
    y = np.stack([R[c]["y"] for c in range(ncores)])
    ys = np.concatenate([R[c]["ys"].reshape(NSS, TS, D) for c in range(ncores)])
    nk = np.stack([R[c]["nk"].reshape(NMETA + S, 4, 128) for c in range(ncores)])[None]
    nv = np.stack([R[c]["nv"].reshape(NMETA + S, 4, 128) for c in range(ncores)])[None]
    ngla = np.stack([R[c]["ngla"].reshape(4, 64, 128) for c in range(ncores)])[None]
    nks = np.concatenate([R[c]["nks"].reshape(NSS, TS, 4, 128) for c in range(ncores)])[None]
    nvs = np.concatenate([R[c]["nvs"].reshape(NSS, TS, 4, 128) for c in range(ncores)])[None]
    nglas = np.concatenate([R[c]["nglas"].reshape(NSS, 4, 64, 128) for c in range(ncores)])[None]
    return tuple(np.ascontiguousarray(a.astype(np.float32)) for a in (y, ys, nk, nv, ngla, nks, nvs, nglas))


def kernel(**inputs):
    S = int(np.asarray(inputs["x_prompt"]).shape[1])
    return _run(S, 8, inputs)
```

```python
import contextlib
import math
import os
import numpy as np
import concourse.bass as bass
import concourse.mybir as mybir
from concourse.bass_utils import run_bass_kernel_spmd

F32 = mybir.dt.float32
BF16 = mybir.dt.bfloat16
AF = mybir.ActivationFunctionType
ALU = mybir.AluOpType
AX = mybir.AxisListType

D = 1024
NIN = 3088
DFF = 2816
NFC = DFF // 128
NMETA = 16
EPS = 1e-6
LI = 0.8 - 0.6 * math.exp(-0.3 * 0)
PAST = 2064
NSS = 4
TS = 16
NCONST = 969


class Buf:
    __slots__ = ("name", "lw", "rd")

    def __init__(self, name):
        self.name = name
        self.lw = None
        self.rd = []


class Prog:
    ENG = ("pe", "act", "dve", "pool", "sp")

    def __init__(self):
        self.ops = []
        self.last = {e: None for e in self.ENG}
        self.pend = {e: set() for e in self.ENG}
        self.lastdma = {}
        self.enabled = True
        self._rec = None
        self._atom = None

    def begin(self):
        self._rec = []
        self._atom = None

    def end(self):
        r = self._rec
        self._rec = None
        return r

    def atom_begin(self):
        if self._rec is not None:
            self._atom = []

    def atom_end(self):
        if self._rec is not None and self._atom is not None:
            self._rec.append(self._atom)
            self._atom = None

    def replay(self, recs):
        for grp in recs:
            for a in grp:
                self.op(*a)

    def interleave(self, *streams, bias=None):
        if bias is None:
            bias = [0.0] * len(streams)
        keep = [i for i, st in enumerate(streams) if st]
        bias = [bias[i] for i in keep]
        streams = [streams[i] for i in keep]
        tot = [sum(len(g) for g in st) for st in streams]
        pos = [0] * len(streams)
        done = [0] * len(streams)
        while True:
            best, bf = None, None
            for i, st in enumerate(streams):
                if pos[i] < len(st):
                    f = done[i] / tot[i] - bias[i]
                    if bf is None or f < bf:
                        best, bf = i, f
            if best is None:
                break
            grp = streams[best][pos[best]]
            for a in grp:
                self.op(*a)
            pos[best] += 1
            done[best] += len(grp)

    def op(self, eng, fn, r=(), w=(), dma=None):
        if not self.enabled:
            return None
        if self._rec is not None:
            a = (eng, fn, tuple(r), tuple(w), dma)
            if self._atom is not None:
                self._atom.append(a)
            else:
                self._rec.append([a])
            return None
        idx = len(self.ops)
        deps = set()
        for b in r:
            if b.lw is not None:
                deps.add(b.lw)
        for b in w:
            if b.lw is not None:
                deps.add(b.lw)
            deps.update(b.rd)
        for b in r:
            b.rd.append(idx)
        for b in w:
            b.lw = idx
            b.rd = []
        deps.discard(idx)
        if self.pend[eng]:
            deps.update(self.pend[eng])
            self.pend[eng] = set()
        self.ops.append(dict(eng=eng, fn=fn, deps=deps, dma=dma, need=False))
        if dma is None:
            self.last[eng] = idx
        else:
            self.lastdma[dma] = idx
        return idx

    def barrier(self):
        s = set(v for v in self.last.values() if v is not None)
        s.update(self.lastdma.values())
        for e in self.ENG:
            self.pend[e].update(s)

    def emit(self, nc):
        ops = self.ops
        for o in ops:
            for d in o["deps"]:
                ops[d]["need"] = True
        cnt = {e: 0 for e in self.ENG}
        dcount = {}
        for o in ops:
            if o["dma"] is not None:
                k = o["dma"]
                dcount[k] = dcount.get(k, 0) + 16
                o["sig"] = ("d", k, dcount[k])
            elif o["need"]:
                cnt[o["eng"]] += 1
                o["sig"] = ("e", o["eng"], cnt[o["eng"]])
            else:
                o["sig"] = None
        with contextlib.ExitStack() as st:
            esem = {e: st.enter_context(nc.semaphore("s_" + e)) for e in self.ENG}
            dsem = {k: st.enter_context(nc.semaphore("d_%s" % (k,))) for k in dcount}
            block = st.enter_context(nc.Block())

            def body(ename):
                def f(eng):
                    waited = {}
                    for o in ops:
                        if o["eng"] != ename:
                            continue
                        need = {}
                        for d in o["deps"]:
                            s = ops[d]["sig"]
                            if s is None:
                                continue
                            if s[0] == "e" and s[1] == "pe" and ename == "pe":
                                continue
                            key = (s[0], s[1])
                            need[key] = max(need.get(key, 0), s[2])
                        pend = []
                        for key, v in need.items():
                            if waited.get(key, 0) >= v:
                                continue
                            waited[key] = v
                            pend.append((esem[key[1]] if key[0] == "e" else dsem[key[1]], v))
                        attach = None
                        if pend:
                            attach = pend.pop()
                        for sem, v in pend:
                            eng.wait_ge(sem, v)
                        ins = o["fn"](eng)
                        if attach is not None:
                            ins._wait_ge(attach[0], attach[1])
                        s = o["sig"]
                        if s is not None:
                            if s[0] == "d":
                                ins.then_inc(dsem[s[1]], 16)
                            else:
                                ins.then_inc(esem[s[1]], 1)
                    if ename == "sp":
                        for k, v in dcount.items():
                            if waited.get(("d", k), 0) < v:
                                eng.wait_ge(dsem[k], v)
                return f

            block.tensor(body("pe"))
            block.scalar(body("act"))
            block.vector(body("dve"))
            block.gpsimd(body("pool"))
            block.sync(body("sp"))


def build(S, stop=None):
    NT = S // 128
    nc = bass.Bass("TRN2", target_bir_lowering=False)
    P = Prog()
    op = P.op

    def din(name, shape, dt=F32):
        return nc.dram_tensor(name, list(shape), dt, kind="ExternalInput").ap()

    def dout(name, shape):
        return nc.dram_tensor(name, list(shape), F32, kind="ExternalOutput").ap()

    def dscr(name, shape, dt):
        return nc.dram_tensor(name, list(shape), dt, kind="Internal").ap()

    x_d = din("x", [S, D]); xs_d = din("xs", [NSS * TS, D])
    ck_d = din("ck", [NSS, PAST, 512]); cv_d = din("cv", [NSS, PAST, 512])
    st_d = din("st", [NSS, 256, 128]); meta_d = din("meta", [NMETA, D])
    win_d = din("w_in", [D, NIN]); wa2b_d = din("wa2b", [17, 256])
    gmix_d = din("gmix", [128, 8]); gffn_d = din("gffn", [128, 8])
    gqk_d = din("gqk", [128, 1024]); lamv_d = din("lamv", [128, 256])
    gd_d = din("gd", [128, 512]); gg_d = din("gg", [128, 512])
    wout_d = din("w_out", [D, D]); wg_d = din("wg", [D, DFF]); wu_d = din("wu", [D, DFF])
    wd_d = din("wd", [DFF, D])
    ropep_d = din("ropep", [S, 16]); ropem_d = din("ropem", [16, 16]); ropes_d = din("ropes", [64, 16])
    cst_d = din("cst", [128, NCONST])

    y_d = dout("y", [S, D]); ys_d = dout("ys", [NSS * TS, D])
    nk_d = dout("nk", [NMETA + S, 512]); nv_d = dout("nv", [NMETA + S, 512])
    ngla_d = dout("ngla", [256, 128])
    nks_d = dout("nks", [NSS * TS, 512]); nvs_d = dout("nvs", [NSS * TS, 512])
    nglas_d = dout("nglas", [NSS, 256, 128])

    qt_scr = dscr("qt_scr", [128, 4, S], BF16)
    kt_scr = dscr("kt_scr", [128, 4, NMETA + S], BF16)
    v_scr = dscr("v_scr", [NMETA + S, 516], BF16)
    ym_scr = dscr("ym_scr", [S, D], BF16)

    OUTB = Buf("outs")

    with contextlib.ExitStack() as st0:
        def sbt(st, name, shape, dt):
            return st.enter_context(nc.sbuf_tensor("sb_" + name, list(shape), dt))

        def pst(st, name, shape, dt):
            return st.enter_context(nc.psum_tensor("ps_" + name, list(shape), dt))

        identb = sbt(st0, "identb", [128, 128], BF16); Bident = Buf("identb")
        gffn = sbt(st0, "gffn", [128, 8], F32); Bgffn = Buf("gffn")
        ymS = sbt(st0, "ymS", [64, 1024], BF16); BymS = Buf("ymS")
        stAB = st0.enter_context(contextlib.ExitStack())
        cst = sbt(stAB, "cst", [128, NCONST], F32); Bcst = Buf("cst")
        gd = sbt(stAB, "gd", [128, 512], F32); Bgd = Buf("gd")
        gmix = sbt(stAB, "gmix", [128, 8], F32); Bgmix = Buf("gmix")
        sm = sbt(stAB, "sm", [128, 16], F32); Bsm = Buf("sm")
        wa2f = sbt(stAB, "wa2f", [32, 256], F32); Bwa2f = Buf("wa2f")
        wa2b = sbt(stAB, "wa2b", [32, 256], BF16); Bwa2 = Buf("wa2b")
        alT = sbt(stAB, "alT", [32, 128], BF16); BalT = Buf("alT")
        QTs = sbt(stAB, "QTs", [128, 8, 64], BF16); BQTs = Buf("QTs")
        vnewS = sbt(stAB, "vnewS", [64, 512], F32); BvnewS = Buf("vnewS")
        stA = stAB.enter_context(contextlib.ExitStack())
        gqk = sbt(stA, "gqk", [128, 1024], F32); Bgqk = Buf("gqk")
        gg = sbt(stA, "gg", [128, 512], F32); Bgg = Buf("gg")
        lamv = sbt(stA, "lamv", [128, 256], F32); Blamv = Buf("lamv")
        neglam = sm[:, 0:1]; negB = sm[:, 1:2]
        ident_f = cst[:, 0:128]
        triI = cst[:, 128:256]; triS = cst[:, 256:384]; maskST = cst[:, 384:512]
        triIs = cst[:, 512:576]; triSs = cst[:, 576:640]; maskSTs = cst[:, 640:704]
        sel16s = cst[:, 704:708]; rowmask = cst[:, 708:712]; sel16p = cst[:, 712:713]
        cmask = cst[:, 713:969]

        op("sp", lambda e: e.dma_start(out=cst[:], in_=cst_d[:, :]), w=[Bcst], dma="c0")
        op("sp", lambda e: e.dma_start(out=gqk[:], in_=gqk_d[:, :]), w=[Bgqk], dma="c1")
        op("sp", lambda e: e.dma_start(out=gd[:], in_=gd_d[:, :]), w=[Bgd], dma="c2")
        op("sp", lambda e: e.dma_start(out=gg[:], in_=gg_d[:, :]), w=[Bgg], dma="c3")
        op("sp", lambda e: e.dma_start(out=gmix[:], in_=gmix_d[:, :]), w=[Bgmix], dma="c4")
        op("sp", lambda e: e.dma_start(out=gffn[:], in_=gffn_d[:, :]), w=[Bgffn], dma="c5")
        op("sp", lambda e: e.dma_start(out=lamv[:], in_=lamv_d[:, :]), w=[Blamv], dma="c6")
        op("sp", lambda e: e.dma_start(out=wa2f[0:17, :], in_=wa2b_d[:, :]), w=[Bwa2f], dma="c7")
        op("dve", lambda e: e.tensor_copy(out=identb[:], in_=ident_f), r=[Bcst], w=[Bident])
        op("dve", lambda e: e.tensor_copy(out=wa2b[0:17, :], in_=wa2f[0:17, :]), r=[Bwa2f], w=[Bwa2])
        op("pool", lambda e: e.memset(alT[:], 1.0), w=[BalT])
        tmpA = sbt(stA, "tmpA", [128, 128], F32); BtmpA = Buf("tmpA")
        op("dve", lambda e: e.tensor_tensor(out=tmpA[:, 0:64], in0=lamv[:, 0:64], in1=lamv[:, 64:128], op=ALU.mult), r=[Blamv], w=[BtmpA])
        op("dve", lambda e: e.tensor_tensor(out=tmpA[:, 64:128], in0=lamv[:, 128:192], in1=lamv[:, 192:256], op=ALU.mult), r=[Blamv, BtmpA], w=[BtmpA])
        op("dve", lambda e: e.reduce_sum(out=sm[:, 2:4], in_=tmpA[:].rearrange("p (a b) -> p a b", b=64), axis=AX.X), r=[BtmpA], w=[Bsm])
        op("act", lambda e: e.activation(out=sm[:, 4:6], in_=sm[:, 2:4], func=AF.Exp), r=[Bsm], w=[Bsm])
        op("dve", lambda e: e.tensor_tensor(out=sm[:, 0:1], in0=sm[:, 5:6], in1=sm[:, 4:5], op=ALU.subtract), r=[Bsm], w=[Bsm])
        op("dve", lambda e: e.tensor_scalar(out=sm[:, 0:1], in0=sm[:, 0:1], scalar1=-LI, scalar2=None, op0=ALU.add), r=[Bsm], w=[Bsm])
        op("dve", lambda e: e.tensor_tensor(out=tmpA[:, 0:64], in0=gqk[:, 0:64], in1=gqk[:, 0:64], op=ALU.mult), r=[Bgqk, BtmpA], w=[BtmpA])
        op("dve", lambda e: e.tensor_tensor(out=tmpA[:, 64:128], in0=gqk[:, 512:576], in1=gqk[:, 512:576], op=ALU.mult), r=[Bgqk, BtmpA], w=[BtmpA])
        op("dve", lambda e: e.tensor_reduce(out=sm[:, 6:8], in_=tmpA[:].rearrange("p (a b) -> p a b", b=64), axis=AX.X, op=ALU.max), r=[BtmpA], w=[Bsm])
        op("dve", lambda e: e.tensor_tensor(out=sm[:, 8:9], in0=sm[:, 6:7], in1=sm[:, 7:8], op=ALU.mult), r=[Bsm], w=[Bsm])
        op("act", lambda e: e.activation(out=sm[:, 9:10], in_=sm[:, 8:9], func=AF.Ln), r=[Bsm], w=[Bsm])
        op("act", lambda e: e.activation(out=sm[:, 10:11], in_=sm[:, 9:10], func=AF.Exp, scale=0.5), r=[Bsm], w=[Bsm])
        op("dve", lambda e: e.tensor_scalar(out=sm[:, 1:2], in0=sm[:, 10:11], scalar1=-8.0, scalar2=None, op0=ALU.mult), r=[Bsm], w=[Bsm])
        op("dve", lambda e: e.tensor_scalar(out=gd[:], in0=gd[:], scalar1=1.0 - LI, scalar2=None, op0=ALU.mult), r=[Bgd], w=[Bgd])

        if True:
            WI = sbt(stA, "WI", [128, 8, NIN], BF16); BWI = Buf("WI")
            with contextlib.ExitStack() as stg:
                wst = [sbt(stg, "wst%d" % i, [128, NIN], F32) for i in range(2)]
                Bwst = [Buf("wst0"), Buf("wst1")]
                for kc in range(8):
                    i = kc % 2
                    op("sp", lambda e, kc=kc, i=i: e.dma_start(out=wst[i][:], in_=win_d[kc * 128:(kc + 1) * 128, :]), w=[Bwst[i]], dma="wst%d" % i)
                    if i == 0:
                        op("dve", lambda e, kc=kc, i=i: e.tensor_scalar(out=WI[:, kc, :], in0=wst[i][:], scalar1=gmix[:, kc:kc + 1], scalar2=None, op0=ALU.mult), r=[Bwst[i], Bgmix], w=[BWI])
                    else:
                        op("act", lambda e, kc=kc, i=i: e.activation(out=WI[:, kc, :], in_=wst[i][:], func=AF.Copy, scale=gmix[:, kc:kc + 1]), r=[Bwst[i], Bgmix], w=[BWI])
                P.barrier()

            if stop == "SETUP":
                P.enabled = False
            xt = [sbt(stA, "xt%d" % i, [128, D], F32) for i in range(2)]; Bxt = [Buf("xt%d" % i) for i in range(2)]
            xn = sbt(stA, "xn", [128, D], BF16); Bxn = Buf("xn")
            xT = sbt(stA, "xT", [128, 8, 128], BF16); BxT = Buf("xT")
            pj = [sbt(stA, "pj%d" % i, [128, NIN], F32) for i in range(3)]; Bpj = [Buf("pj%d" % i) for i in range(3)]
            sq = sbt(stA, "sq", [128, D], F32); Bsq = Buf("sq")
            qkn = [sbt(stA, "qkn%d" % i, [128, D], F32) for i in range(3)]; Bqkn = [Buf("qkn%d" % i) for i in range(3)]
            qkb = sbt(stA, "qkb", [128, D], BF16); Bqkb = Buf("qkb")
            rtmp = sbt(stA, "rtmp", [128, 4, 16, 8], F32); Brtmp = Buf("rtmp")
            st16 = sbt(stA, "st16", [128, 48], F32); Bst16 = Buf("st16")
            stq = sbt(stA, "stq", [128, 48], F32); Bstq = Buf("stq")
            sto = sbt(stA, "sto", [128, 48], F32); Bsto = Buf("sto")
            qkT = [sbt(stA, "qkT%d" % i, [128, 8, 512], BF16) for i in range(2)]; BqkT = [Buf("qkT0"), Buf("qkT1")]
            v16 = [sbt(stA, "v16_%d" % i, [128, 4, 129], BF16) for i in range(3)]; Bv16 = [Buf("v16_%d" % i) for i in range(3)]
            ropeP = sbt(stA, "ropeP", [128, NT, 16], F32); BropeP = Buf("ropeP")
            ropeM = sbt(stA, "ropeM", [16, 16], F32); BropeM = Buf("ropeM")
            ropeS = sbt(stA, "ropeS", [64, 16], F32); BropeS = Buf("ropeS")
            al16 = sbt(stA, "al16", [128, 16], BF16); Bal16 = Buf("al16")
            gz = sbt(stA, "gz", [128, 6, 256], F32); Bgz = [Buf("gz%d" % i) for i in range(6)]
            gb16 = sbt(stA, "gb16", [128, 3, 256], BF16); Bgb16 = [Buf("gb16_%d" % i) for i in range(3)]
            qfm = sbt(stA, "qfm", [128, 4, NSS, 64], BF16); Bqfm = Buf("qfm")
            qTm = sbt(stA, "qTm", [128, 4, 128], BF16); BqTm = Buf("qTm")
            kTm = sbt(stA, "kTm", [128, 4, 128], BF16); BkTm = Buf("kTm")
            khm = sbt(stA, "khm", [64, NSS, 256], BF16); Bkhm = Buf("khm")
            AT = sbt(stA, "AT", [128, 4, 128], BF16); BAT = Buf("AT")
            dec = sbt(stA, "dec", [128, 8], F32); Bdec = Buf("dec")
            Sst = sbt(stA, "Sst", [128, 2, 128], F32); BSst = Buf("Sst")
            S16 = sbt(stA, "S16", [128, 2, 128], BF16); BS16 = Buf("S16")
            SstS = sbt(stA, "SstS", [128, NSS, 2, 128], F32); BSstS = Buf("SstS")
            S16S = sbt(stA, "S16S", [128, NSS, 2, 128], BF16); BS16S = Buf("S16S")
            on = sbt(stA, "on", [128, 4, 512], F32); Bon = [Buf("on%d" % i) for i in range(4)]
            yb16 = [sbt(stA, "yb16_%d" % i, [128, 512], BF16) for i in range(3)]; Byb16 = [Buf("yb16_%d" % i) for i in range(3)]
            pA = pst(stA, "pA", [128, 8, 128], BF16); BpA = Buf("pA")
            pP = [pst(stA, "pP%d" % i, [128, 512], F32) for i in range(2)]; BpP = [Buf("pP0"), Buf("pP1")]
            pU2 = pst(stA, "pU2", [128, 512], F32); BpU2 = Buf("pU2")
            pBC = pst(stA, "pBC", [128, 512], F32); BpBC = Buf("pBC")
            pT2 = pst(stA, "pT2", [128, 1024], BF16); BpT2 = Buf("pT2")
            pSC = pst(stA, "pSC", [128, 512], F32); BpSC = Buf("pSC")
            pO = pst(stA, "pO", [128, 512], F32); BpO = Buf("pO")

            op("sp", lambda e: e.dma_start(out=ropeP[:], in_=ropep_d.rearrange("(t p) c -> p t c", p=128)), w=[BropeP], dma="c8")
            op("sp", lambda e: e.dma_start(out=ropeM[:], in_=ropem_d[:, :]), w=[BropeM], dma="c9")
            op("sp", lambda e: e.dma_start(out=ropeS[:], in_=ropes_d[:, :]), w=[BropeS], dma="c10")
            for i in range(3):
                op("pool", lambda e, i=i: e.memset(v16[i][:], 1.0), w=[Bv16[i]])
            op("pool", lambda e: e.memset(qTm[:], 0.0), w=[BqTm])

            op("pool", lambda e: e.memset(kTm[:], 0.0), w=[BkTm])

            cols = [(0, 512), (512, 512), (1024, 512), (1536, 512), (2048, 512), (2560, 512), (3072, 16)]
            state = dict(tile=0, ppar=0)

            def load_x(src, n, par):
                op("sp", lambda e: e.dma_start(out=xt[par][:n, :], in_=src), w=[Bxt[par]], dma="x%d" % par)

            def proj_tile(n, par, rope_ap, Brope):
                proj_tile_a(n, par, par)
                proj_tile_b(n, par, rope_ap, Brope)

            def proj_tile_a(n, par, xpar):
                x_ = xt[xpar]; pj_ = pj[par]; q_ = qkn[par]
                op("act", lambda e: e.activation(out=xn[:n, :], in_=x_[:n, :], func=AF.Square, scale=1.0 / 32.0, accum_out=st16[:n, 0:1]), r=[Bxt[xpar]], w=[Bxn, Bst16])
                op("act", lambda e: e.activation(out=st16[:n, 1:2], in_=st16[:n, 0:1], func=AF.Ln, bias=EPS), r=[Bst16], w=[Bst16])
                op("act", lambda e: e.activation(out=st16[:n, 2:3], in_=st16[:n, 1:2], func=AF.Exp, scale=-0.5), r=[Bst16], w=[Bst16])
                op("act", lambda e: e.activation(out=xn[:n, :], in_=x_[:n, :], func=AF.Copy, scale=st16[:n, 2:3]), r=[Bxt[xpar], Bst16, Bxn], w=[Bxn])
                P.atom_begin()
                for kc in range(8):
                    op("pe", lambda e, kc=kc: e.transpose(out=pA[:, kc, :n], in_=xn[:n, kc * 128:(kc + 1) * 128], identity=identb[:n, :n]), r=[Bxn, Bident], w=[BpA])
                op("act", lambda e: e.activation(out=xT[:, :, :n], in_=pA[:, :, :n], func=AF.Copy), r=[BpA], w=[BxT])
                P.atom_end()
                for ci, (c0, cw) in enumerate(cols):
                    pp = state["ppar"]; state["ppar"] ^= 1
                    for kc in range(8):
                        op("pe", lambda e, kc=kc, c0=c0, cw=cw, pp=pp: e.matmul(pP[pp][:n, :cw], lhsT=xT[:, kc, :n], rhs=WI[:, kc, c0:c0 + cw], start=(kc == 0), stop=(kc == 7)), r=[BxT, BWI], w=[BpP[pp]])
                    if ci % 2 == 0:
                        op("dve", lambda e, c0=c0, cw=cw, pp=pp: e.tensor_copy(out=pj_[:n, c0:c0 + cw], in_=pP[pp][:n, :cw]), r=[BpP[pp]], w=[Bpj[par]])
                    else:
                        op("act", lambda e, c0=c0, cw=cw, pp=pp: e.activation(out=pj_[:n, c0:c0 + cw], in_=pP[pp][:n, :cw], func=AF.Copy), r=[BpP[pp]], w=[Bpj[par]])

            def proj_tile_b(n, par, rope_ap, Brope):
                pj_ = pj[par]; q_ = qkn[par]
                op("act", lambda e: e.activation(out=sq[:n, :], in_=pj_[:n, 0:1024], func=AF.Square, scale=0.125), r=[Bpj[par]], w=[Bsq])
                op("dve", lambda e: e.reduce_sum(out=stq[:n, 16:32], in_=sq[:n, :].rearrange("p (g d) -> p g d", d=64), axis=AX.X), r=[Bsq], w=[Bstq])
                op("act", lambda e: e.activation(out=stq[:n, 32:48], in_=stq[:n, 16:32], func=AF.Ln, bias=EPS), r=[Bstq], w=[Bstq])
                op("act", lambda e: e.activation(out=stq[:n, 16:32], in_=stq[:n, 32:48], func=AF.Exp, scale=-0.5), r=[Bstq], w=[Bstq])
                op("dve", lambda e: e.tensor_tensor(out=sq[:n, :].rearrange("p (g d) -> p g d", d=64), in0=pj_[:n, 0:1024].rearrange("p (g d) -> p g d", d=64), in1=stq[:n, 16:32].unsqueeze(2).broadcast_to([n, 16, 64]), op=ALU.mult), r=[Bpj[par], Bstq, Bsq], w=[Bsq])
                op("dve", lambda e: e.tensor_tensor(out=q_[:n, :], in0=sq[:n, :], in1=gqk[:n, :], op=ALU.mult), r=[Bsq, Bgqk], w=[Bqkn[par]])
                qv = q_[:n, :].rearrange("p (g d) -> p g d", d=64)
                x1 = qv[:, :, 0:8]; x2 = qv[:, :, 8:16]
                cosb = rope_ap[:, 0:8].unsqueeze(1).broadcast_to([n, 16, 8])
                sinb = rope_ap[:, 8:16].unsqueeze(1).broadcast_to([n, 16, 8])
                op("dve", lambda e: e.tensor_tensor(out=rtmp[:n, 0], in0=x1, in1=cosb, op=ALU.mult), r=[Bqkn[par], Brope], w=[Brtmp])
                op("dve", lambda e: e.tensor_tensor(out=rtmp[:n, 1], in0=x2, in1=sinb, op=ALU.mult), r=[Bqkn[par], Brope, Brtmp], w=[Brtmp])
                op("dve", lambda e: e.tensor_tensor(out=rtmp[:n, 2], in0=x2, in1=cosb, op=ALU.mult), r=[Bqkn[par], Brope, Brtmp], w=[Brtmp])
                op("dve", lambda e: e.tensor_tensor(out=rtmp[:n, 3], in0=x1, in1=sinb, op=ALU.mult), r=[Bqkn[par], Brope, Brtmp], w=[Brtmp])
                op("dve", lambda e: e.tensor_tensor(out=x1, in0=rtmp[:n, 0], in1=rtmp[:n, 1], op=ALU.subtract), r=[Brtmp, Bqkn[par]], w=[Bqkn[par]])
                op("dve", lambda e: e.tensor_tensor(out=x2, in0=rtmp[:n, 2], in1=rtmp[:n, 3], op=ALU.add), r=[Brtmp, Bqkn[par]], w=[Bqkn[par]])

            def qk_transposes(n, par, dst, Bdst, coff, which):
                op("act", lambda e: e.activation(out=qkb[:n, :], in_=qkn[par][:n, :], func=AF.Copy), r=[Bqkn[par]], w=[Bqkb])
                js = [j for j in range(8) if (j < 4 and "q" in which) or (j >= 4 and "k" in which)]
                P.atom_begin()
                for j in js:
                    op("pe", lambda e, j=j: e.transpose(out=pA[:, j, :n], in_=qkb[:n, j * 128:(j + 1) * 128], identity=identb[:n, :n]), r=[Bqkb, Bident], w=[BpA])
                j0, j1 = js[0], js[-1] + 1
                op("act", lambda e: e.activation(out=dst[:, j0:j1, coff:coff + n], in_=pA[:, j0:j1, :n], func=AF.Copy), r=[BpA], w=[Bdst])
                P.atom_end()

            def gla(n, par, nstr, tri_i, tri_s, mask_st, sel, Sf, BSf, Sb, BSb, want_out, ybdst, Bybdst):
                vv = gv16[par]; Bvv = Bgv16[par]
                key = (n, nstr)
                if state.get("trikey") != key:
                    state["trikey"] = key
                    op("dve", lambda e: e.tensor_copy(out=trib[:n, 0, :n], in_=tri_i), r=[Bcst, Btrib], w=[Btrib])
                    op("dve", lambda e: e.tensor_copy(out=trib[:n, 1, :n], in_=tri_s), r=[Bcst, Btrib], w=[Btrib])
                    op("dve", lambda e: e.tensor_copy(out=trib[:n, 2, 0:nstr], in_=sel), r=[Bcst, Btrib], w=[Btrib])
                pj_ = pj[par]
                qb = pj_[:n, 1536:1792]; kb = pj_[:n, 1792:2048]; gbv = pj_[:n, 2560:3072]
                z_az, z_e, z_l, z_la, z_eb, z_x = [gz[:n, i, :] for i in range(6)]
                if want_out:
                    o_sq, o_t1, o_e, o_t3 = [on[:n, i, :] for i in range(4)]
                    op("act", lambda e: e.activation(out=o_e, in_=gbv, func=AF.Exp, scale=-1.0), r=[Bpj[par]], w=[Bon[2]])
                    op("act", lambda e: e.activation(out=o_t3, in_=o_e, func=AF.Ln, bias=1.0), r=[Bon[2]], w=[Bon[3]])
                    op("act", lambda e: e.activation(out=o_e, in_=o_t3, func=AF.Exp, scale=-1.0), r=[Bon[3], Bon[2]], w=[Bon[2]])
                    op("dve", lambda e: e.tensor_tensor(out=o_t3, in0=o_e, in1=gbv, op=ALU.mult), r=[Bon[2], Bon[3], Bpj[par]], w=[Bon[3]])
                op("dve", lambda e: e.tensor_copy(out=al16[:n, :], in_=pj_[:n, 3072:3088]), r=[Bpj[par]], w=[Bal16])
                op("pe", lambda e: e.transpose(out=pT2[0:16, 512:512 + n], in_=al16[:n, :], identity=identb[:n, :n]), r=[Bal16, Bident], w=[BpT2])
                op("dve", lambda e: e.tensor_copy(out=alT[0:16, :n], in_=pT2[0:16, 512:512 + n]), r=[BpT2], w=[BalT])
                op("pe", lambda e: e.matmul(pSC[:n, 0:256], lhsT=alT[0:17, :n], rhs=wa2b[0:17, :], start=True, stop=True), r=[BalT, Bwa2], w=[BpSC])
                op("act", lambda e: e.activation(out=z_az, in_=pSC[:n, 0:256], func=AF.Abs), r=[BpSC], w=[Bgz[0]])
                op("act", lambda e: e.activation(out=z_e, in_=z_az, func=AF.Exp, scale=-1.0), r=[Bgz[0]], w=[Bgz[1]])
                op("act", lambda e: e.activation(out=z_l, in_=z_e, func=AF.Ln, bias=1.0), r=[Bgz[1]], w=[Bgz[2]])
                op("dve", lambda e: e.tensor_scalar(out=z_az, in0=pSC[:n, 0:256], scalar1=0.0, scalar2=None, op0=ALU.min), r=[BpSC, Bgz[0]], w=[Bgz[0]])
                op("dve", lambda e: e.tensor_tensor(out=z_la, in0=z_az, in1=z_l, op=ALU.subtract), r=[Bgz[0], Bgz[2]], w=[Bgz[3]])
                if stop == "G1":
                    P.enabled = False
                la_hi = lahl[:n, 0, :]; la_lo = lahl[:n, 1, :]
                op("act", lambda e: e.activation(out=la_hi, in_=z_la, func=AF.Copy), r=[Bgz[3]], w=[Blahl])
                op("dve", lambda e: e.tensor_tensor(out=z_x, in0=z_la, in1=la_hi, op=ALU.subtract), r=[Bgz[3], Blahl], w=[Bgz[5]])
                op("act", lambda e: e.activation(out=la_lo, in_=z_x, func=AF.Copy), r=[Bgz[5], Blahl], w=[Blahl])
                tI = trib[:n, 0, :n]; tS = trib[:n, 1, :n]
                for hl in range(2):
                    op("pe", lambda e, hl=hl: e.matmul(pBC[:n, 0:256], lhsT=tI, rhs=lahl[:n, hl, :], start=(hl == 0), stop=(hl == 1)), r=[Blahl, Btrib], w=[BpBC])
                for hl in range(2):
                    op("pe", lambda e, hl=hl: e.matmul(pBC[:n, 256:512], lhsT=tS, rhs=lahl[:n, hl, :], start=(hl == 0), stop=(hl == 1)), r=[Blahl, Btrib], w=[BpBC])
                for j in range(2):
                    for hl in range(2):
                        op("pe", lambda e, j=j, hl=hl: e.matmul(pSC[:, 256 + j * nstr:256 + (j + 1) * nstr], lhsT=lahl[:n, hl, j * 128:(j + 1) * 128], rhs=trib[:n, 2, 0:nstr], start=(hl == 0), stop=(hl == 1)), r=[Blahl, Btrib], w=[BpSC])
                op("act", lambda e: e.activation(out=z_eb, in_=pBC[:n, 0:256], func=AF.Exp), r=[BpBC], w=[Bgz[4]])
                op("act", lambda e: e.activation(out=z_e, in_=pBC[:n, 0:256], func=AF.Exp, scale=-1.0), r=[BpBC, Bgz[1]], w=[Bgz[1]])
                op("act", lambda e: e.activation(out=z_l, in_=pBC[:n, 256:512], func=AF.Exp), r=[BpBC, Bgz[2]], w=[Bgz[2]])
                op("act", lambda e: e.activation(out=dec[:, 0:2 * nstr], in_=pSC[:, 256:256 + 2 * nstr], func=AF.Exp), r=[BpSC], w=[Bdec])
                if stop == "G2":
                    P.enabled = False
                qf = gb16[:n, 0, :]; kt = gb16[:n, 1, :]; kh = gb16[:n, 2, :]
                op("dve", lambda e: e.scalar_tensor_tensor(out=qf, in0=qb, scalar=0.125, in1=z_eb, op0=ALU.mult, op1=ALU.mult), r=[Bpj[par], Bgz[4]], w=[Bgb16[0]])
                op("dve", lambda e: e.tensor_tensor(out=kt, in0=kb, in1=z_e, op=ALU.mult), r=[Bpj[par], Bgz[1]], w=[Bgb16[1]])
                op("dve", lambda e: e.tensor_tensor(out=kh, in0=kb, in1=z_l, op=ALU.mult), r=[Bpj[par], Bgz[2]], w=[Bgb16[2]])
                if want_out:
                    for j in range(2):
                        op("pe", lambda e, j=j: e.transpose(out=pT2[:, j * 128:j * 128 + n], in_=qf[:, j * 128:(j + 1) * 128], identity=identb[:n, :n]), r=[Bgb16[0], Bident], w=[BpT2])
                        op("pe", lambda e, j=j: e.transpose(out=pT2[:, (2 + j) * 128:(2 + j) * 128 + n], in_=kt[:, j * 128:(j + 1) * 128], identity=identb[:n, :n]), r=[Bgb16[1], Bident], w=[BpT2])
                    pv = pT2[:, 0:512].rearrange("p (a t) -> p a t", t=128)
                    for i in range(2):
                        sl = slice(i * 64, (i + 1) * 64)
                        op("dve", lambda e, i=i, sl=sl: e.tensor_copy(out=qTm[sl, i::2, :n], in_=pv[sl, 0:2, :n]), r=[BpT2, BqTm], w=[BqTm])
                        op("dve", lambda e, i=i, sl=sl: e.tensor_copy(out=kTm[sl, i::2, :n], in_=pv[sl, 2:4, :n]), r=[BpT2, BkTm], w=[BkTm])
                    for h in range(4):
                        op("pe", lambda e, h=h: e.matmul(pSC[:n, h * 128:h * 128 + n], lhsT=kTm[:, h, :n], rhs=qTm[:, h, :n], start=True, stop=True), r=[BkTm, BqTm], w=[BpSC])
                    op("dve", lambda e: e.tensor_tensor(out=AT[:n, :, :n], in0=pSC[:n, :].rearrange("p (h t) -> p h t", t=128)[:, :, :n], in1=mask_st.unsqueeze(1).broadcast_to([n, 4, n]), op=ALU.mult), r=[BpSC, Bcst], w=[BAT])
                    if nstr > 1:
                        for h in range(4):
                            op("dve", lambda e, h=h: e.tensor_tensor(out=qfm[:, h, :, :n], in0=qTm[:, h, :n].unsqueeze(1).broadcast_to([128, nstr, n]), in1=cmask.rearrange("p (s t) -> p s t", t=64), op=ALU.mult), r=[BqTm, Bcst, Bqfm], w=[Bqfm])
                    for h in range(4):
                        j = h // 2
                        op("pe", lambda e, h=h: e.matmul(pO[:n, h * 128:(h + 1) * 128], lhsT=AT[:n, h, :n], rhs=vv[:n, h, 0:128], start=True, stop=False), r=[BAT, Bvv], w=[BpO])
                        if nstr == 1:
                            op("pe", lambda e, h=h, j=j: e.matmul(pO[:n, h * 128:(h + 1) * 128], lhsT=qTm[:, h, :n], rhs=Sb[:, j, :], start=False, stop=True), r=[BqTm, BSb], w=[BpO])
                        else:
                            for s in range(nstr):
                                op("pe", lambda e, h=h, j=j, s=s: e.matmul(pO[:n, h * 128:(h + 1) * 128], lhsT=qfm[:, h, s, :n], rhs=Sb[:, s, j, :], start=False, stop=(s == nstr - 1)), r=[Bqfm, BSb], w=[BpO])
                if stop == "G4":
                    P.enabled = False
                if nstr == 1:
                    for j in range(2):
                        op("pe", lambda e, j=j: e.matmul(pSC[:, j * 256:(j + 1) * 256] if not want_out else pBC[:, j * 256:(j + 1) * 256], lhsT=kh[:, j * 128:(j + 1) * 128], rhs=vv[:n, 2 * j:2 * j + 2, 0:128], start=True, stop=True), r=[Bgb16[2], Bvv], w=[BpSC if not want_out else BpBC])
                    pU = pSC if not want_out else pBC
                    BpU = BpSC if not want_out else BpBC
                    for j in range(2):
                        for i in range(2):
                            sl = slice(i * 64, (i + 1) * 64)
                            op("dve", lambda e, j=j, i=i, sl=sl: e.scalar_tensor_tensor(out=Sf[sl, j, :], in0=Sf[sl, j, :], scalar=dec[sl, j:j + 1], in1=pU[sl, j * 256 + i * 128:j * 256 + (i + 1) * 128], op0=ALU.mult, op1=ALU.add), r=[BSf, Bdec, BpU], w=[BSf])
                    op("act", lambda e: e.activation(out=Sb[:], in_=Sf[:], func=AF.Copy), r=[BSf], w=[BSb])
                else:
                    for s in range(nstr):
                        op("dve", lambda e, s=s: e.tensor_scalar(out=khm[:n, s, :], in0=kh, scalar1=rowmask[:n, s:s + 1], scalar2=None, op0=ALU.mult), r=[Bgb16[2], Bcst, Bkhm], w=[Bkhm])
                    for s in range(nstr):
                        for j in range(2):
                            op("pe", lambda e, s=s, j=j: e.matmul(pBC[:, j * 256:(j + 1) * 256], lhsT=khm[:n, s, j * 128:(j + 1) * 128], rhs=vv[:n, 2 * j:2 * j + 2, 0:128], start=True, stop=True), r=[Bkhm, Bvv], w=[BpBC])
                        for j in range(2):
                            for i in range(2):
                                sl = slice(i * 64, (i + 1) * 64)
                                op("dve", lambda e, s=s, j=j, i=i, sl=sl: e.scalar_tensor_tensor(out=Sf[sl, s, j, :], in0=Sf[sl, s, j, :], scalar=dec[sl, j * nstr + s:j * nstr + s + 1], in1=pBC[sl, j * 256 + i * 128:j * 256 + (i + 1) * 128], op0=ALU.mult, op1=ALU.add), r=[BSf, Bdec, BpBC], w=[BSf])
                if want_out:
                    if stop == "G5":
                        P.enabled = False
                    o_sq, o_t1, o_e, o_t3 = [on[:n, i, :] for i in range(4)]
                    op("act", lambda e: e.activation(out=o_sq, in_=pO[:n, :], func=AF.Square, scale=1.0 / math.sqrt(128.0)), r=[BpO], w=[Bon[0]])
                    op("dve", lambda e: e.reduce_sum(out=sto[:n, 4:8], in_=o_sq.rearrange("p (h d) -> p h d", d=128), axis=AX.X), r=[Bon[0]], w=[Bsto])
                    op("act", lambda e: e.activation(out=sto[:n, 8:12], in_=sto[:n, 4:8], func=AF.Ln, bias=EPS), r=[Bsto], w=[Bsto])
                    op("act", lambda e: e.activation(out=sto[:n, 4:8], in_=sto[:n, 8:12], func=AF.Exp, scale=-0.5), r=[Bsto], w=[Bsto])
                    op("dve", lambda e: e.tensor_tensor(out=o_t1.rearrange("p (h d) -> p h d", d=128), in0=pO[:n, :].rearrange("p (h d) -> p h d", d=128), in1=sto[:n, 4:8].unsqueeze(2).broadcast_to([n, 4, 128]), op=ALU.mult), r=[BpO, Bsto], w=[Bon[1]])
                    op("dve", lambda e: e.tensor_tensor(out=o_t1, in0=o_t1, in1=gg[:n, :], op=ALU.mult), r=[Bon[1], Bgg], w=[Bon[1]])
                    op("dve", lambda e: e.tensor_tensor(out=ybdst, in0=o_t1, in1=o_t3, op=ALU.mult), r=[Bon[1], Bon[3]], w=[Bybdst])

            def gla_a(t):
                par = t % 3; p = t % 2; n = 128
                pj_ = pj[par]
                qb = pj_[:n, 1536:1792]; kb = pj_[:n, 1792:2048]; gbv = pj_[:n, 2560:3072]
                z_az, z_e, z_l, z_la, z_eb, z_x = [gz[:n, i, :] for i in range(6)]
                o_e = on[:n, 2, :]; o_tmp = on[:n, 3, :]
                op("act", lambda e: e.activation(out=o_e, in_=gbv, func=AF.Exp, scale=-1.0), r=[Bpj[par]], w=[Bon[2]])
                op("act", lambda e: e.activation(out=o_tmp, in_=o_e, func=AF.Ln, bias=1.0), r=[Bon[2]], w=[Bon[3]])
                op("act", lambda e: e.activation(out=o_e, in_=o_tmp, func=AF.Exp, scale=-1.0), r=[Bon[3], Bon[2]], w=[Bon[2]])
                op("dve", lambda e: e.tensor_tensor(out=t3_2[p][:, :], in0=o_e, in1=gbv, op=ALU.mult), r=[Bon[2], Bpj[par]], w=[Bt3_2[p]])
                gla_v(n, par)
                op("dve", lambda e: e.tensor_copy(out=al16[:n, :], in_=pj_[:n, 3072:3088]), r=[Bpj[par]], w=[Bal16])
                op("pe", lambda e: e.transpose(out=pT2[0:16, 512:512 + n], in_=al16[:n, :], identity=identb[:n, :n]), r=[Bal16, Bident], w=[BpT2])
                op("dve", lambda e: e.tensor_copy(out=alT[0:16, :n], in_=pT2[0:16, 512:512 + n]), r=[BpT2], w=[BalT])
                op("pe", lambda e: e.matmul(pSC[:n, 0:256], lhsT=alT[0:17, :n], rhs=wa2b[0:17, :], start=True, stop=True), r=[BalT, Bwa2], w=[BpSC])
                op("act", lambda e: e.activation(out=z_az, in_=pSC[:n, 0:256], func=AF.Abs), r=[BpSC], w=[Bgz[0]])
                op("act", lambda e: e.activation(out=z_e, in_=z_az, func=AF.Exp, scale=-1.0), r=[Bgz[0]], w=[Bgz[1]])
                op("act", lambda e: e.activation(out=z_l, in_=z_e, func=AF.Ln, bias=1.0), r=[Bgz[1]], w=[Bgz[2]])
                op("dve", lambda e: e.tensor_scalar(out=z_az, in0=pSC[:n, 0:256], scalar1=0.0, scalar2=None, op0=ALU.min), r=[BpSC, Bgz[0]], w=[Bgz[0]])
                op("dve", lambda e: e.tensor_tensor(out=z_la, in0=z_az, in1=z_l, op=ALU.subtract), r=[Bgz[0], Bgz[2]], w=[Bgz[3]])
                la_hi = lahl[:n, 0, :]; la_lo = lahl[:n, 1, :]
                op("act", lambda e: e.activation(out=la_hi, in_=z_la, func=AF.Copy), r=[Bgz[3]], w=[Blahl])
                op("dve", lambda e: e.tensor_tensor(out=z_x, in0=z_la, in1=la_hi, op=ALU.subtract), r=[Bgz[3], Blahl], w=[Bgz[5]])
                op("act", lambda e: e.activation(out=la_lo, in_=z_x, func=AF.Copy), r=[Bgz[5], Blahl], w=[Blahl])
                tI = trib[:n, 0, :n]; tS = trib[:n, 1, :n]
                for hl in range(2):
                    op("pe", lambda e, hl=hl: e.matmul(pBC[:n, 0:256], lhsT=tI, rhs=lahl[:n, hl, :], start=(hl == 0), stop=(hl == 1)), r=[Blahl, Btrib], w=[BpBC])
                for hl in range(2):
                    op("pe", lambda e, hl=hl: e.matmul(pBC[:n, 256:512], lhsT=tS, rhs=lahl[:n, hl, :], start=(hl == 0), stop=(hl == 1)), r=[Blahl, Btrib], w=[BpBC])
                for j in range(2):
                    for hl in range(2):
                        op("pe", lambda e, j=j, hl=hl: e.matmul(pSC[:, 256 + j:257 + j], lhsT=lahl[:n, hl, j * 128:(j + 1) * 128], rhs=trib[:n, 2, 0:1], start=(hl == 0), stop=(hl == 1)), r=[Blahl, Btrib], w=[BpSC])
                op("act", lambda e: e.activation(out=z_eb, in_=pBC[:n, 0:256], func=AF.Exp), r=[BpBC], w=[Bgz[4]])
                op("act", lambda e: e.activation(out=z_e, in_=pBC[:n, 0:256], func=AF.Exp, scale=-1.0), r=[BpBC, Bgz[1]], w=[Bgz[1]])
                op("act", lambda e: e.activation(out=z_l, in_=pBC[:n, 256:512], func=AF.Exp), r=[BpBC, Bgz[2]], w=[Bgz[2]])
                op("act", lambda e: e.activation(out=dec2[p][:, 0:2], in_=pSC[:, 256:258], func=AF.Exp), r=[BpSC], w=[Bdec2[p]])
                qf = gb16[:n, 0, :]; kt = gb16[:n, 1, :]
                op("dve", lambda e: e.scalar_tensor_tensor(out=qf, in0=qb, scalar=0.125, in1=z_eb, op0=ALU.mult, op1=ALU.mult), r=[Bpj[par], Bgz[4]], w=[Bgb16[0]])
                op("dve", lambda e: e.tensor_tensor(out=kt, in0=kb, in1=z_e, op=ALU.mult), r=[Bpj[par], Bgz[1]], w=[Bgb16[1]])
                op("dve", lambda e: e.tensor_tensor(out=kh2[p][:, :], in0=kb, in1=z_l, op=ALU.mult), r=[Bpj[par], Bgz[2]], w=[Bkh2[p]])
                for j in range(2):
                    op("pe", lambda e, j=j: e.transpose(out=pT2[:, j * 128:j * 128 + n], in_=qf[:, j * 128:(j + 1) * 128], identity=identb[:n, :n]), r=[Bgb16[0], Bident], w=[BpT2])
                    op("pe", lambda e, j=j: e.transpose(out=pT2[:, (2 + j) * 128:(2 + j) * 128 + n], in_=kt[:, j * 128:(j + 1) * 128], identity=identb[:n, :n]), r=[Bgb16[1], Bident], w=[BpT2])
                pv = pT2[:, 0:512].rearrange("p (a t) -> p a t", t=128)
                for i in range(2):
                    sl = slice(i * 64, (i + 1) * 64)
                    op("dve", lambda e, i=i, sl=sl: e.tensor_copy(out=qTm2[p][sl, i::2, :n], in_=pv[sl, 0:2, :n]), r=[BpT2, BqTm2[p]], w=[BqTm2[p]])
                    op("dve", lambda e, i=i, sl=sl: e.tensor_copy(out=kTm[sl, i::2, :n], in_=pv[sl, 2:4, :n]), r=[BpT2, BkTm], w=[BkTm])
                for h in range(4):
                    op("pe", lambda e, h=h: e.matmul(pSC[:n, h * 128:h * 128 + n], lhsT=kTm[:, h, :n], rhs=qTm2[p][:, h, :n], start=True, stop=True), r=[BkTm, BqTm2[p]], w=[BpSC])
                op("dve", lambda e: e.tensor_tensor(out=AT2[p][:n, :, :n], in0=pSC[:n, :].rearrange("p (h t) -> p h t", t=128)[:, :, :n], in1=maskST.unsqueeze(1).broadcast_to([n, 4, n]), op=ALU.mult), r=[BpSC, Bcst], w=[BAT2[p]])

            def gla_b(t):
                par = t % 3; p = t % 2; n = 128
                vv = gv16[par]; Bvv = Bgv16[par]
                for h in range(4):
                    j = h // 2
                    op("pe", lambda e, h=h: e.matmul(pO[:n, h * 128:(h + 1) * 128], lhsT=AT2[p][:n, h, :n], rhs=vv[:n, h, 0:128], start=True, stop=False), r=[BAT2[p], Bvv], w=[BpO])
                    op("pe", lambda e, h=h, j=j: e.matmul(pO[:n, h * 128:(h + 1) * 128], lhsT=qTm2[p][:, h, :n], rhs=S16[:, j, :], start=False, stop=True), r=[BqTm2[p], BS16], w=[BpO])
                for j in range(2):
                    op("pe", lambda e, j=j: e.matmul(pU2[:, j * 256:(j + 1) * 256], lhsT=kh2[p][:, j * 128:(j + 1) * 128], rhs=vv[:n, 2 * j:2 * j + 2, 0:128], start=True, stop=True), r=[Bkh2[p], Bvv], w=[BpU2])
                for j in range(2):
                    for i in range(2):
                        sl = slice(i * 64, (i + 1) * 64)
                        op("dve", lambda e, j=j, i=i, sl=sl: e.scalar_tensor_tensor(out=Sst[sl, j, :], in0=Sst[sl, j, :], scalar=dec2[p][sl, j:j + 1], in1=pU2[sl, j * 256 + i * 128:j * 256 + (i + 1) * 128], op0=ALU.mult, op1=ALU.add), r=[BSst, Bdec2[p], BpU2], w=[BSst])
                op("act", lambda e: e.activation(out=S16[:], in_=Sst[:], func=AF.Copy), r=[BSst], w=[BS16])
                o_sq = on[:n, 0, :]; o_t1 = on[:n, 1, :]
                op("act", lambda e: e.activation(out=o_sq, in_=pO[:n, :], func=AF.Square, scale=1.0 / math.sqrt(128.0)), r=[BpO], w=[Bon[0]])
                op("dve", lambda e: e.reduce_sum(out=sto[:n, 4:8], in_=o_sq.rearrange("p (h d) -> p h d", d=128), axis=AX.X), r=[Bon[0]], w=[Bsto])
                op("act", lambda e: e.activation(out=sto[:n, 8:12], in_=sto[:n, 4:8], func=AF.Ln, bias=EPS), r=[Bsto], w=[Bsto])
                op("act", lambda e: e.activation(out=sto[:n, 4:8], in_=sto[:n, 8:12], func=AF.Exp, scale=-0.5), r=[Bsto], w=[Bsto])
                op("dve", lambda e: e.tensor_tensor(out=o_t1.rearrange("p (h d) -> p h d", d=128), in0=pO[:n, :].rearrange("p (h d) -> p h d", d=128), in1=sto[:n, 4:8].unsqueeze(2).broadcast_to([n, 4, 128]), op=ALU.mult), r=[BpO, Bsto], w=[Bon[1]])
                op("dve", lambda e: e.tensor_tensor(out=o_t1, in0=o_t1, in1=gg[:n, :], op=ALU.mult), r=[Bon[1], Bgg], w=[Bon[1]])
                op("dve", lambda e: e.tensor_tensor(out=yb16[par][:, :], in0=o_t1, in1=t3_2[p][:, :], op=ALU.mult), r=[Bon[1], Bt3_2[p]], w=[Byb16[par]])
                op("sp", lambda e, t=t, par=par: e.dma_start(out=ym_scr[t * 128:(t + 1) * 128, 512:1024], in_=yb16[par][:, :]), r=[Byb16[par]], dma="s_yb%d" % par)

            def v_to_bf16(n, par):
                op("act", lambda e: e.activation(out=v16[par][:n, :, 0:128], in_=pj[par][:n, 1024:1536].rearrange("p (h d) -> p h d", d=128), func=AF.Copy), r=[Bpj[par]], w=[Bv16[par]])

            qTm2 = [sbt(stA, "qTm2_%d" % i, [128, 4, 128], BF16) for i in range(2)]; BqTm2 = [Buf("qTm2_0"), Buf("qTm2_1")]
            for i in range(2):
                op("pool", lambda e, i=i: e.memset(qTm2[i][:], 0.0), w=[BqTm2[i]])
            AT2 = [sbt(stA, "AT2_%d" % i, [128, 4, 128], BF16) for i in range(2)]; BAT2 = [Buf("AT2_0"), Buf("AT2_1")]
            kh2 = [sbt(stA, "kh2_%d" % i, [128, 256], BF16) for i in range(2)]; Bkh2 = [Buf("kh2_0"), Buf("kh2_1")]
            dec2 = [sbt(stA, "dec2_%d" % i, [128, 8], F32) for i in range(2)]; Bdec2 = [Buf("dec2_0"), Buf("dec2_1")]
            t3_2 = [sbt(stA, "t3_2_%d" % i, [128, 512], F32) for i in range(2)]; Bt3_2 = [Buf("t3_2_0"), Buf("t3_2_1")]
            lahl = sbt(stA, "lahl", [128, 2, 256], BF16); Blahl = Buf("lahl")
            trib = sbt(stA, "trib", [128, 3, 128], BF16); Btrib = Buf("trib")
            gv16 = [sbt(stA, "gv16_%d" % i, [128, 4, 128], BF16) for i in range(3)]; Bgv16 = [Buf("gv16_%d" % i) for i in range(3)]

            def gla_v(n, par):
                op("act", lambda e: e.activation(out=gv16[par][:n, :, :], in_=pj[par][:n, 2048:2560].rearrange("p (h d) -> p h d", d=128), func=AF.Copy), r=[Bpj[par]], w=[Bgv16[par]])

            NS_TOK = NSS * TS
            load_x(xs_d[:, :], NS_TOK, 0)
            op("sp", lambda e: e.dma_start(out=SstS[:].rearrange("p s j v -> p (s j) v"), in_=st_d.rearrange("s (j p) v -> p (s j) v", p=128)), w=[BSstS], dma="st")
            op("dve", lambda e: e.tensor_copy(out=S16S[:], in_=SstS[:]), r=[BSstS], w=[BS16S])
            proj_tile(NS_TOK, 0, ropeS[:NS_TOK, :], BropeS)
            if stop == "S1":
                P.enabled = False
            op("sp", lambda e: e.dma_start(out=nks_d[:, :], in_=qkn[0][:NS_TOK, 512:1024]), r=[Bqkn[0]], dma="o_nks")
            op("sp", lambda e: e.dma_start(out=nvs_d[:, :], in_=pj[0][:NS_TOK, 1024:1536]), r=[Bpj[0]], dma="o_nvs")
            if stop == "S2":
                P.enabled = False
            gla_v(NS_TOK, 0)
            gla(NS_TOK, 0, NSS, triIs[:NS_TOK, :NS_TOK], triSs[:NS_TOK, :NS_TOK], maskSTs[:NS_TOK, :NS_TOK], sel16s[:NS_TOK, :], SstS, BSstS, S16S, BS16S, True, ymS[:NS_TOK, 512:1024], BymS)
            op("sp", lambda e: e.dma_start(out=nglas_d.rearrange("s (j p) v -> p (s j) v", p=128), in_=SstS[:].rearrange("p s j v -> p (s j) v")), r=[BSstS], dma="o_nglas")

            qk_transposes(NS_TOK, 0, QTs, BQTs, 0, "qk")
            op("dve", lambda e: e.tensor_copy(out=vnewS[:, :], in_=pj[0][:NS_TOK, 1024:1536]), r=[Bpj[0]], w=[BvnewS])

            if stop == "S0":
                P.enabled = False
            load_x(meta_d[:, :], NMETA, 1)
            op("pool", lambda e: e.memset(Sst[:], 0.0), w=[BSst])
            proj_tile(NMETA, 1, ropeM[:, :], BropeM)
            op("sp", lambda e: e.dma_start(out=nk_d[0:NMETA, :], in_=qkn[1][:NMETA, 512:1024]), r=[Bqkn[1]], dma="o_nk1")
            op("sp", lambda e: e.dma_start(out=nv_d[0:NMETA, :], in_=pj[1][:NMETA, 1024:1536]), r=[Bpj[1]], dma="o_nv1")
            qk_transposes(NMETA, 1, qkT[1], BqkT[1], 0, "k")
            op("sp", lambda e: e.dma_start(out=kt_scr[:, :, 0:NMETA], in_=qkT[1][:, 4:8, 0:NMETA]), r=[BqkT[1]], dma="s_kt1")
            v_to_bf16(NMETA, 1)
            op("sp", lambda e: e.dma_start(out=v_scr[0:NMETA, :], in_=v16[1][:NMETA, :, :].rearrange("p h d -> p (h d)")), r=[Bv16[1]], dma="s_v1")
            gla_v(NMETA, 1)
            gla(NMETA, 1, 1, triI[:NMETA, :NMETA], triS[:NMETA, :NMETA], maskST[:NMETA, :NMETA], sel16p[:NMETA, :], Sst, BSst, S16, BS16, False, None, None)

            load_x(x_d[0:128, :], 128, 0)

            def stream_x(t):
                par = t % 3
                if t + 1 < NT:
                    load_x(x_d[(t + 1) * 128:(t + 2) * 128, :], 128, (t + 1) % 2)
                proj_tile_a(128, par, t % 2)

            def stream_y1(t):
                par = t % 3
                proj_tile_b(128, par, ropeP[:, t, :], BropeP)
                r0 = NMETA + t * 128
                op("sp", lambda e, r0=r0, par=par: e.dma_start(out=nk_d[r0:r0 + 128, :], in_=qkn[par][:, 512:1024]), r=[Bqkn[par]], dma="o_nk%d" % par)
                op("sp", lambda e, r0=r0, par=par: e.dma_start(out=nv_d[r0:r0 + 128, :], in_=pj[par][:, 1024:1536]), r=[Bpj[par]], dma="o_nv%d" % par)
                g4, sub = t // 4, t % 4
                gp = g4 % 2
                qk_transposes(128, par, qkT[gp], BqkT[gp], sub * 128, "qk")
                if sub == 3:
                    op("sp", lambda e, g4=g4, gp=gp: e.dma_start(out=qt_scr[:, :, g4 * 512:(g4 + 1) * 512], in_=qkT[gp][:, 0:4, :]), r=[BqkT[gp]], dma="s_qt%d" % gp)
                    op("sp", lambda e, g4=g4, gp=gp: e.dma_start(out=kt_scr[:, :, NMETA + g4 * 512:NMETA + (g4 + 1) * 512], in_=qkT[gp][:, 4:8, :]), r=[BqkT[gp]], dma="s_kt%d" % gp)
                v_to_bf16(128, par)
                op("sp", lambda e, r0=r0, par=par: e.dma_start(out=v_scr[r0:r0 + 128, :], in_=v16[par][:, :, :].rearrange("p h d -> p (h d)")), r=[Bv16[par]], dma="s_v%d" % par)

            state["trikey"] = (128, 1)
            op("dve", lambda e: e.tensor_copy(out=trib[:, 0, :], in_=triI), r=[Bcst, Btrib], w=[Btrib])
            op("dve", lambda e: e.tensor_copy(out=trib[:, 1, :], in_=triS), r=[Bcst, Btrib], w=[Btrib])
            op("dve", lambda e: e.tensor_copy(out=trib[:, 2, 0:1], in_=sel16p), r=[Bcst, Btrib], w=[Btrib])
            for step_ in range(NT + 3):
                strs = []; bs = []
                for fn_, tt, b_ in ((stream_x, step_, 0.0), (stream_y1, step_ - 1, 0.05), (gla_a, step_ - 2, 0.2), (gla_b, step_ - 3, 0.1)):
                    if 0 <= tt < NT:
                        P.begin(); fn_(tt); strs.append(P.end()); bs.append(b_)
                P.interleave(*strs, bias=bs)
            op("sp", lambda e: e.dma_start(out=ngla_d.rearrange("(j p) v -> p j v", p=128), in_=Sst[:]), r=[BSst], dma="o_ngla")
            P.barrier()

        stA.close()
        if stop == "A":
            P.enabled = False
        with contextlib.ExitStack() as stS:
            NKT_S = 17
            kst = sbt(stS, "kst", [128, NKT_S, 512], F32); Bkst = Buf("kst")
            vst = sbt(stS, "vst", [128, NKT_S, 512], F32); Bvst = Buf("vst")
            kb16s = sbt(stS, "kb16s", [128, NKT_S, 512], BF16); Bkb16s = Buf("kb16s")
            KTs = sbt(stS, "KTs", [128, 4, NKT_S * 128], BF16); BKTs = Buf("KTs")
            VAs = sbt(stS, "VAs", [128, NKT_S, 4, 129], BF16); BVAs = Buf("VAs")
            STs = sbt(stS, "STs", [128, 8, NKT_S * 16], F32); BSTs = Buf("STs")
            PTs = sbt(stS, "PTs", [128, 8, NKT_S * 16], BF16); BPTs = Buf("PTs")
            mx = sbt(stS, "mx", [128, 16], F32); Bmx = Buf("mx")
            mxT = sbt(stS, "mxT", [8, 128], F32); BmxT = Buf("mxT")
            mx1 = sbt(stS, "mx1", [8, 8], F32); Bmx1 = Buf("mx1")
            mxd = sbt(stS, "mxd", [8, 8], F32); Bmxd = Buf("mxd")
            ones8b = sbt(stS, "ones8b", [8, 128], BF16); Bones8 = Buf("ones8")
            mxb = sbt(stS, "mxb", [128, 8], BF16); Bmxb = Buf("mxb")
            mxdb = sbt(stS, "mxdb", [8, 8], BF16)
            oas = sbt(stS, "oas", [16, 4, 128], F32); Boas = Buf("oas")
            oat = sbt(stS, "oat", [16, 128], F32); Boat = Buf("oat")
            osm = sbt(stS, "osm", [16, 32], F32); Bosm = Buf("osm")
            ya_s = sbt(stS, "ya_s", [16, 512], BF16); Bya_s = Buf("ya_s")
            junk16 = sbt(stS, "junk16", [16, 128], F32); Bjunk16 = Buf("junk16")
            sA = pst(stS, "sA", [128, 8, 128], BF16); BsA = Buf("sA")
            sP = [pst(stS, "sP%d" % i, [128, 512], F32) for i in range(2)]; BsP = [Buf("sP0"), Buf("sP1")]
            sM = pst(stS, "sM", [128, 512], F32); BsM = Buf("sM")
            sO = pst(stS, "sO", [128, 512], F32); BsO = Buf("sO")
            op("pool", lambda e: e.memset(ones8b[:], 1.0), w=[Bones8])
            QTm = sbt(stS, "QTm", [128, 2, 4, 64], BF16); BQTm = Buf("QTm")
            op("pool", lambda e: e.memset(QTm[:], 0.0), w=[BQTm])
            for c in range(2):
                op("dve", lambda e, c=c: e.tensor_copy(out=QTm[c * 64:(c + 1) * 64, c, :, :], in_=QTs[c * 64:(c + 1) * 64, 0:4, :]), r=[BQTs, BQTm], w=[BQTm])
            op("pool", lambda e: e.memset(VAs[:], 1.0), w=[BVAs])
            op("pool", lambda e: e.memset(kb16s[:], 0.0), w=[Bkb16s])
            def sa_load(s):
                op("sp", lambda e, s=s: e.dma_start(out=kst[:, 0:16, :], in_=ck_d[s, 0:2048, :].rearrange("(p t) c -> p t c", p=128)), w=[Bkst], dma="kst")
                op("sp", lambda e, s=s: e.dma_start(out=kst[0:16, 16, :], in_=ck_d[s, 2048:2064, :]), w=[Bkst], dma="kst")
                op("sp", lambda e, s=s: e.dma_start(out=vst[:, 0:16, :], in_=cv_d[s, 0:2048, :].rearrange("(p t) c -> p t c", p=128)), w=[Bvst], dma="vst")
                op("sp", lambda e, s=s: e.dma_start(out=vst[0:16, 16, :], in_=cv_d[s, 2048:2064, :]), w=[Bvst], dma="vst")
                op("sp", lambda e, s=s: e.dma_start(out=vst[16:32, 16, :], in_=vnewS[s * TS:(s + 1) * TS, :]), r=[BvnewS], w=[Bvst], dma="vst")

            sa_load(0)
            for s in range(NSS):
                op("dve", lambda e: e.tensor_copy(out=kb16s[:, 0:16, :], in_=kst[:, 0:16, :]), r=[Bkst], w=[Bkb16s])
                op("dve", lambda e: e.tensor_copy(out=kb16s[0:16, 16, :], in_=kst[0:16, 16, :]), r=[Bkst, Bkb16s], w=[Bkb16s])
                op("act", lambda e: e.activation(out=VAs[:, 0:16, :, 0:128], in_=vst[:, 0:16, :].rearrange("p t (h d) -> p t h d", d=128), func=AF.Copy), r=[Bvst], w=[BVAs])
                op("act", lambda e: e.activation(out=VAs[0:32, 16, :, 0:128], in_=vst[0:32, 16, :].rearrange("p (h d) -> p h d", d=128), func=AF.Copy), r=[Bvst, BVAs], w=[BVAs])
                if s + 1 < NSS:
                    sa_load(s + 1)
                for t in range(NKT_S):
                    rows = 128 if t < 16 else 16
                    for h in range(4):
                        op("pe", lambda e, t=t, h=h, rows=rows: e.transpose(out=sA[:, h, :rows], in_=kb16s[:rows, t, h * 128:(h + 1) * 128], identity=identb[:rows, :rows]), r=[Bkb16s, Bident], w=[BsA])
                    eng = "dve" if t % 2 == 0 else "act"
                    if eng == "dve":
                        op("dve", lambda e, t=t, rows=rows: e.tensor_copy(out=KTs[:, :, t * 128:t * 128 + rows], in_=sA[:, 0:4, :rows]), r=[BsA], w=[BKTs])
                    else:
                        op("act", lambda e, t=t, rows=rows: e.activation(out=KTs[:, :, t * 128:t * 128 + rows], in_=sA[:, 0:4, :rows], func=AF.Copy), r=[BsA], w=[BKTs])
                op("dve", lambda e, s=s: e.tensor_copy(out=KTs[:, :, 16 * 128 + 16:16 * 128 + 32], in_=QTs[:, 4:8, s * TS:(s + 1) * TS]), r=[BQTs, BKTs], w=[BKTs])
                for h in range(4):
                    for c in range(2):
                        hc = h * 2 + c
                        pp = state["ppar"]; state["ppar"] ^= 1
                        for t in range(NKT_S):
                            rows = 128 if t < 16 else 32
                            op("pe", lambda e, t=t, h=h, c=c, rows=rows, pp=pp, s=s: e.matmul(sP[pp][:rows, t * 16:(t + 1) * 16], lhsT=KTs[:, h, t * 128:t * 128 + rows], rhs=QTm[:, c, h, s * TS:(s + 1) * TS], start=True, stop=True), r=[BKTs, BQTm], w=[BsP[pp]])
                        op("dve", lambda e, hc=hc, pp=pp: e.tensor_copy(out=STs[:, hc, 0:256], in_=sP[pp][:, 0:256]), r=[BsP[pp]], w=[BSTs])
                        op("dve", lambda e, hc=hc, pp=pp: e.tensor_copy(out=STs[0:32, hc, 256:272], in_=sP[pp][0:32, 256:272]), r=[BsP[pp], BSTs], w=[BSTs])
                op("dve", lambda e: e.tensor_reduce(out=mx[:, 0:8], in_=STs[:, :, 0:256], axis=AX.X, op=ALU.max), r=[BSTs], w=[Bmx])
                op("dve", lambda e: e.tensor_reduce(out=mx[0:32, 8:16], in_=STs[0:32, :, 256:272], axis=AX.X, op=ALU.max), r=[BSTs, Bmx], w=[Bmx])
                op("dve", lambda e: e.tensor_tensor(out=mx[0:32, 0:8], in0=mx[0:32, 0:8], in1=mx[0:32, 8:16], op=ALU.max), r=[Bmx], w=[Bmx])
                op("dve", lambda e: e.tensor_copy(out=mxb[:, 0:8], in_=mx[:, 0:8]), r=[Bmx], w=[Bmxb])
                op("pe", lambda e: e.transpose(out=sA[0:8, 0, :], in_=mxb[:, 0:8], identity=identb[:, :]), r=[Bmxb, Bident], w=[BsA])
                op("dve", lambda e: e.tensor_reduce(out=mx1[:, 0:1], in_=sA[0:8, 0, :], axis=AX.X, op=ALU.max), r=[BsA], w=[Bmx1])
                op("dve", lambda e: e.tensor_scalar(out=mxdb[:, :], in0=ident_f[0:8, 0:8], scalar1=mx1[:, 0:1], scalar2=-0.125, op0=ALU.mult, op1=ALU.mult), r=[Bmx1, Bcst], w=[Bmxd])
                op("pe", lambda e: e.matmul(sM[:, 128:136], lhsT=ones8b[:, :], rhs=mxdb[:, :], start=True, stop=True), r=[Bones8, Bmxd], w=[BsM])
                op("dve", lambda e: e.tensor_copy(out=mx[:, 8:16], in_=sM[:, 128:136]), r=[BsM, Bmx], w=[Bmx])
                for hc in range(8):
                    op("act", lambda e, hc=hc: e.activation(out=PTs[:, hc, 0:256], in_=STs[:, hc, 0:256], func=AF.Exp, scale=0.125, bias=mx[:, 8 + hc:9 + hc]), r=[BSTs, Bmx], w=[BPTs])
                    op("act", lambda e, hc=hc: e.activation(out=PTs[0:32, hc, 256:272], in_=STs[0:32, hc, 256:272], func=AF.Exp, scale=0.125, bias=mx[0:32, 8 + hc:9 + hc]), r=[BSTs, Bmx, BPTs], w=[BPTs])
                for h in range(4):
                    for c in range(2):
                        hc = h * 2 + c
                        for t in range(NKT_S):
                            rows = 128 if t < 16 else 32
                            op("pe", lambda e, t=t, h=h, c=c, hc=hc, rows=rows: e.matmul(sO[0:16, c * 129:(c + 1) * 129], lhsT=PTs[:rows, hc, t * 16:(t + 1) * 16], rhs=VAs[:rows, t, h, :], start=(t == 0), stop=(t == NKT_S - 1)), r=[BPTs, BVAs], w=[BsO])
                    op("dve", lambda e: e.reciprocal(out=osm[:, 0:1], in_=sO[0:16, 128:129]), r=[BsO], w=[Bosm])
                    op("dve", lambda e: e.reciprocal(out=osm[:, 1:2], in_=sO[0:16, 257:258]), r=[BsO, Bosm], w=[Bosm])
                    op("dve", lambda e: e.tensor_tensor(out=osm[:, 2:3], in0=osm[:, 1:2], in1=neglam[0:16, :], op=ALU.mult), r=[Bosm, Bsm], w=[Bosm])
                    op("dve", lambda e: e.tensor_scalar(out=oat[:, :], in0=sO[0:16, 0:128], scalar1=osm[:, 0:1], scalar2=None, op0=ALU.mult), r=[BsO, Bosm], w=[Boat])
                    op("dve", lambda e, h=h: e.scalar_tensor_tensor(out=oas[:, h, :], in0=sO[0:16, 129:257], scalar=osm[:, 2:3], in1=oat[:, :], op0=ALU.mult, op1=ALU.add), r=[BsO, Bosm, Boat], w=[Boas])
                    op("act", lambda e, h=h: e.activation(out=junk16[:, :], in_=oas[:, h, :], func=AF.Square, scale=1.0 / math.sqrt(128.0), accum_out=osm[:, 4 + h:5 + h]), r=[Boas], w=[Bjunk16, Bosm])
                op("act", lambda e: e.activation(out=osm[:, 8:12], in_=osm[:, 4:8], func=AF.Ln, bias=EPS), r=[Bosm], w=[Bosm])
                op("act", lambda e: e.activation(out=osm[:, 12:16], in_=osm[:, 8:12], func=AF.Exp, scale=-0.5), r=[Bosm], w=[Bosm])
                for h in range(4):
                    op("dve", lambda e, h=h: e.scalar_tensor_tensor(out=ya_s[:, h * 128:(h + 1) * 128], in0=oas[:, h, :], scalar=osm[:, 12 + h:13 + h], in1=gd[0:16, h * 128:(h + 1) * 128], op0=ALU.mult, op1=ALU.mult), r=[Boas, Bosm, Bgd], w=[Bya_s])
                op("sp", lambda e, s=s: e.dma_start(out=ymS[s * TS:(s + 1) * TS, 0:512], in_=ya_s[:, :]), r=[Bya_s], w=[BymS], dma="yms")
            P.barrier()


        if stop == "SA":
            P.enabled = False
        NKT = NT + 1
        NQB = S // 512
        with contextlib.ExitStack() as stB:
            KT = sbt(stB, "KT", [128, 4, NMETA + S], BF16); BKT = Buf("KT")
            KTm = sbt(stB, "KTm", [128, 4, 128], BF16); BKTm = Buf("KTm")
            VA = sbt(stB, "VA", [128, NKT, 516], BF16); BVA = Buf("VA")
            QB = [sbt(stB, "QB%d" % i, [128, 2, 4, 512], BF16) for i in range(2)]; BQB = [Buf("QB0"), Buf("QB1")]
            for i in range(2):
                op("pool", lambda e, i=i: e.memset(QB[i][:], 0.0), w=[BQB[i]])
            PT2 = [sbt(stB, "PT2_%d" % i, [128, 2, 512], BF16) for i in range(2)]; BPT2 = [Buf("PT2_0"), Buf("PT2_1")]
            fsm = sbt(stB, "fsm", [128, 32], F32); Bfsm = Buf("fsm")
            facc = sbt(stB, "facc", [128, 8, 129], F32); Bfacc = Buf("facc")
            ft8 = sbt(stB, "ft8", [128, 8, 128], F32); Bft8 = Buf("ft8")
            foa = sbt(stB, "foa", [128, 4, 128], F32); Bfoa = Buf("foa")
            fsq = sbt(stB, "fsq", [128, 4, 128], F32); Bfsq = Buf("fsq")
            yab = [sbt(stB, "yab%d" % i, [128, 4, 512], BF16) for i in range(2)]; Byab = [Buf("yab0"), Buf("yab1")]
            pS2 = [pst(stB, "pS2_%d" % i, [128, 2, 512], F32) for i in range(2)]; BpS2 = [Buf("pS2_0"), Buf("pS2_1")]
            pAcc = [pst(stB, "pAcc%d" % i, [128, 512], F32) for i in range(3)]
            BAcc = {}

            def acc_ap(c, j):
                if j < 3:
                    return pAcc[c][:, j * 129:(j + 1) * 129]
                return pAcc[2][:, c * 129:(c + 1) * 129]
            BAccBank = [Buf("accbank%d" % i) for i in range(3)]
            for c in range(2):
                for j in range(4):
                    BAcc[(c, j)] = BAccBank[c] if j < 3 else BAccBank[2]

            nchunk = max(1, (NMETA + S) // 2048)
            BKTc = [Buf("KTc%d" % i) for i in range(nchunk)]
            kedges = [0] + [NMETA + (i + 1) * (S // nchunk) for i in range(nchunk)]

            def kt_buf(kc0, rows):
                return [BKTc[i] for i in range(nchunk) if kc0 < kedges[i + 1] and kc0 + rows > kedges[i]]
            tper = 8
            BVAc = [Buf("VAc%d" % i) for i in range((NT + tper - 1) // tper)]

            def va_buf(vt):
                return BVAc[0] if vt == 0 else BVAc[(vt - 1) // tper]
            op("pool", lambda e: e.memset(KTm[:], 0.0), w=[BKTm])
            op("pool", lambda e: e.memset(VA[:, 0, :], 0.0), w=[BVAc[0]])
            op("sp", lambda e: e.dma_start(out=KTm[:, :, 0:NMETA], in_=kt_scr[:, :, 0:NMETA]), w=[BKTm], dma="l_ktm")
            op("sp", lambda e: e.dma_start(out=VA[0:NMETA, 0, :], in_=v_scr[0:NMETA, :]), w=[BVAc[0]], dma="l_va0")
            tiles_per_chunk = max(1, NT // nchunk)
            for i in range(nchunk):
                a_, b_ = kedges[i], kedges[i + 1]
                for h in range(4):
                    op("sp", lambda e, h=h, a_=a_, b_=b_: e.dma_start(out=KT[:, h, a_:b_], in_=kt_scr[:, h, a_:b_]), w=[BKTc[i]], dma="l_kt%d" % i)
                for t0 in range(i * tiles_per_chunk, (i + 1) * tiles_per_chunk if i + 1 < nchunk else NT, tper):
                    t1 = min(NT, t0 + tper)
                    op("sp", lambda e, t0=t0, t1=t1: e.dma_start(out=VA[:, 1 + t0:1 + t1, :], in_=v_scr[NMETA + t0 * 128:NMETA + t1 * 128, :].rearrange("(t p) c -> p t c", p=128)), w=[BVAc[t0 // tper]], dma="l_va%d" % (t0 // tper))

            def load_q(qb):
                qp = qb % 2
                for c in range(2):
                    op("sp", lambda e, qb=qb, qp=qp, c=c: e.dma_start(out=QB[qp][c * 64:(c + 1) * 64, c, :, :], in_=qt_scr[c * 64:(c + 1) * 64, :, qb * 512:(qb + 1) * 512]), w=[BQB[qp]], dma="l_q%d" % qp)

            steps = []
            for qb in range(NQB):
                for h in range(4):
                    ktiles = [(-1, 0)] + [(i, 0) for i in range(4 * qb)] + [(4 * qb + d, d) for d in range(4)]
                    for n_, (ki, d) in enumerate(ktiles):
                        steps.append(dict(qb=qb, h=h, ki=ki, d=d, first=(n_ == 0), last=(n_ == len(ktiles) - 1)))

            def geom(stp):
                if stp["ki"] < 0:
                    return 128, -1, 0
                return 128, NMETA + stp["ki"] * 128, 1 + stp["ki"]

            def front(stp, idx):
                sp_ = idx % 2
                qb, h, ki, d = stp["qb"], stp["h"], stp["ki"], stp["d"]
                qp = qb % 2
                if stp["first"] and h == 0 and qb + 1 < NQB:
                    load_q(qb + 1)
                rows, kc0, vt = geom(stp)
                q0 = d * 128
                for c in range(2):
                    op("pe", lambda e, c=c, sp_=sp_, rows=rows, kc0=kc0, q0=q0, h=h, qp=qp: e.matmul(pS2[sp_][:rows, c, q0:512], lhsT=(KTm[:, h, :] if kc0 < 0 else KT[:, h, kc0:kc0 + rows]), rhs=QB[qp][:, c, h, q0:512], start=True, stop=True), r=([BKTm] if kc0 < 0 else kt_buf(kc0, rows)) + [BQB[qp]], w=[BpS2[sp_]])
                op("act", lambda e, sp_=sp_, rows=rows, q0=q0: e.activation(out=PT2[sp_][:rows, :, q0:512], in_=pS2[sp_][:rows, :, q0:512], func=AF.Exp, scale=0.125, bias=negB[:rows, :]), r=[BpS2[sp_], Bsm], w=[BPT2[sp_]])
                if ki >= 4 * qb:
                    op("pool", lambda e, sp_=sp_, q0=q0: e.memset(PT2[sp_][64:128, :, q0:q0 + 64], 0.0), r=[BPT2[sp_]], w=[BPT2[sp_]])

            def back(stp, idx):
                sp_ = idx % 2
                qb, h, ki, d = stp["qb"], stp["h"], stp["ki"], stp["d"]
                qp = qb % 2
                rows, kc0, vt = geom(stp)
                for c in range(2):
                    for j in range(d, 4):
                        first = (ki < 0) and ((j == 0) or (j == 3 and c == 0))
                        last = (ki == 4 * qb + j)
                        op("pe", lambda e, c=c, j=j, sp_=sp_, rows=rows, vt=vt, h=h, first=first, last=last: e.matmul(acc_ap(c, j), lhsT=PT2[sp_][:rows, c, j * 128:(j + 1) * 128], rhs=VA[:rows, vt, h * 129:(h + 1) * 129], start=first, stop=last, skip_group_check=True), r=[BPT2[sp_], va_buf(vt)], w=[BAcc[(c, j)]])
                if not stp["last"]:
                    return
                for c in range(2):
                    op("dve", lambda e, c=c: e.tensor_copy(out=facc[:, c * 4:c * 4 + 3, :], in_=pAcc[c][:, 0:387].rearrange("p (j e) -> p j e", e=129)), r=[BAccBank[c], Bfacc], w=[Bfacc])
                op("dve", lambda e: e.tensor_copy(out=facc[:, 3::4, :], in_=pAcc[2][:, 0:258].rearrange("p (j e) -> p j e", e=129)), r=[BAccBank[2], Bfacc], w=[Bfacc])
                op("dve", lambda e: e.reciprocal(out=fsm[:, 0:8], in_=facc[:, :, 128]), r=[Bfacc], w=[Bfsm])
                op("dve", lambda e: e.tensor_scalar(out=fsm[:, 4:8], in0=fsm[:, 4:8], scalar1=neglam, scalar2=None, op0=ALU.mult), r=[Bfsm, Bsm], w=[Bfsm])
                op("dve", lambda e: e.tensor_tensor(out=ft8[:], in0=facc[:, :, 0:128], in1=fsm[:, 0:8].unsqueeze(2).broadcast_to([128, 8, 128]), op=ALU.mult), r=[Bfacc, Bfsm], w=[Bft8])
                op("dve", lambda e: e.tensor_tensor(out=foa[:], in0=ft8[:, 0:4, :], in1=ft8[:, 4:8, :], op=ALU.add), r=[Bft8], w=[Bfoa])
                op("dve", lambda e: e.tensor_tensor(out=fsq[:], in0=foa[:], in1=foa[:], op=ALU.mult), r=[Bfoa], w=[Bfsq])
                op("dve", lambda e: e.reduce_sum(out=fsm[:, 8:12], in_=fsq[:], axis=AX.X), r=[Bfsq], w=[Bfsm])
                op("act", lambda e: e.activation(out=fsm[:, 12:16], in_=fsm[:, 8:12], func=AF.Ln, scale=1.0 / 128.0, bias=EPS), r=[Bfsm], w=[Bfsm])
                op("act", lambda e: e.activation(out=fsm[:, 16:20], in_=fsm[:, 12:16], func=AF.Exp, scale=-0.5), r=[Bfsm], w=[Bfsm])
                op("dve", lambda e: e.tensor_tensor(out=fsq[:], in0=foa[:], in1=fsm[:, 16:20].unsqueeze(2).broadcast_to([128, 4, 128]), op=ALU.mult), r=[Bfoa, Bfsm, Bfsq], w=[Bfsq])
                op("dve", lambda e, h=h, qp=qp: e.tensor_tensor(out=yab[qp][:, :, h * 128:(h + 1) * 128], in0=fsq[:], in1=gd[:, h * 128:(h + 1) * 128].unsqueeze(1).broadcast_to([128, 4, 128]), op=ALU.mult), r=[Bfsq, Bgd], w=[Byab[qp]])
                if h == 3:
                    op("sp", lambda e, qb=qb, qp=qp: e.dma_start(out=ym_scr[qb * 512:(qb + 1) * 512, 0:512].rearrange("(j p) c -> p j c", p=128), in_=yab[qp][:]), r=[Byab[qp]], dma="s_ya%d" % qp)

            load_q(0)
            for idx, stp in enumerate(steps):
                front(stp, idx)
                if idx > 0:
                    back(steps[idx - 1], idx - 1)
            back(steps[-1], len(steps) - 1)
            P.barrier()

        if stop == "B":
            P.enabled = False
        stAB.close()
        with contextlib.ExitStack() as stC:
            WO = sbt(stC, "WO", [128, 8, D], BF16); BWO = Buf("WO")
            WG = sbt(stC, "WG", [128, 8, DFF], BF16); BWG = Buf("WG")
            WU = sbt(stC, "WU", [128, 8, DFF], BF16); BWU = Buf("WU")
            WD = sbt(stC, "WD", [128, NFC, D], BF16); BWD = Buf("WD")
            GT = 256
            ym = [sbt(stC, "ym%d" % i, [128, D], BF16) for i in range(2)]; Bym = [Buf("ym0"), Buf("ym1")]
            xc = [sbt(stC, "xc%d" % i, [128, D], F32) for i in range(1)]; Bxc = [Buf("xc0")]
            ymT = sbt(stC, "ymT", [128, 8, 128], BF16); BymT = Buf("ymT")
            h1 = [sbt(stC, "h1_%d" % i, [128, 2, D], F32) for i in range(2)]; Bh1 = [[Buf("h1_%d_%d" % (i, s_)) for s_ in range(2)] for i in range(2)]
            h1n = sbt(stC, "h1n", [128, D], BF16); Bh1n = Buf("h1n")
            h1nT = [sbt(stC, "h1nT%d" % i, [128, 8, GT], BF16) for i in range(2)]; Bh1nT = [Buf("h1nT0"), Buf("h1nT1")]
            hhT = sbt(stC, "hhT", [128, NFC, GT], BF16); BhhT = Buf("hhT")
            sgb = [sbt(stC, "sgb%d" % i, [128, GT], BF16) for i in range(2)]; Bsgb = [Buf("sgb0"), Buf("sgb1")]
            csm = sbt(stC, "csm", [128, 8], F32); Bcsm = Buf("csm")
            pTp = pst(stC, "pTp", [128, 8, 128], BF16); BpTp = Buf("pTp")
            pH = [pst(stC, "pH%d" % i, [128, 512], F32) for i in range(2)]; BpH = [Buf("pH0"), Buf("pH1")]
            pG = [pst(stC, "pG%d" % i, [128, 512], F32) for i in range(2)]; BpG = [Buf("pG0"), Buf("pG1")]
            pU_ = [pst(stC, "pUu%d" % i, [128, 512], F32) for i in range(2)]; BpUu = [Buf("pU0"), Buf("pU1")]
            pD = pst(stC, "pD", [128, 512], F32); BpD = Buf("pD")

            SW = 704
            NSTG = 3
            wstc = [sbt(stC, "wstc%d" % i, [128, SW], F32) for i in range(NSTG)]
            Bwstc = [Buf("wstc%d" % i) for i in range(NSTG)]
            wk = dict(k=0)

            def wload(src_rows, ncols, dst_fn, Bd, fold_col):
                c0 = 0
                while c0 < ncols:
                    cw = min(SW, ncols - c0)
                    i = wk["k"] % NSTG; wk["k"] += 1
                    op("sp", lambda e, i=i, c0=c0, cw=cw: e.dma_start(out=wstc[i][:, :cw], in_=src_rows[:, c0:c0 + cw]), w=[Bwstc[i]], dma="wstc%d" % i)
                    dst = dst_fn(c0, cw)
                    if wk["k"] % 2 == 0:
                        if fold_col is None:
                            op("dve", lambda e, i=i, cw=cw, dst=dst: e.tensor_copy(out=dst, in_=wstc[i][:, :cw]), r=[Bwstc[i]], w=[Bd])
                        else:
                            op("dve", lambda e, i=i, cw=cw, dst=dst: e.tensor_scalar(out=dst, in0=wstc[i][:, :cw], scalar1=fold_col, scalar2=None, op0=ALU.mult), r=[Bwstc[i], Bgffn], w=[Bd])
                    else:
                        if fold_col is None:
                            op("act", lambda e, i=i, cw=cw, dst=dst: e.activation(out=dst, in_=wstc[i][:, :cw], func=AF.Copy), r=[Bwstc[i]], w=[Bd])
                        else:
                            op("act", lambda e, i=i, cw=cw, dst=dst: e.activation(out=dst, in_=wstc[i][:, :cw], func=AF.Copy, scale=fold_col), r=[Bwstc[i], Bgffn], w=[Bd])
                    c0 += cw

            for kc in range(8):
                wload(wout_d[kc * 128:(kc + 1) * 128, :], D, (lambda c0, cw, kc=kc: WO[:, kc, c0:c0 + cw]), BWO, None)
            for kc in range(8):
                wload(wg_d[kc * 128:(kc + 1) * 128, :], DFF, (lambda c0, cw, kc=kc: WG[:, kc, c0:c0 + cw]), BWG, gffn[:, kc:kc + 1])
                wload(wu_d[kc * 128:(kc + 1) * 128, :], DFF, (lambda c0, cw, kc=kc: WU[:, kc, c0:c0 + cw]), BWU, gffn[:, kc:kc + 1])
            for fc in range(NFC):
                wload(wd_d[fc * 128:(fc + 1) * 128, :], D, (lambda c0, cw, fc=fc: WD[:, fc, c0:c0 + cw]), BWD, None)

            groups = [(g * GT, GT, False) for g in range(S // GT)] + [(0, NSS * TS, True)]
            cstate = dict(ldi=0, fstep=0)

            def c_pro(gi):
                g0, gt, is_s = groups[gi]
                hp = gi % 2
                nsub = max(1, gt // 128)
                n = min(128, gt)
                for sub in range(nsub):
                    lp = cstate["ldi"] % 2; cstate["ldi"] += 1
                    if is_s:
                        op("sp", lambda e, n=n: e.dma_start(out=xc[0][:n, :], in_=xs_d[:, :]), w=[Bxc[0]], dma="l_xc0")
                        ymsrc = ymS; Bymsrc = BymS
                    else:
                        r0 = g0 + sub * 128
                        op("sp", lambda e, lp=lp, r0=r0: e.dma_start(out=ym[lp][:], in_=ym_scr[r0:r0 + 128, :]), w=[Bym[lp]], dma="l_ym%d" % lp)
                        op("sp", lambda e, r0=r0: e.dma_start(out=xc[0][:], in_=x_d[r0:r0 + 128, :]), w=[Bxc[0]], dma="l_xc0")
                        ymsrc = ym[lp]; Bymsrc = Bym[lp]
                    for kc in range(8):
                        op("pe", lambda e, kc=kc, n=n, ymsrc=ymsrc: e.transpose(out=pTp[:, kc, :n], in_=ymsrc[:n, kc * 128:(kc + 1) * 128], identity=identb[:n, :n]), r=[Bymsrc, Bident], w=[BpTp])
                    op("act", lambda e, n=n: e.activation(out=ymT[:, :, :n], in_=pTp[:, :, :n], func=AF.Copy), r=[BpTp], w=[BymT])
                    for half in range(2):
                        for kc in range(8):
                            op("pe", lambda e, kc=kc, half=half, n=n: e.matmul(pH[half][:n, :], lhsT=ymT[:, kc, :n], rhs=WO[:, kc, half * 512:(half + 1) * 512], start=(kc == 0), stop=(kc == 7)), r=[BymT, BWO], w=[BpH[half]])
                        op("dve", lambda e, half=half, n=n, hp=hp, sub=sub: e.tensor_tensor(out=h1[hp][:n, sub, half * 512:(half + 1) * 512], in0=pH[half][:n, :], in1=xc[0][:n, half * 512:(half + 1) * 512], op=ALU.add), r=[BpH[half], Bxc[0]], w=[Bh1[hp][sub]])
                    op("act", lambda e, n=n, hp=hp, sub=sub: e.activation(out=h1n[:n, :], in_=h1[hp][:n, sub, :], func=AF.Square, scale=1.0 / 32.0, accum_out=csm[:n, 0:1]), r=[Bh1[hp][sub]], w=[Bh1n, Bcsm])
                    op("act", lambda e, n=n: e.activation(out=csm[:n, 1:2], in_=csm[:n, 0:1], func=AF.Ln, bias=EPS), r=[Bcsm], w=[Bcsm])
                    op("act", lambda e, n=n: e.activation(out=csm[:n, 2:3], in_=csm[:n, 1:2], func=AF.Exp, scale=-0.5), r=[Bcsm], w=[Bcsm])
                    op("act", lambda e, n=n, hp=hp, sub=sub: e.activation(out=h1n[:n, :], in_=h1[hp][:n, sub, :], func=AF.Copy, scale=csm[:n, 2:3]), r=[Bh1[hp][sub], Bcsm, Bh1n], w=[Bh1n])
                    for kc in range(8):
                        op("pe", lambda e, kc=kc, n=n: e.transpose(out=pTp[:, kc, :n], in_=h1n[:n, kc * 128:(kc + 1) * 128], identity=identb[:n, :n]), r=[Bh1n, Bident], w=[BpTp])
                    op("dve", lambda e, n=n, sub=sub, hp=hp: e.tensor_copy(out=h1nT[hp][:, :, sub * 128:sub * 128 + n], in_=pTp[:, :, :n]), r=[BpTp], w=[Bh1nT[hp]])

            def c_ffn(gi):
                g0, gt, is_s = groups[gi]
                hp = gi % 2
                nsub = max(1, gt // 128)
                n = min(128, gt)
                for fc in range(NFC):
                    fp = cstate["fstep"] % 2; cstate["fstep"] += 1
                    for kc in range(8):
                        op("pe", lambda e, kc=kc, fc=fc, fp=fp, gt=gt, hp=hp: e.matmul(pG[fp][:, :gt], lhsT=WG[:, kc, fc * 128:(fc + 1) * 128], rhs=h1nT[hp][:, kc, :gt], start=(kc == 0), stop=(kc == 7)), r=[BWG, Bh1nT[hp]], w=[BpG[fp]])
                    for kc in range(8):
                        op("pe", lambda e, kc=kc, fc=fc, fp=fp, gt=gt, hp=hp: e.matmul(pU_[fp][:, :gt], lhsT=WU[:, kc, fc * 128:(fc + 1) * 128], rhs=h1nT[hp][:, kc, :gt], start=(kc == 0), stop=(kc == 7)), r=[BWU, Bh1nT[hp]], w=[BpUu[fp]])
                    op("act", lambda e, fp=fp, gt=gt: e.activation(out=sgb[fp][:, :gt], in_=pG[fp][:, :gt], func=AF.Silu), r=[BpG[fp]], w=[Bsgb[fp]])
                    op("dve", lambda e, fp=fp, fc=fc, gt=gt: e.tensor_tensor(out=hhT[:, fc, :gt], in0=pU_[fp][:, :gt], in1=sgb[fp][:, :gt], op=ALU.mult), r=[BpUu[fp], Bsgb[fp]], w=[BhhT])
                for sub in range(nsub):
                    for half in range(2):
                        for fc in range(NFC):
                            op("pe", lambda e, fc=fc, half=half, n=n, sub=sub: e.matmul(pD[:n, :], lhsT=hhT[:, fc, sub * 128:sub * 128 + n], rhs=WD[:, fc, half * 512:(half + 1) * 512], start=(fc == 0), stop=(fc == NFC - 1)), r=[BhhT, BWD], w=[BpD])
                        op("dve", lambda e, half=half, n=n, hp=hp, sub=sub: e.tensor_tensor(out=h1[hp][:n, sub, half * 512:(half + 1) * 512], in0=pD[:n, :], in1=h1[hp][:n, sub, half * 512:(half + 1) * 512], op=ALU.add), r=[BpD, Bh1[hp][sub]], w=[Bh1[hp][sub]])
                    if is_s:
                        op("sp", lambda e, n=n, hp=hp, sub=sub: e.dma_start(out=ys_d[:, :], in_=h1[hp][:n, sub, :]), r=[Bh1[hp][sub]], dma="o_ys")
                    else:
                        r0 = g0 + sub * 128
                        op("sp", lambda e, r0=r0, hp=hp, sub=sub: e.dma_start(out=y_d[r0:r0 + 128, :], in_=h1[hp][:, sub, :]), r=[Bh1[hp][sub]], dma="o_y%d_%d" % (hp, sub))

            c_pro(0)
            for gi in range(len(groups)):
                P.begin(); c_ffn(gi); Yc = P.end()
                if gi + 1 < len(groups):
                    P.begin(); c_pro(gi + 1); Xc = P.end()
                    P.interleave(Xc, Yc, bias=[0.0, 0.2])
                else:
                    P.replay(Yc)
            P.barrier()
        P.emit(nc)
    return nc


def _consts():
    c = np.zeros((128, NCONST), np.float32)
    idx = np.arange(128)
    c[:, 0:128] = np.eye(128, dtype=np.float32)
    le = (idx[:, None] <= idx[None, :]).astype(np.float32)
    gt = (idx[:, None] > idx[None, :]).astype(np.float32)
    c[:, 128:256] = le / 16.0
    c[:, 256:384] = gt / 16.0
    c[:, 384:512] = le
    i64 = np.arange(64)
    same = (i64[:, None] // TS == i64[None, :] // TS).astype(np.float32)
    c[:64, 512:576] = same * le[:64, :64] / 16.0
    c[:64, 576:640] = same * gt[:64, :64] / 16.0
    c[:64, 640:704] = same * le[:64, :64]
    oh = (i64[:, None] // TS == np.arange(NSS)[None, :]).astype(np.float32)
    c[:64, 704:708] = oh / 16.0
    c[:64, 708:712] = oh
    c[:, 712] = 1.0 / 16.0
    c[:, 713:969] = np.broadcast_to(oh.T.reshape(1, NSS * 64), (128, NSS * 64))
    return c


def _rope(pos):
    half = 8
    inv = (500000.0 ** (-np.arange(0, 16, 2, dtype=np.float32) / np.float32(16))).astype(np.float32)
    ang = pos.astype(np.float32)[:, None] * inv[None, :]
    return np.concatenate([np.cos(ang), np.sin(ang)], axis=1).astype(np.float32)


_NC_CACHE = {}


def _run(S, ncores, inputs):
    if S not in _NC_CACHE:
        _NC_CACHE[S] = build(S, os.environ.get("KSTOP"))
    nc = _NC_CACHE[S]
    f = lambda a: np.ascontiguousarray(np.asarray(a, dtype=np.float32))
    i = {k: np.asarray(v) for k, v in inputs.items()}
    rep = lambda v, n=128: np.ascontiguousarray(np.broadcast_to(np.asarray(v, np.float32).reshape(1, -1), (n, np.asarray(v).size)))
    qn, kn = i["q_norm"][0], i["k_norm"][0]
    gqk = np.concatenate([np.tile(qn, 8), np.tile(kn, 8)])
    lamv = np.concatenate([i["lambda_q1"][0], i["lambda_k1"][0], i["lambda_q2"][0], i["lambda_k2"][0]])
    shared = {
        "meta": f(i["meta_tokens"]), "w_in": f(i["w_in"][0]),
        "wa2b": f(np.concatenate([i["w_a2"][0], i["b_a"][0][None, :]], axis=0)),
        "gmix": f(i["norm_mix"][0].reshape(8, 128).T), "gffn": f(i["norm_ffn"][0].reshape(8, 128).T),
        "gqk": rep(gqk), "lamv": rep(lamv), "gd": rep(i["g_diff"][0]), "gg": rep(i["g_gla"][0]),
        "w_out": f(i["w_out"][0]), "wg": f(i["w_ffn_gate"][0]), "wu": f(i["w_ffn_up"][0]), "wd": f(i["w_ffn_down"][0]),
        "ropep": _rope(NMETA + np.arange(S)), "ropem": _rope(np.arange(NMETA)),
        "ropes": np.ascontiguousarray(np.tile(_rope(PAST + np.arange(TS)), (NSS, 1))),
        "cst": _consts(),
    }
    in_maps = []
    for c in range(ncores):
        m = dict(shared)
        m["x"] = f(i["x_prompt"][c, :S])
        sl = slice(NSS * c, NSS * (c + 1))
        m["xs"] = f(i["x_sample"][sl].reshape(NSS * TS, D))
        m["ck"] = f(i["cache_k_diff"][0, sl].reshape(NSS, PAST, 512))
        m["cv"] = f(i["cache_v_diff"][0, sl].reshape(NSS, PAST, 512))
        m["st"] = f(i["state_gla"][0, sl].reshape(NSS, 256, 128))
        in_maps.append(m)
    res = run_bass_kernel_spmd(nc, in_maps, core_ids=list(range(ncores)))
    R = res.results
    y = np.stack([R[c]["y"] for c in range(ncores)])
    ys = np.concatenate([R[c]["ys"].reshape(NSS, TS, D) for c in range(ncores)])
    nk = np.stack([R[c]["nk"].reshape(NMETA + S, 4, 128) for c in range(ncores)])[None]
    nv = np.stack([R[c]["nv"].reshape(NMETA + S, 4, 128) for c in range(ncores)])[None]
    ngla = np.stack([R[c]["ngla"].reshape(4, 64, 128) for c in range(ncores)])[None]
    nks = np.concatenate([R[c]["nks"].reshape(NSS, TS, 4, 128) for c in range(ncores)])[None]
    nvs = np.concatenate([R[c]["nvs"].reshape(NSS, TS, 4, 128) for c in range(ncores)])[None]
    nglas = np.concatenate([R[c]["nglas"].reshape(NSS, 4, 64, 128) for c in range(ncores)])[None]
    return tuple(np.ascontiguousarray(a.astype(np.float32)) for a in (y, ys, nk, nv, ngla, nks, nvs, nglas))


def kernel(**inputs):
    S = int(np.asarray(inputs["x_prompt"]).shape[1])
    return _run(S, 8, inputs)
```
